# Optimizing a Trainium2 kernel written in Bass

```python
import math
import jax, jax.numpy as jnp
from jax import lax
import numpy as np

D_MODEL = 1024
BATCH = 8
SEQ = 2048
DEPTH = 2
DEC_BATCH = 128
DEC_SEQ = 1
PAST_LEN = 16384
PAGE_SIZE = 128

CHUNK = 128
CONV_K = 4
EPS = 1e-6

RET_H = 4
RET_DK = D_MODEL // 8
RET_DV = D_MODEL // 4
RET_QK = RET_H * RET_DK
RET_V = RET_H * RET_DV
ROPE_THETA = 10000.0
ML_H = 4
ML_DH = D_MODEL // 4
ML_W = ML_H * ML_DH
SSM_H = 16
SSM_P = D_MODEL // 16
SSM_W = SSM_H * SSM_P
SSM_G = 4
SSM_N = 128
SSM_CONV_DIM = SSM_W + 2 * SSM_G * SSM_N
MEM_LEN = 256
MEM_H = 4
MEM_DH = D_MODEL // MEM_H
D_FF = 4 * D_MODEL
N_RET = 2 * RET_QK + 2 * RET_V
N_ML = 3 * ML_W + 2 * ML_H
N_SSM = SSM_W + SSM_CONV_DIM + SSM_H
N_GATE = 3 * D_MODEL
N_IN = N_RET + N_ML + N_SSM + N_GATE

kernel_name = 'hybrid_retention_mlstm_ssd_decoder_step'


def rmsnorm(x, w):
    xf = x.astype(jnp.float32)
    y = xf * lax.rsqrt(jnp.mean(xf * xf, axis=-1, keepdims=True) + EPS)
    return (y * w.astype(jnp.float32)).astype(x.dtype)


def headnorm(x, w, n_groups, center):
    shp = x.shape
    xf = x.astype(jnp.float32).reshape(shp[:-1] + (n_groups, shp[-1] // n_groups))
    if center:
        xf = xf - jnp.mean(xf, axis=-1, keepdims=True)
    y = xf * lax.rsqrt(jnp.mean(xf * xf, axis=-1, keepdims=True) + EPS)
    return (y.reshape(shp) * w.astype(jnp.float32)).astype(x.dtype)


def rope(x, pos):
    half = x.shape[-1] // 2
    freqs = ROPE_THETA ** (-jnp.arange(half, dtype=jnp.float32) / half)
    ang = pos.astype(jnp.float32)[:, None] * freqs[None, :]
    cos = jnp.cos(ang)[None, :, None, :]
    sin = jnp.sin(ang)[None, :, None, :]
    xf = x.astype(jnp.float32)
    x1, x2 = xf[..., :half], xf[..., half:]
    return jnp.concatenate([x1 * cos - x2 * sin, x1 * sin + x2 * cos], axis=-1)


def causal_conv(x, buf, w, b):
    L = x.shape[1]
    xp = jnp.concatenate([buf.astype(x.dtype), x], axis=1)
    out = b
    for j in range(CONV_K):
        out = out + xp[:, j:j + L] * w[j]
    return jax.nn.silu(out), xp[:, -(CONV_K - 1):]


def chunk_len(L):
    return CHUNK if L % CHUNK == 0 else L


def to_chunks(x, c):
    b, L = x.shape[0], x.shape[1]
    return jnp.moveaxis(x.reshape((b, L // c, c) + x.shape[2:]), 1, 0)


def from_chunks(y):
    y = jnp.moveaxis(y, 0, 1)
    return y.reshape((y.shape[0], y.shape[1] * y.shape[2]) + y.shape[3:])


def retention(q, k, v, S0):
    c = chunk_len(q.shape[1])
    out_dtype = v.dtype
    log_g = jnp.log1p(-jnp.exp2(-5.0 - jnp.arange(RET_H, dtype=jnp.float32)))
    idx = jnp.arange(c, dtype=jnp.float32)
    diff = idx[:, None] - idx[None, :]
    causal = diff >= 0
    decay = jnp.where(causal, jnp.exp(log_g[:, None, None] * jnp.where(causal, diff, 0.0)), 0.0)
    q_decay = jnp.exp(log_g[None, :] * (idx[:, None] + 1.0))
    k_decay = jnp.exp(log_g[None, :] * (c - 1.0 - idx[:, None]))
    c_decay = jnp.exp(log_g * c)

    def step(S, blk):
        qc, kc, vc = blk
        sc = jnp.einsum('bthd,bshd->bhts', qc, kc) * decay
        y = (jnp.einsum('bhts,bshe->bthe', sc, vc)
             + jnp.einsum('bthd,bhde->bthe', qc, S) * q_decay[None, :, :, None])
        S = S * c_decay[None, :, None, None] + jnp.einsum('bshd,bshe,sh->bhde', kc, vc, k_decay)
        return S, y

    S, y = lax.scan(step, S0.astype(jnp.float32),
                    (to_chunks(q, c), to_chunks(k, c), to_chunks(v.astype(jnp.float32), c)))
    return from_chunks(y).astype(out_dtype), S.astype(S0.dtype)


def mlstm(q, k, v, i_pre, log_f, C0, n0, m0):
    c = chunk_len(q.shape[1])
    out_dtype = v.dtype
    f32 = jnp.float32
    causal = jnp.tril(jnp.ones((c, c), dtype=bool))

    def step(carry, blk):
        C, n, m = carry
        qc, kc, vc, ic, fc = blk
        b = jnp.swapaxes(jnp.cumsum(fc, axis=1), 1, 2)
        it = jnp.swapaxes(ic, 1, 2)
        logw = jnp.where(causal, b[..., :, None] - b[..., None, :] + it[..., None, :], -jnp.inf)
        inter = b + m[..., None]
        m_t = jnp.maximum(inter, jnp.max(logw, axis=-1))
        sc = jnp.einsum('bthd,bshd->bhts', qc, kc) * jnp.exp(logw - m_t[..., None])
        w_int = jnp.swapaxes(jnp.exp(inter - m_t), 1, 2)[..., None]
        num = jnp.einsum('bhts,bshe->bthe', sc, vc) + jnp.einsum('bthd,bhde->bthe', qc, C) * w_int
        den = jnp.swapaxes(jnp.sum(sc, axis=-1), 1, 2) + jnp.einsum('bthd,bhd->bth', qc, n) * w_int[..., 0]
        den = jnp.maximum(jnp.abs(den), jnp.swapaxes(jnp.exp(-m_t), 1, 2))
        h = num / den[..., None]
        b_end = b[..., -1]
        logw_s = b_end[..., None] - b + it
        m_new = jnp.maximum(b_end + m, jnp.max(logw_s, axis=-1))
        w_s = jnp.exp(logw_s - m_new[..., None])
        w_prev = jnp.exp(b_end + m - m_new)
        C = C * w_prev[..., None, None] + jnp.einsum('bshd,bshe,bhs->bhde', kc, vc, w_s)
        n = n * w_prev[..., None] + jnp.einsum('bshd,bhs->bhd', kc, w_s)
        return (C, n, m_new), h

    xs = (to_chunks(q.astype(f32), c), to_chunks(k.astype(f32), c), to_chunks(v.astype(f32), c),
          to_chunks(i_pre.astype(f32), c), to_chunks(log_f.astype(f32), c))
    (C, n, m), h = lax.scan(step, (C0.astype(f32), n0.astype(f32), m0.astype(f32)), xs)
    return (from_chunks(h).astype(out_dtype), C.astype(C0.dtype), n.astype(n0.dtype), m.astype(m0.dtype))


def ssd(x, dt, A, Bm, Cm, S0):
    b_, L = x.shape[0], x.shape[1]
    c = chunk_len(L)
    out_dtype = x.dtype
    f32 = jnp.float32
    R = SSM_H // SSM_G
    causal = jnp.tril(jnp.ones((c, c), dtype=bool))
    A_gr = A.reshape(SSM_G, R)

    def step(S, blk):
        xc, dtc, Bc, Cc = blk
        cum = jnp.cumsum(dtc * A_gr, axis=1)
        cum_t = jnp.moveaxis(cum, 1, -1)
        dt_t = jnp.moveaxis(dtc, 1, -1)
        seg = jnp.where(causal, jnp.exp(jnp.where(causal, cum_t[..., :, None] - cum_t[..., None, :], 0.0)), 0.0)
        M = jnp.einsum('btgn,bsgn->bgts', Cc, Bc)[:, :, None] * seg * dt_t[..., None, :]
        y = (jnp.einsum('bgrts,bsgrp->btgrp', M, xc)
             + jnp.einsum('btgn,bgrpn->btgrp', Cc, S) * jnp.exp(cum)[..., None])
        w_s = jnp.exp(cum_t[..., -1:] - cum_t) * dt_t
        S = S * jnp.exp(cum_t[..., -1])[..., None, None] + jnp.einsum('bgrs,bsgrp,bsgn->bgrpn', w_s, xc, Bc)
        return S, y

    xs = (to_chunks(x.astype(f32).reshape(b_, L, SSM_G, R, SSM_P), c),
          to_chunks(dt.astype(f32).reshape(b_, L, SSM_G, R), c),
          to_chunks(Bm.astype(f32), c), to_chunks(Cm.astype(f32), c))
    S, y = lax.scan(step, S0.astype(f32).reshape(b_, SSM_G, R, SSM_P, SSM_N), xs)
    return (from_chunks(y).reshape(b_, L, SSM_H, SSM_P).astype(out_dtype),
            S.reshape(b_, SSM_H, SSM_P, SSM_N).astype(S0.dtype))


def mem_attention(h, mk, mv, wq, wo):
    b, L = h.shape[0], h.shape[1]
    q = (h @ wq).reshape(b, L, MEM_H, MEM_DH)
    s = jnp.einsum('blhd,bmhd->bhlm', q, mk).astype(jnp.float32) * (MEM_DH ** -0.5)
    p = jax.nn.softmax(s, axis=-1).astype(h.dtype)
    o = jnp.einsum('bhlm,bmhd->blhd', p, mv).reshape(b, L, MEM_H * MEM_DH)
    return o @ wo


def split_cols(proj):
    sizes = (RET_QK, RET_QK, RET_V, RET_V, ML_W, ML_W, ML_W, ML_H, ML_H,
             SSM_W, SSM_CONV_DIM, SSM_H, D_MODEL, D_MODEL, D_MODEL)
    offs = np.cumsum(sizes)[:-1].tolist()
    return jnp.split(proj, offs, axis=-1)


def zero_states(nb, dtype):
    return (jnp.zeros((nb, RET_H, RET_DK, RET_DV), dtype),
            jnp.zeros((nb, ML_H, ML_DH, ML_DH), dtype),
            jnp.zeros((nb, ML_H, ML_DH), dtype),
            jnp.zeros((nb, ML_H), dtype),
            jnp.zeros((nb, CONV_K - 1, ML_W), dtype),
            jnp.zeros((nb, SSM_H, SSM_P, SSM_N), dtype),
            jnp.zeros((nb, CONV_K - 1, SSM_CONV_DIM), dtype))


def layer(l, x, pos, st, mem_k, mem_v, W):
    ret_S, ml_C, ml_n, ml_m, ml_buf, ssm_S, ssm_buf = st
    b, L = x.shape[0], x.shape[1]
    h = rmsnorm(x, W['norm_mix_w'][l])
    (q, k, v, g, u, vm, om, ig, fg, z, xbc, dt, gate_r, gate_m, gate_s) = split_cols(h @ W['w_in'][l])

    qr = rope(q.reshape(b, L, RET_H, RET_DK), pos)
    kr = rope(k.reshape(b, L, RET_H, RET_DK), pos) * (RET_DK ** -0.5)
    y_r, ret_S = retention(qr, kr, v.reshape(b, L, RET_H, RET_DV), ret_S)
    y_r = headnorm(y_r.reshape(b, L, RET_V), W['ret_norm_w'][l], RET_H, True) * jax.nn.silu(g)

    uc, ml_buf = causal_conv(u, ml_buf, W['ml_conv_w'][l], W['ml_conv_b'][l])
    uh = uc.reshape(b, L, ML_H, ML_DH)
    qm = jnp.einsum('blhd,hde->blhe', uh, W['ml_wq'][l])
    km = jnp.einsum('blhd,hde->blhe', uh, W['ml_wk'][l]) * (ML_DH ** -0.5)
    gbias = W['ml_gate_b'][l]
    i_pre = ig + gbias[:ML_H]
    log_f = jax.nn.log_sigmoid((fg + gbias[ML_H:]).astype(jnp.float32))
    h_m, ml_C, ml_n, ml_m = mlstm(qm, km, vm.reshape(b, L, ML_H, ML_DH), i_pre, log_f, ml_C, ml_n, ml_m)
    y_m = headnorm(h_m.reshape(b, L, ML_W), W['ml_norm_w'][l], ML_H, True) * jax.nn.sigmoid(om)

    xbc_c, ssm_buf = causal_conv(xbc, ssm_buf, W['ssm_conv_w'][l], W['ssm_conv_b'][l])
    xs, Bm, Cm = jnp.split(xbc_c, [SSM_W, SSM_W + SSM_G * SSM_N], axis=-1)
    delta = jax.nn.softplus((dt + W['ssm_dt_bias'][l]).astype(jnp.float32))
    A = -jnp.exp(W['ssm_A_log'][l].astype(jnp.float32))
    xh = xs.reshape(b, L, SSM_H, SSM_P)
    y_s, ssm_S = ssd(xh, delta, A, Bm.reshape(b, L, SSM_G, SSM_N), Cm.reshape(b, L, SSM_G, SSM_N), ssm_S)
    y_s = (y_s + W['ssm_D'][l][:, None] * xh).reshape(b, L, SSM_W)
    y_s = headnorm(y_s * jax.nn.silu(z), W['ssm_norm_w'][l], SSM_G, False)

    merged = (jax.nn.sigmoid(gate_r) * (y_r @ W['w_br_ret'][l])
              + jax.nn.sigmoid(gate_m) * (y_m @ W['w_br_ml'][l])
              + jax.nn.sigmoid(gate_s) * (y_s @ W['w_br_ssm'][l]))
    x = x + merged @ W['w_out_mix'][l]

    h = rmsnorm(x, W['norm_mem_w'][l])
    x = x + mem_attention(h, mem_k, mem_v, W['mem_wq'][l], W['mem_wo'][l])

    h = rmsnorm(x, W['norm_mlp_w'][l])
    x = x + jnp.square(jax.nn.relu(h @ W['mlp_w1'][l])) @ W['mlp_w2'][l]
    return x, (ret_S, ml_C, ml_n, ml_m, ml_buf, ssm_S, ssm_buf)


def trunk(x, pos, states, mem_ks, mem_vs, W):
    new = []
    for l in range(DEPTH):
        x, st = layer(l, x, pos, states[l], mem_ks[l], mem_vs[l], W)
        new.append(st)
    y = rmsnorm(x, W['norm_f_w'])
    stacked = [jnp.stack([new[l][i] for l in range(DEPTH)]) for i in range(7)]
    return y, stacked


def setup_inputs(seed: int = 0) -> dict:
    key = jax.random.key(seed)
    keys = jax.random.split(key, 64)
    counter = [0]

    def nk():
        counter[0] += 1
        return keys[counter[0] - 1]

    def nrm(shape, scale=1.0):
        return jax.random.normal(nk(), shape, jnp.float32) * scale

    def gain(shape):
        return 1.0 + nrm(shape, 0.02)

    dt0 = jnp.exp(jax.random.uniform(nk(), (DEPTH, SSM_H), jnp.float32, math.log(1e-3), math.log(1e-1)))
    dt_bias = dt0 + jnp.log(-jnp.expm1(-dt0))
    a_log = jnp.log(jax.random.uniform(nk(), (DEPTH, SSM_H), jnp.float32, 1.0, 16.0))
    f_bias = jnp.linspace(3.0, 6.0, ML_H, dtype=jnp.float32)[None, :] + nrm((DEPTH, ML_H), 0.01)
    i_bias = nrm((DEPTH, ML_H), 0.1)
    return {
        'x_prompt': nrm((BATCH, SEQ, D_MODEL)),
        'x_sample': nrm((DEC_BATCH, DEC_SEQ, D_MODEL)),
        'mem_prompt': nrm((BATCH, MEM_LEN, D_MODEL)),
        'state_ret': nrm((DEPTH, DEC_BATCH, RET_H, RET_DK, RET_DV), 0.1),
        'state_mlstm_C': nrm((DEPTH, DEC_BATCH, ML_H, ML_DH, ML_DH), 0.1),
        'state_mlstm_n': nrm((DEPTH, DEC_BATCH, ML_H, ML_DH), 0.5),
        'state_mlstm_m': nrm((DEPTH, DEC_BATCH, ML_H), 0.5),
        'state_mlstm_conv': nrm((DEPTH, DEC_BATCH, CONV_K - 1, ML_W)),
        'state_ssm': nrm((DEPTH, DEC_BATCH, SSM_H, SSM_P, SSM_N), 0.1),
        'state_ssm_conv': nrm((DEPTH, DEC_BATCH, CONV_K - 1, SSM_CONV_DIM)),
        'cache_mem_k': nrm((DEPTH, DEC_BATCH, MEM_LEN, MEM_H, MEM_DH)),
        'cache_mem_v': nrm((DEPTH, DEC_BATCH, MEM_LEN, MEM_H, MEM_DH)),
        'norm_mix_w': gain((DEPTH, D_MODEL)),
        'w_in': nrm((DEPTH, D_MODEL, N_IN), D_MODEL ** -0.5),
        'ret_norm_w': gain((DEPTH, RET_V)),
        'ml_conv_w': nrm((DEPTH, CONV_K, ML_W), CONV_K ** -0.5),
        'ml_conv_b': nrm((DEPTH, ML_W), 0.02),
        'ml_wq': nrm((DEPTH, ML_H, ML_DH, ML_DH), ML_DH ** -0.5),
        'ml_wk': nrm((DEPTH, ML_H, ML_DH, ML_DH), ML_DH ** -0.5),
        'ml_gate_b': jnp.concatenate([i_bias, f_bias], axis=-1),
        'ml_norm_w': gain((DEPTH, ML_W)),
        'ssm_conv_w': nrm((DEPTH, CONV_K, SSM_CONV_DIM), CONV_K ** -0.5),
        'ssm_conv_b': nrm((DEPTH, SSM_CONV_DIM), 0.02),
        'ssm_dt_bias': dt_bias,
        'ssm_A_log': a_log,
        'ssm_D': gain((DEPTH, SSM_H)),
        'ssm_norm_w': gain((DEPTH, SSM_W)),
        'w_br_ret': nrm((DEPTH, RET_V, D_MODEL), RET_V ** -0.5),
        'w_br_ml': nrm((DEPTH, ML_W, D_MODEL), ML_W ** -0.5),
        'w_br_ssm': nrm((DEPTH, SSM_W, D_MODEL), SSM_W ** -0.5),
        'w_out_mix': nrm((DEPTH, D_MODEL, D_MODEL), D_MODEL ** -0.5),
        'norm_mem_w': gain((DEPTH, D_MODEL)),
        'mem_wq': nrm((DEPTH, D_MODEL, MEM_H * MEM_DH), D_MODEL ** -0.5),
        'mem_wk': nrm((DEPTH, D_MODEL, MEM_H * MEM_DH), D_MODEL ** -0.5),
        'mem_wv': nrm((DEPTH, D_MODEL, MEM_H * MEM_DH), D_MODEL ** -0.5),
        'mem_wo': nrm((DEPTH, MEM_H * MEM_DH, D_MODEL), (MEM_H * MEM_DH) ** -0.5),
        'norm_mlp_w': gain((DEPTH, D_MODEL)),
        'mlp_w1': nrm((DEPTH, D_MODEL, D_FF), D_MODEL ** -0.5),
        'mlp_w2': nrm((DEPTH, D_FF, D_MODEL), D_FF ** -0.5),
        'norm_f_w': gain((D_MODEL,)),
    }


def reference(x_prompt, x_sample, mem_prompt, state_ret, state_mlstm_C, state_mlstm_n, state_mlstm_m,
              state_mlstm_conv, state_ssm, state_ssm_conv, cache_mem_k, cache_mem_v,
              norm_mix_w, w_in, ret_norm_w, ml_conv_w, ml_conv_b, ml_wq, ml_wk, ml_gate_b, ml_norm_w,
              ssm_conv_w, ssm_conv_b, ssm_dt_bias, ssm_A_log, ssm_D, ssm_norm_w,
              w_br_ret, w_br_ml, w_br_ssm, w_out_mix, norm_mem_w, mem_wq, mem_wk, mem_wv, mem_wo,
              norm_mlp_w, mlp_w1, mlp_w2, norm_f_w):
    W = dict(norm_mix_w=norm_mix_w, w_in=w_in, ret_norm_w=ret_norm_w, ml_conv_w=ml_conv_w,
             ml_conv_b=ml_conv_b, ml_wq=ml_wq, ml_wk=ml_wk, ml_gate_b=ml_gate_b, ml_norm_w=ml_norm_w,
             ssm_conv_w=ssm_conv_w, ssm_conv_b=ssm_conv_b, ssm_dt_bias=ssm_dt_bias, ssm_A_log=ssm_A_log,
             ssm_D=ssm_D, ssm_norm_w=ssm_norm_w, w_br_ret=w_br_ret, w_br_ml=w_br_ml, w_br_ssm=w_br_ssm,
             w_out_mix=w_out_mix, norm_mem_w=norm_mem_w, mem_wq=mem_wq, mem_wo=mem_wo,
             norm_mlp_w=norm_mlp_w, mlp_w1=mlp_w1, mlp_w2=mlp_w2, norm_f_w=norm_f_w)

    nb = x_prompt.shape[0]
    pos_p = jnp.arange(x_prompt.shape[1], dtype=jnp.int32)
    init_p = zero_states(nb, x_prompt.dtype)
    mk_p = [(mem_prompt @ mem_wk[l]).reshape(nb, MEM_LEN, MEM_H, MEM_DH) for l in range(DEPTH)]
    mv_p = [(mem_prompt @ mem_wv[l]).reshape(nb, MEM_LEN, MEM_H, MEM_DH) for l in range(DEPTH)]
    y_prompt, sp = trunk(x_prompt, pos_p, [init_p] * DEPTH, mk_p, mv_p, W)
    ret_p, mlC_p, mln_p, mlm_p, mlconv_p, ssm_p, ssmconv_p = sp
    memk_p = jnp.stack(mk_p)
    memv_p = jnp.stack(mv_p)

    pos_s = PAST_LEN + jnp.arange(x_sample.shape[1], dtype=jnp.int32)
    init_s = [(state_ret[l], state_mlstm_C[l], state_mlstm_n[l], state_mlstm_m[l], state_mlstm_conv[l],
               state_ssm[l], state_ssm_conv[l]) for l in range(DEPTH)]
    y_sample, ss = trunk(x_sample, pos_s, init_s, [cache_mem_k[l] for l in range(DEPTH)],
                         [cache_mem_v[l] for l in range(DEPTH)], W)
    ret_s, mlC_s, mln_s, mlm_s, mlconv_s, ssm_s, ssmconv_s = ss

    return (y_prompt, y_sample, ret_p, mlC_p, mln_p, mlm_p, mlconv_p, ssm_p, ssmconv_p, memk_p, memv_p,
            ret_s, mlC_s, mln_s, mlm_s, mlconv_s, ssm_s, ssmconv_s)
```

```python
import contextlib
import os
import numpy as np
import concourse.bass as bass
import concourse.mybir as mybir
from concourse.bass_utils import run_bass_kernel_spmd

F32 = mybir.dt.float32
BF16 = mybir.dt.bfloat16
AF = mybir.ActivationFunctionType
ALU = mybir.AluOpType
AX = mybir.AxisListType

NCORES = 8
D = 1024
SEQ = 2048
DEPTH = 2
NSMP = 16
MEMLEN = 256
EPS = 1e-6
NCH = 2
TT = 128 * NCH
NTILES = SEQ // TT
NIN = 12312
C_Q, C_K, C_V, C_G, C_U, C_VM, C_OM, C_IG, C_Z, C_XBC, C_DT, C_GR, C_GM, C_GS = (
    0, 512, 1024, 2048, 3072, 4096, 5120, 6144, 6152, 7176, 9224, 9240, 10264, 11288)
PAST = 16384
NEG = -1.0e30


class Op:
    __slots__ = ("eng", "emit", "deps", "pos", "need_inc", "is_dma", "sem", "val", "waits", "swkey")


class Tracker:
    ENGS = ("pe", "act", "dve", "pool", "sp")

    def __init__(self, nc, n_dma_sems=48):
        self.nc = nc
        self.ops = {e: [] for e in self.ENGS}
        self.bufs = {}
        self.n_dma_sems = n_dma_sems
        self.dma_count = 0
        self.dma_hist = []
        self.out_dmas = []
        self.sw_count = 0
        self.sw_hist = []
        self.n_sw_sems = 8

    def add(self, eng, emit, reads=(), writes=(), dma=False, is_out=False, swkey=None):
        op = Op()
        op.eng = eng
        op.emit = emit
        op.is_dma = dma
        op.need_inc = dma
        op.sem = None
        op.val = None
        deps = []
        bufs = self.bufs
        for k in reads:
            st = bufs.get(k)
            if st is not None and st[0] is not None:
                deps.append(st[0])
            if st is not None and k[0] == "ps":
                for r in st[1]:
                    if r.eng != eng:
                        deps.append(r)
        for k in writes:
            st = bufs.get(k)
            if st is not None:
                if st[0] is not None:
                    deps.append(st[0])
                deps.extend(st[1])
        for k in reads:
            st = bufs.get(k)
            if st is None:
                st = bufs[k] = [None, []]
            st[1].append(op)
        for k in writes:
            bufs[k] = [op, []]
        op.swkey = None
        if dma and eng == "pool":
            j = self.sw_count
            self.sw_count += 1
            op.sem = ("sw", j % self.n_sw_sems)
            op.val = 16 * (j // self.n_sw_sems + 1)
            if j >= self.n_sw_sems:
                deps.append(self.sw_hist[j - self.n_sw_sems])
            self.sw_hist.append(op)
            if is_out:
                self.out_dmas.append(op)
        elif dma:
            j = self.dma_count
            self.dma_count += 1
            op.sem = ("dma", j % self.n_dma_sems)
            op.val = 16 * (j // self.n_dma_sems + 1)
            if j >= self.n_dma_sems:
                deps.append(self.dma_hist[j - self.n_dma_sems])
            self.dma_hist.append(op)
            if is_out:
                self.out_dmas.append(op)
        fd = []
        seen = set()
        for d in deps:
            if id(d) in seen or d is op:
                continue
            seen.add(id(d))
            if eng == "pe" and d.eng == "pe" and not d.is_dma:
                continue
            d.need_inc = True
            fd.append(d)
        op.deps = fd
        op.pos = len(self.ops[eng])
        self.ops[eng].append(op)
        return op

    def emit_all(self, stack):
        nc = self.nc
        esem = {}
        for e in ("pe", "act", "dve", "pool"):
            esem[e] = stack.enter_context(nc.semaphore("sem_" + e))
        nds = min(self.n_dma_sems, max(1, self.dma_count))
        dsem = [stack.enter_context(nc.semaphore("dsem%d" % i)) for i in range(nds)]
        for e in self.ENGS:
            c = 0
            for op in self.ops[e]:
                if op.is_dma:
                    continue
                if op.need_inc:
                    c += 1
                    op.sem = ("eng", e)
                    op.val = c

        swsem = [stack.enter_context(nc.semaphore("swsem%d" % i)) for i in range(self.n_sw_sems)]

        def semobj(s):
            if s[0] == "eng":
                return esem[s[1]]
            if s[0] == "sw":
                return swsem[s[1]]
            return dsem[s[1]]

        for e in self.ENGS:
            wm = {}
            for op in self.ops[e]:
                need = {}
                for d in op.deps:
                    if need.get(d.sem, 0) < d.val:
                        need[d.sem] = d.val
                ws = []
                for s, v in need.items():
                    if wm.get(s, 0) >= v:
                        continue
                    wm[s] = v
                    ws.append((s, v))
                op.waits = ws
        block = stack.enter_context(nc.Block())
        cnt = {e: 0 for e in self.ENGS}

        def run(engname, eng):
            for op in self.ops[engname]:
                for s, v in op.waits:
                    eng.wait_ge(semobj(s), v)
                    cnt[engname] += 1
                ins = op.emit(eng)
                cnt[engname] += 1
                if op.need_inc:
                    ins.then_inc(semobj(op.sem), 16 if op.is_dma else 1)
            if engname == "sp":
                fin = {}
                for op in self.out_dmas:
                    if fin.get(op.sem, 0) < op.val:
                        fin[op.sem] = op.val
                for s, v in fin.items():
                    eng.wait_ge(semobj(s), v)

        @block.tensor
        def _(eng):
            run("pe", eng)

        @block.scalar
        def _(eng):
            run("act", eng)

        @block.vector
        def _(eng):
            run("dve", eng)

        @block.gpsimd
        def _(eng):
            run("pool", eng)

        @block.sync
        def _(eng):
            run("sp", eng)
        return cnt


class V:
    __slots__ = ("ap", "keys")

    def __init__(self, ap, keys):
        self.ap = ap
        self.keys = tuple(keys)

    def __getitem__(self, idx):
        return V(self.ap[idx], self.keys)

    def k(self, *keys):
        return V(self.ap, keys)

    def bc(self, shape):
        return V(self.ap.to_broadcast(list(shape)), self.keys)

    def re(self, s, **kw):
        return V(self.ap.rearrange(s, **kw), self.keys)

    def bitcast(self, dt):
        return V(self.ap.bitcast(dt), self.keys)


def _keys(vs):
    out = []
    for v in vs:
        if v is None:
            continue
        out.extend(v.keys)
    return out


class Bld:
    def __init__(self, nc):
        self.nc = nc
        self.T = Tracker(nc)
        self.st = contextlib.ExitStack()
        self.sb_bytes = 0
        ps = self.st.enter_context(nc.psum_tensor("psum", [128, 4096], F32))
        self.psum = ps
        self.ps_ctr = 0
        self.uid = 0
        self.ps_open = {}

    def sb(self, name, shape, dt, keys=None):
        t = self.st.enter_context(self.nc.sbuf_tensor("s_" + name, list(shape), dt))
        n = 1
        for s in shape[1:]:
            n *= s
        self.sb_bytes += n * (2 if dt == BF16 else 4)
        return V(t[:], keys if keys is not None else [name])

    def pb(self, n=1):
        if n == 2 and self.ps_ctr % 2 == 1:
            self.ps_ctr += 1
        i = self.ps_ctr % 8
        self.ps_ctr += n
        for j in range(n):
            assert not self.ps_open.get(i + j, False), "PSUM bank %d still open" % (i + j)
            self.ps_open[i + j] = True
        return V(self.psum[:, i * 512:(i + n) * 512], [("ps", i + j) for j in range(n)])

    def pf(self, v):
        for k in v.keys:
            self.ps_open[k[1]] = False

    def mm(self, out, lhsT, rhs, start=True, stop=True):
        o, l, r = out.ap, lhsT.ap, rhs.ap
        self.T.add("pe", lambda e: e.matmul(o, lhsT=l, rhs=r, start=start, stop=stop),
                   reads=_keys([lhsT, rhs]), writes=out.keys)

    def tr(self, out, in_, ident):
        o, i, d = out.ap, in_.ap, ident.ap
        self.T.add("pe", lambda e: e.transpose(o, i, d), reads=_keys([in_, ident]), writes=out.keys)

    def act(self, out, in_, func, bias=None, scale=None, accum=None, extra_reads=()):
        kw = {}
        rd = [in_]
        if bias is not None:
            if isinstance(bias, V):
                kw["bias"] = bias.ap
                rd.append(bias)
            else:
                kw["bias"] = float(bias)
        if scale is not None:
            if isinstance(scale, V):
                kw["scale"] = scale.ap
                rd.append(scale)
            else:
                kw["scale"] = float(scale)
        wr = list(out.keys)
        if accum is not None:
            kw["accum_out"] = accum.ap
            wr += list(accum.keys)
        o, i = out.ap, in_.ap
        self.T.add("act", lambda e: e.activation(out=o, in_=i, func=func, **kw),
                   reads=_keys(rd) + list(extra_reads), writes=wr)

    def tt(self, out, a, b, op, eng="dve"):
        o, x, y = out.ap, a.ap, b.ap
        self.T.add(eng, lambda e: e.tensor_tensor(out=o, in0=x, in1=y, op=op),
                   reads=_keys([a, b]), writes=out.keys)

    def ts(self, out, a, s1, op0, s2=None, op1=None, eng="dve", accum=None):
        rd = [a]
        if isinstance(s1, V):
            rd.append(s1)
            s1 = s1.ap
        else:
            s1 = float(s1)
        if isinstance(s2, V):
            rd.append(s2)
            s2 = s2.ap
        elif s2 is not None:
            s2 = float(s2)
        o, x = out.ap, a.ap
        kw = {}
        wr = list(out.keys)
        if op1 is not None:
            kw["op1"] = op1
        if accum is not None:
            kw["accum_out"] = accum.ap
            wr += list(accum.keys)
        self.T.add(eng, lambda e: e.tensor_scalar(out=o, in0=x, scalar1=s1, scalar2=s2, op0=op0, **kw),
                   reads=_keys(rd), writes=wr)

    def stt(self, out, a, s, b, op0, op1):
        rd = [a, b]
        if isinstance(s, V):
            rd.append(s)
            s = s.ap
        else:
            s = float(s)
        o, x, y = out.ap, a.ap, b.ap
        self.T.add("dve", lambda e: e.scalar_tensor_tensor(out=o, in0=x, scalar=s, in1=y, op0=op0, op1=op1),
                   reads=_keys(rd), writes=out.keys)

    def cp(self, out, in_, eng="dve"):
        o, i = out.ap, in_.ap
        if eng == "act":
            self.T.add("act", lambda e: e.copy(out=o, in_=i), reads=in_.keys, writes=out.keys)
        else:
            self.T.add(eng, lambda e: e.tensor_copy(out=o, in_=i), reads=in_.keys, writes=out.keys)

    def red(self, out, in_, op, axis=AX.X):
        o, i = out.ap, in_.ap
        self.T.add("dve", lambda e: e.tensor_reduce(out=o, in_=i, axis=axis, op=op),
                   reads=in_.keys, writes=out.keys)

    def recip(self, out, in_):
        o, i = out.ap, in_.ap
        self.T.add("dve", lambda e: e.reciprocal(out=o, in_=i), reads=in_.keys, writes=out.keys)

    def memset(self, out, val, eng="dve"):
        o = out.ap
        self.T.add(eng, lambda e: e.memset(o, val), writes=out.keys)

    def bnstats(self, out, in_):
        o, i = out.ap, in_.ap
        self.T.add("dve", lambda e: e.bn_stats(out=o, in_=i), reads=in_.keys, writes=out.keys)

    def bnaggr(self, out, in_):
        o, i = out.ap, in_.ap
        self.T.add("dve", lambda e: e.bn_aggr(out=o, in_=i), reads=in_.keys, writes=out.keys)

    def dma_in(self, out, src_ap, q="sp", small=False):
        o = out.ap
        swk = out.keys[0] if q == "pool" else None
        if small:
            self.T.add(q, lambda e: e.dma_start(out=o, in_=src_ap, allow_slow_non_contiguous=True),
                       writes=out.keys, dma=True, swkey=swk)
        else:
            self.T.add(q, lambda e: e.dma_start(out=o, in_=src_ap), writes=out.keys, dma=True, swkey=swk)

    def dma_out(self, dst_ap, src, q="sp", small=False):
        i = src.ap
        if small:
            self.T.add(q, lambda e: e.dma_start(out=dst_ap, in_=i, allow_slow_non_contiguous=True),
                       reads=src.keys, dma=True, is_out=True)
        else:
            self.T.add(q, lambda e: e.dma_start(out=dst_ap, in_=i), reads=src.keys, dma=True, is_out=True)


CST_W = 1928


def make_consts():
    c = np.zeros((128, CST_W), np.float32)
    idx = np.arange(128)
    c[:, 0:128] = np.eye(128, dtype=np.float32)
    le = (idx[:, None] <= idx[None, :])
    c[:, 128:256] = le.astype(np.float32)
    c[:, 256:384] = 1.0
    c[:, 384:512] = np.where(le, 0.0, NEG)
    c[:, 512:640] = np.where(idx[None, :] <= idx[:, None], 0.0, NEG)
    g = 1.0 - np.exp2(-5.0 - np.arange(4, dtype=np.float64))
    lg = np.log(g)
    diff = (idx[None, :] - idx[:, None]).astype(np.float64)
    for h in range(4):
        dec = np.where(diff >= 0, np.exp(lg[h] * np.where(diff >= 0, diff, 0.0)), 0.0)
        c[:, 640 + h * 128: 640 + (h + 1) * 128] = dec
        c[:, 1152 + h * 128: 1152 + (h + 1) * 128] = np.exp(lg[h] * (idx + 1.0))[None, :]
        c[:, 1664 + h] = np.exp(lg[h] * (127.0 - idx))
    c[:, 1672:1928] = np.eye(16, dtype=np.float32).reshape(1, 256)
    cdec = [float(np.exp(lg[h] * 128.0)) for h in range(4)]
    g1 = [float(g[h]) for h in range(4)]
    half = 64
    freqs = (np.float32(10000.0) ** (-(np.arange(half, dtype=np.float32) / np.float32(half)))).astype(np.float32)
    pos = np.concatenate([np.arange(SEQ), np.full(16, PAST)]).astype(np.float32)
    ang = (pos[:, None] * freqs[None, :]).astype(np.float32).astype(np.float64)
    sc = 128.0 ** -0.5
    rope = np.concatenate([np.cos(ang), np.sin(ang), np.cos(ang) * sc, np.sin(ang) * sc], axis=1).astype(np.float32)
    return c, rope, cdec, g1


RET_CDEC = None
RET_G = None


def build_program(do_prompt=True, do_sample=True, n_tiles=NTILES, dbg=False, stage=99, memkv=True):
    global RET_CDEC, RET_G
    cst_np, rope_np, RET_CDEC, RET_G = make_consts()
    nc = bass.Bass("TRN2", target_bir_lowering=False)
    B = Bld(nc)

    def din(name, shape):
        return nc.dram_tensor(name, list(shape), F32, kind="ExternalInput").ap()

    def dout(name, shape):
        return nc.dram_tensor(name, list(shape), F32, kind="ExternalOutput").ap()

    I = {}
    I["x_p"] = din("x_p", [SEQ, D])
    I["x_s"] = din("x_s", [NSMP, D])
    I["mem"] = din("mem", [MEMLEN, D])
    I["cst"] = din("cst", [128, CST_W])
    I["rope"] = din("rope", [SEQ + 16, 256])
    I["st_ret"] = din("st_ret", [DEPTH, NSMP, 4, 128, 256])
    I["st_mlC"] = din("st_mlC", [DEPTH, NSMP, 4, 256, 256])
    I["st_mln"] = din("st_mln", [DEPTH, NSMP, 4, 256])
    I["st_mlm"] = din("st_mlm", [DEPTH, NSMP, 4])
    I["st_mlconv"] = din("st_mlconv", [DEPTH, NSMP, 3, 1024])
    I["st_ssm"] = din("st_ssm", [DEPTH, NSMP, 16, 64, 128])
    I["st_ssmconv"] = din("st_ssmconv", [DEPTH, NSMP, 3, 2048])
    I["c_memk"] = din("c_memk", [DEPTH, NSMP, MEMLEN, 1024])
    I["c_memv"] = din("c_memv", [DEPTH, NSMP, MEMLEN, 1024])
    wshapes = dict(
        norm_mix_w=[DEPTH, D], w_in=[DEPTH, D, NIN], ret_norm_w=[DEPTH, 1024], ml_conv_w=[DEPTH, 4, 1024],
        ml_conv_b=[DEPTH, 1024], ml_wq=[DEPTH, 4, 256, 256], ml_wk=[DEPTH, 4, 256, 256], ml_gate_b=[DEPTH, 8],
        ml_norm_w=[DEPTH, 1024], ssm_conv_w=[DEPTH, 4, 2048], ssm_conv_b=[DEPTH, 2048], ssm_dt_bias=[DEPTH, 16],
        ssm_A_log=[DEPTH, 16], ssm_D=[DEPTH, 16], ssm_norm_w=[DEPTH, 1024], w_br_ret=[DEPTH, 1024, 1024],
        w_br_ml=[DEPTH, 1024, 1024], w_br_ssm=[DEPTH, 1024, 1024], w_out_mix=[DEPTH, 1024, 1024],
        norm_mem_w=[DEPTH, D], mem_wq=[DEPTH, D, 1024], mem_wk=[DEPTH, D, 1024], mem_wv=[DEPTH, D, 1024],
        mem_wo=[DEPTH, 1024, D], norm_mlp_w=[DEPTH, D], mlp_w1=[DEPTH, D, 4096], mlp_w2=[DEPTH, 4096, D],
        norm_f_w=[D])
    W = {k: din(k, s) for k, s in wshapes.items()}
    O = {}
    O["y_p"] = dout("y_p", [SEQ, D])
    O["y_s"] = dout("y_s", [NSMP, D])
    O["ret_p"] = dout("ret_p", [DEPTH, 4, 128, 256])
    O["mlC_p"] = dout("mlC_p", [DEPTH, 4, 256, 256])
    O["mln_p"] = dout("mln_p", [DEPTH, 4, 256])
    O["mlm_p"] = dout("mlm_p", [DEPTH, 4])
    O["mlconv_p"] = dout("mlconv_p", [DEPTH, 3, 1024])
    O["ssm_p"] = dout("ssm_p", [DEPTH, 16, 64, 128])
    O["ssmconv_p"] = dout("ssmconv_p", [DEPTH, 3, 2048])
    O["memk_p"] = dout("memk_p", [DEPTH, MEMLEN, 1024])
    O["memv_p"] = dout("memv_p", [DEPTH, MEMLEN, 1024])
    O["ret_s"] = dout("ret_s", [DEPTH, NSMP, 4, 128, 256])
    O["mlC_s"] = dout("mlC_s", [DEPTH, NSMP, 4, 256, 256])
    O["mln_s"] = dout("mln_s", [DEPTH, NSMP, 4, 256])
    O["mlm_s"] = dout("mlm_s", [DEPTH, NSMP, 4])
    O["mlconv_s"] = dout("mlconv_s", [DEPTH, NSMP, 3, 1024])
    O["ssm_s"] = dout("ssm_s", [DEPTH, NSMP, 16, 64, 128])
    O["ssmconv_s"] = dout("ssmconv_s", [DEPTH, NSMP, 3, 2048])
    if dbg:
        O["dbg"] = dout("dbg", [SEQ, D])

    cst = B.sb("cst", [128, CST_W], F32)
    B.dma_in(cst, I["cst"])
    ident = cst[:, 0:128]
    tri = cst[:, 128:256]
    ones = cst[:, 256:384]
    maskST = cst[:, 384:512]
    maskTM = cst[:, 512:640]
    decayT = cst[:, 640:1152].re("p (h t) -> p h t", h=4)
    qdec = cst[:, 1152:1664].re("p (h t) -> p h t", h=4)
    kdec = cst[:, 1664:1668]
    i16bc = cst[:, 1672:1928].re("p (a b) -> p a b", a=16)
    identb = B.sb("identb", [128, 128], BF16)
    B.cp(identb, ident)
    onesb = B.sb("onesb", [128, 128], BF16)
    B.cp(onesb, ones)

    NSLAB = 4
    slabs = [B.sb("slab%d" % i, [128, 8, 512], BF16, keys=[("slab", i)]) for i in range(NSLAB)]
    slab_ctr = [0]

    WC_MAX = 136
    wcache = nc.dram_tensor("wcache", [WC_MAX, 128, 4096], BF16, kind="Internal").ap()
    wc_ids = {}
    use_cache = not os.environ.get("NO_WCACHE")

    def cached_load(v, src, kc, n):
        key = repr(src)
        if not use_cache:
            B.dma_in(v, src, q="pool")
            return
        if key not in wc_ids:
            sid = len(wc_ids)
            assert sid < WC_MAX
            wc_ids[key] = sid
            B.dma_in(v, src, q="pool")
            dst = wcache[sid][:, 0:kc * n].rearrange("p (k n) -> p k n", k=kc)
            vap = v.ap
            B.T.add("sp", lambda e: e.dma_start(out=dst, in_=vap), reads=v.keys, writes=[("wc", sid)], dma=True)
        else:
            sid = wc_ids[key]
            srcc = wcache[sid][:, 0:kc * n].rearrange("p (k n) -> p k n", k=kc)
            vap = v.ap
            B.T.add("sp", lambda e: e.dma_start(out=vap, in_=srcc), reads=[("wc", sid)], writes=v.keys, dma=True)

    def load_slab(src, kc, n):
        i = slab_ctr[0] % NSLAB
        slab_ctr[0] += 1
        v = slabs[i][:, 0:kc, 0:n]
        cached_load(v, src, kc, n)
        return v

    def win_slab(l, c0, n=512):
        return load_slab(W["w_in"][l][:, c0:c0 + n].rearrange("(kc p) n -> p kc n", p=128), 8, n)

    def sq_slab(w2d, half):
        return load_slab(w2d[:, half * 512:(half + 1) * 512].rearrange("(kc p) n -> p kc n", p=128), 8, 512)

    pstage = B.sb("pstage", [128, 128], F32)
    P = []
    for l in range(DEPTH if not os.environ.get('SKIP_PARAMS') else 0):
        p = {}
        pa = B.sb("parA%d" % l, [128, 72], F32)
        pbt = B.sb("parB%d" % l, [128, 96], F32)
        r = 0
        offs = {}
        for nm, key, nb in (("g_mix", "norm_mix_w", 8), ("g_mem", "norm_mem_w", 8), ("g_mlp", "norm_mlp_w", 8),
                            ("g_ret", "ret_norm_w", 8), ("g_ml", "ml_norm_w", 8), ("g_ssm", "ssm_norm_w", 8),
                            ("cb_ml", "ml_conv_b", 8), ("cb_ssm", "ssm_conv_b", 16)):
            B.dma_in(pstage[r:r + nb, :], W[key][l].rearrange("(b p) -> b p", p=128))
            offs[nm] = (r, nb)
            r += nb
        ps = B.pb()
        B.tr(ps[:, 0:72], pstage[0:72, :], ident[0:72, 0:72])
        B.cp(pa, ps[:, 0:72])
        B.pf(ps)
        for nm, (r0, nb) in offs.items():
            p[nm] = pa[:, r0:r0 + nb]
        B.dma_in(pstage[0:32, :], W["ml_conv_w"][l].rearrange("j (b p) -> (j b) p", p=128))
        B.dma_in(pstage[32:96, :], W["ssm_conv_w"][l].rearrange("j (b p) -> (j b) p", p=128))
        ps = B.pb()
        B.tr(ps[:, 0:96], pstage[0:96, :], ident[0:96, 0:96])
        B.cp(pbt, ps[:, 0:96])
        B.pf(ps)
        p["cw_ml"] = pbt[:, 0:32].re("p (j b) -> p b j", j=4)
        p["cw_ssm"] = pbt[:, 32:96].re("p (j b) -> p b j", j=4)
        t = B.sb("gateb%d" % l, [128, 8], F32)
        B.dma_in(t, W["ml_gate_b"][l].partition_broadcast(128))
        p["gateb"] = t
        t = B.sb("dtb%d" % l, [128, 16], F32)
        B.dma_in(t, W["ssm_dt_bias"][l].partition_broadcast(128))
        p["dtb"] = t
        t = B.sb("alog%d" % l, [128, 16], F32)
        B.dma_in(t, W["ssm_A_log"][l].partition_broadcast(128))
        nega = B.sb("nega%d" % l, [128, 16], F32)
        B.act(nega, t, AF.Exp)
        B.ts(nega, nega, -1.0, ALU.mult)
        p["nega"] = nega
        t = B.sb("ssmD%d" % l, [128, 16], F32)
        B.dma_in(t, W["ssm_D"][l].partition_broadcast(128))
        p["ssmD"] = t
        P.append(p)

    xt = B.sb("xt", [128, NCH, D], F32)
    hT = B.sb("hT", [128, 8, TT], BF16)
    hb = B.sb("hb", [128, D], BF16)
    junk = B.sb("junk", [128, D], BF16)
    f32a = B.sb("f32a", [128, D], F32, keys=[("f32a", q) for q in range(4)])
    f32a_ap = f32a.ap
    small = B.sb("small", [128, 256], F32)
    sm_ctr = [0]

    def sm(n, key=None):
        if sm_ctr[0] + n > 256:
            sm_ctr[0] = 0
        a = sm_ctr[0]
        sm_ctr[0] += n
        return V(small.ap[:, a:a + n], [("small", c) for c in range(a, a + n)])

    def fa(c0, c1):
        return V(f32a_ap[:, c0:c1], [("f32a", q) for q in range(c0 // 256, (c1 + 255) // 256)])

    def rms_rstd(src, width, key):
        ss = sm(1)
        B.act(junk[:, 0:width], src, AF.Square, accum=ss)
        sq = sm(1, key + "sq")
        B.act(sq, ss, AF.Ln, scale=1.0 / width, bias=EPS)
        rs = sm(1)
        B.act(rs, sq, AF.Exp, scale=-0.5)
        return rs

    def transpose_to(dst, src_tm, nblk, gain=None, evac="dve"):
        for b0 in range(0, nblk, 8):
            nb = min(8, nblk - b0)
            ps = B.pb().bitcast(BF16)
            psv = ps[:, 0:nb * 128].re("p (a b) -> p a b", a=nb)
            for j in range(nb):
                B.tr(psv[:, j, :], src_tm[:, (b0 + j) * 128:(b0 + j + 1) * 128], identb)
            d = dst[:, b0:b0 + nb, :]
            if gain is not None:
                B.tt(d, psv, gain[:, b0:b0 + nb].re("p (a o) -> p a o", o=1).bc([128, nb, 128]), ALU.mult)
            elif evac == "act":
                B.cp(d, psv, eng="act")
            else:
                B.cp(d, psv)
            B.pf(ps)

    def norm_to_hT(gain):
        for s in range(NCH):
            rs = rms_rstd(xt[:, s, :], D, "n")
            B.ts(hb, xt[:, s, :], rs, ALU.mult)
            transpose_to(hT[:, :, s * 128:(s + 1) * 128], hb, 8, gain=gain)

    def proj_tm(src_T, s, slab, evac):
        n = slab.ap.shape[2]
        ps = B.pb()
        for kc in range(8):
            B.mm(ps[:, 0:n], src_T[:, kc, s * 128:(s + 1) * 128], slab[:, kc, :], start=(kc == 0), stop=(kc == 7))
        evac(ps[:, 0:n])
        B.pf(ps)

    def proj_fm(src_T, slab, cb, evac, ntok=TT):
        ps = B.pb()
        for kc in range(8):
            B.mm(ps[:, 0:ntok], slab[:, kc, cb * 128:(cb + 1) * 128], src_T[:, kc, 0:ntok],
                 start=(kc == 0), stop=(kc == 7))
        evac(ps[:, 0:ntok])
        B.pf(ps)

    memT = B.sb("memT", [128, 8, MEMLEN], BF16)
    KT = [B.sb("KT%d" % l, [128, 8, MEMLEN], BF16) for l in range(DEPTH)]
    Vtm = [B.sb("Vtm%d" % l, [128, 2, 1024], BF16) for l in range(DEPTH)]
    MKL = int(os.environ.get("MEMKV_LEVEL", "9"))
    mk_ng = [0]
    if do_prompt and memkv:
        for mc in range(2):
            B.dma_in(hb, I["mem"][mc * 128:(mc + 1) * 128, :], q="pool")
            transpose_to(memT[:, :, mc * 128:(mc + 1) * 128], hb, 8)
        for l in range(DEPTH if MKL >= 2 else 0):
            for which, wname, oname in ((0, "mem_wk", "memk_p"), (1, "mem_wv", "memv_p")):
                for half in range(2):
                    slab = sq_slab(W[wname][l], half)
                    for mc in range(2 if MKL >= 3 else 0):
                        mk_ng[0] += 1
                        if mk_ng[0] > int(os.environ.get("MK_NG", "999")):
                            continue
                        ps = B.pb()
                        for kc in range(8):
                            B.mm(ps, memT[:, kc, mc * 128:(mc + 1) * 128], slab[:, kc, :], start=(kc == 0), stop=(kc == 7))
                        stg = fa(0, 512) if (mc % 2 == 0) else fa(512, 1024)
                        B.cp(stg, ps, eng="act")
                        if MKL >= 4:
                            B.dma_out(O[oname][l][mc * 128:(mc + 1) * 128, half * 512:(half + 1) * 512], stg)
                        if which == 1 and not os.environ.get("MK_NOV"):
                            B.cp(Vtm[l][:, mc, half * 512:(half + 1) * 512], ps)
                        B.pf(ps)
                    if which == 0 and MKL >= 5:
                        for cb in range(4):
                            ps = B.pb()
                            for kc in range(8):
                                B.mm(ps[:, 0:MEMLEN], slab[:, kc, cb * 128:(cb + 1) * 128], memT[:, kc, :],
                                     start=(kc == 0), stop=(kc == 7))
                            B.cp(KT[l][:, half * 4 + cb, :], ps[:, 0:MEMLEN])
                            B.pf(ps)

    retS = [B.sb("retS%d" % l, [128, 4, 256], F32) for l in range(DEPTH)]
    mlC = [B.sb("mlC%d" % l, [128, 8, 256], F32) for l in range(DEPTH)]
    mln = [B.sb("mln%d" % l, [128, 8], F32) for l in range(DEPTH)]
    mlm = [B.sb("mlm%d" % l, [128, 4], F32) for l in range(DEPTH)]
    ssmS = [B.sb("ssmS%d" % l, [128, 4, 256], F32) for l in range(DEPTH)]
    hist_ml = [B.sb("hist_ml%d" % l, [128, 8, 3], BF16) for l in range(DEPTH)]
    hist_ssm = [B.sb("hist_ssm%d" % l, [128, 16, 3], BF16) for l in range(DEPTH)]
    stb = B.sb("stb", [128, 8, 256], BF16)
    nb16 = B.sb("nb16", [128, 8], BF16)
    if do_prompt and not os.environ.get('SKIP_MEMSET'):
        for l in range(DEPTH):
            for t in (retS[l], mlC[l], mln[l], mlm[l], ssmS[l], hist_ml[l], hist_ssm[l]):
                B.memset(t, 0.0)

    ropet = B.sb("ropet", [128, NCH, 256], F32)
    qkr = B.sb("qkr", [128, NCH, 1024], BF16)
    v_tm = B.sb("v_tm", [128, NCH, 1024], BF16)
    gsil = B.sb("gsil", [128, NCH, 1024], BF16)
    yT = B.sb("yT", [128, 8, TT], BF16)
    merged = B.sb("merged", [128, NCH, D], F32)
    gtmp = B.sb("gtmp", [128, 512], F32)
    yb = B.sb("yb", [128, D], BF16)
    qkT = B.sb("qkT", [128, 8, 128], BF16)
    qTd = B.sb("qTd", [128, 4, 128], BF16)
    kd = B.sb("kd", [128, 512], BF16)
    scm = B.sb("scm", [128, 4, 128], BF16)
    UW = TT + 8
    big = B.sb("big", [128, 16 * UW + 16 * TT], BF16)
    upre_ap = big.ap[:, 0:16 * UW].rearrange("p (b w) -> p b w", b=16)
    ucT_ap = big.ap[:, 16 * UW:16 * UW + 16 * TT].rearrange("p (b w) -> p b w", b=16)
    big_keys = [("upre", b) for b in range(16)] + [("ucT", b) for b in range(16)]
    upre = V(upre_ap, [("upre", b) for b in range(16)])
    ucT = V(ucT_ap, [("ucT", b) for b in range(16)])

    def upre_b(b):
        return V(upre_ap[:, b, :], [("upre", b)])

    def ucT_b(b):
        return V(ucT_ap[:, b, :], [("ucT", b)])
    h1T = V(big.ap[:, 0:32 * TT].rearrange("p (j t) -> p j t", j=32), big_keys)
    diag = B.sb("diag", [128, 4, 128], BF16)
    igf = B.sb("igf", [128, NCH, 16], F32)
    wqk = B.sb("wqk", [128, 2, 8, 256], BF16)
    mqT = B.sb("mqT", [128, 8, 128], BF16)
    mkT = B.sb("mkT", [128, 8, 128], BF16)
    kw = B.sb("kw", [128, 1024], BF16)
    dg = B.sb("dg", [128, 4, 128], F32)
    Dm = B.sb("Dm", [128, 4, 128], F32)
    xtm = V(kw.ap, kw.keys)
    xdt = V(mqT.ap.rearrange("p a b -> p (a b)"), mqT.keys)
    xw = V(mkT.ap.rearrange("p a b -> p (a b)"), mkT.keys)
    Btm = B.sb("Btm", [128, 512], BF16)
    MTs = B.sb("MTs", [128, 4, 128], F32)
    MTh = B.sb("MTh", [128, 4, 128], BF16)
    Dm2 = B.sb("Dm2", [128, 4, 128], F32)
    MTh2 = B.sb("MTh2", [128, 4, 128], BF16)
    cumT = B.sb("cumT", [16, 128], F32)
    rblk = B.sb("rblk", [16, 4, 128], F32)
    qaT = V(qkr.ap.rearrange("p a (k t) -> p (a k) t", t=TT)[:, 0:8, :], qkr.keys)
    pn = B.sb("pn", [128, 256], BF16)
    pT = V(v_tm.ap.rearrange("p a (h m t) -> p (a h) m t", m=2, t=TT)[:, 0:4, :, :], v_tm.keys)
    rl = B.sb("rl", [128, TT], F32)
    c3 = gtmp[0:3, :]
    stT = B.sb("stT", [128, 128], F32)

    def bcast_mid(v, n_mid, n_in):
        return v.re("p (a o) -> p a o", o=1).bc([128, n_mid, n_in])

    def branch(l, c_gate, wname):
        first = (wname == "w_br_ret")
        for half in range(2):
            gs = win_slab(l, c_gate + half * 512)
            ws = sq_slab(W[wname][l], half)
            for s in range(NCH):
                proj_tm(hT, s, gs, lambda ps: B.act(gtmp, ps, AF.Sigmoid))
                dst = merged[:, s, half * 512:(half + 1) * 512]

                def ev(ps, dst=dst):
                    if first:
                        B.tt(dst, ps, gtmp, ALU.mult)
                    else:
                        B.tt(gtmp, ps, gtmp, ALU.mult)
                        B.tt(dst, dst, gtmp, ALU.add)
                proj_tm(yT, s, ws, ev)

    def headnorm_center(src, h, scale_extra=None):
        st6 = sm(6, "bn6")
        B.bnstats(st6, src[:, h * 256:(h + 1) * 256])
        mv = sm(2, "bnmv")
        B.bnaggr(mv, st6)
        return mv

    def retention_phase(l, ti):
        p = P[l]
        for which in range(2):
            slab = win_slab(l, C_Q + which * 512)
            for s in range(NCH):
                def ev(ps, s=s, which=which):
                    cs = ropet[:, s, which * 128: which * 128 + 64]
                    sn = ropet[:, s, which * 128 + 64: which * 128 + 128]
                    csb = cs.re("p (o d) -> p o d", o=1).bc([128, 4, 64])
                    snb = sn.re("p (o d) -> p o d", o=1).bc([128, 4, 64])
                    pv = ps.re("p (h d) -> p h d", h=4)
                    x1 = pv[:, :, 0:64]
                    x2 = pv[:, :, 64:128]
                    t1 = fa(0, 256).re("p (h d) -> p h d", h=4)
                    t2 = fa(256, 512).re("p (h d) -> p h d", h=4)
                    ov = qkr[:, s, which * 512:(which + 1) * 512].re("p (h d) -> p h d", h=4)
                    B.tt(t1, x1, csb, ALU.mult)
                    B.tt(t2, x2, snb, ALU.mult)
                    B.tt(ov[:, :, 0:64], t1, t2, ALU.subtract)
                    B.tt(t1, x1, snb, ALU.mult)
                    B.tt(t2, x2, csb, ALU.mult)
                    B.tt(ov[:, :, 64:128], t1, t2, ALU.add)
                proj_tm(hT, s, slab, ev)
        for half in range(2):
            slab = win_slab(l, C_V + half * 512)
            for s in range(NCH):
                proj_tm(hT, s, slab, lambda ps, s=s, half=half: B.cp(v_tm[:, s, half * 512:(half + 1) * 512], ps, eng="act"))
        for half in range(2):
            slab = win_slab(l, C_G + half * 512)
            for s in range(NCH):
                proj_tm(hT, s, slab, lambda ps, s=s, half=half: B.act(gsil[:, s, half * 512:(half + 1) * 512], ps, AF.Silu))
        for s in range(NCH):
            ps = B.pb().bitcast(BF16).re("p (a b) -> p a b", a=8)
            for j in range(8):
                B.tr(ps[:, j, :], qkr[:, s, j * 128:(j + 1) * 128], identb)
            B.cp(qkT, ps, eng="act")
            B.tt(qTd, ps[:, 0:4, :], qdec, ALU.mult)
            B.pf(ps)
            B.tt(kd.re("p (h d) -> p h d", h=4), qkr[:, s, 512:1024].re("p (h d) -> p h d", h=4),
                 bcast_mid(kdec, 4, 128), ALU.mult)
            ps = B.pb()
            psv = ps.re("p (h t) -> p h t", h=4)
            for h in range(4):
                B.mm(psv[:, h, :], qkT[:, 4 + h, :], qkT[:, h, :])
            B.tt(scm, psv, decayT, ALU.mult)
            B.pf(ps)
            B.cp(stb[:, 0:4, :], retS[l], eng="act")
            yps = B.pb(2)
            for h in range(4):
                o = yps[:, h * 256:(h + 1) * 256]
                B.mm(o, scm[:, h, :], v_tm[:, s, h * 256:(h + 1) * 256], start=True, stop=False)
                B.mm(o, qTd[:, h, :], stb[:, h, :], start=False, stop=True)
            sps = B.pb(2)
            for h in range(4):
                B.mm(sps[:, h * 256:(h + 1) * 256], kd[:, h * 128:(h + 1) * 128], v_tm[:, s, h * 256:(h + 1) * 256])
            for h in range(4):
                mv = headnorm_center(yps, h)
                sq = sm(1, "hsq")
                B.act(sq, mv[:, 1:2], AF.Ln, bias=EPS)
                rs = sm(1)
                B.act(rs, sq, AF.Exp, scale=-0.5)
                tmp = fa(h * 256, (h + 1) * 256)
                B.ts(tmp, yps[:, h * 256:(h + 1) * 256], mv[:, 0:1], ALU.subtract, rs, ALU.mult)
                B.tt(yb[:, h * 256:(h + 1) * 256], tmp, gsil[:, s, h * 256:(h + 1) * 256], ALU.mult)
            B.pf(yps)
            transpose_to(yT[:, :, s * 128:(s + 1) * 128], yb, 8, gain=p["g_ret"])
            for h in range(4):
                B.stt(retS[l][:, h, :], retS[l][:, h, :], RET_CDEC[h], sps[:, h * 256:(h + 1) * 256], ALU.mult, ALU.add)
            B.pf(sps)
        branch(l, C_GR, "w_br_ret")

    identbc4 = V(identb.ap.rearrange("p (o s) -> p o s", o=1).to_broadcast([128, 4, 128]), identb.keys)
    identc4 = ident.re("p (o s) -> p o s", o=1).bc([128, 4, 128])
    maskST4 = maskST.re("p (o s) -> p o s", o=1).bc([128, 4, 128])
    maskTM4 = maskTM.re("p (o s) -> p o s", o=1).bc([128, 4, 128])

    def conv_blocks(l, slab, blk0, cwv, cbv, hist):
        for cb in range(4):
            blk = blk0 + cb
            ub = upre_b(blk)
            B.cp(ub[:, 0:3], hist[:, blk, :])
            proj_fm(hT, slab, cb, lambda ps, ub=ub: B.cp(ub[:, 3:3 + TT], ps, eng="act"))
            B.cp(hist[:, blk, :], ub[:, TT:TT + 3])
            B.tt(diag, identbc4, bcast_mid(cwv[:, blk, :], 4, 128), ALU.mult)
            ps = B.pb()
            for j in range(4):
                B.mm(ps[:, 0:TT], diag[:, j, :], ub[:, j:j + TT], start=(j == 0), stop=(j == 3))
            B.act(ucT_b(blk), ps[:, 0:TT], AF.Silu, bias=cbv[:, blk:blk + 1])
            B.pf(ps)

    def conv_state_out(l, slab, oname, c0):
        ps = B.pb()
        for kc in range(8):
            B.mm(ps[0:3, :], hT[:, kc, TT - 3:TT], slab[:, kc, :], start=(kc == 0), stop=(kc == 7))
        B.cp(c3, ps[0:3, :])
        B.pf(ps)
        B.dma_out(O[oname][l][:, c0:c0 + 512], c3)

    def mlstm_phase(l, ti):
        p = P[l]
        last = (ti == n_tiles - 1)
        for half in range(2):
            slab = win_slab(l, C_U + half * 512)
            conv_blocks(l, slab, half * 4, p["cw_ml"], p["cb_ml"], hist_ml[l])
            if last:
                conv_state_out(l, slab, "mlconv_p", half * 512)
        for half in range(2):
            slab = win_slab(l, C_VM + half * 512)
            for s in range(NCH):
                proj_tm(hT, s, slab, lambda ps, s=s, half=half: B.cp(v_tm[:, s, half * 512:(half + 1) * 512], ps, eng="act"))
        for half in range(2):
            slab = win_slab(l, C_OM + half * 512)
            for s in range(NCH):
                proj_tm(hT, s, slab, lambda ps, s=s, half=half: B.act(gsil[:, s, half * 512:(half + 1) * 512], ps, AF.Sigmoid))
        slab = win_slab(l, C_IG, 8)
        for s in range(NCH):
            proj_tm(hT, s, slab, lambda ps, s=s: B.cp(igf[:, s, 0:8], ps))
        for wi, wname in ((0, "ml_wq"), (1, "ml_wk")):
            cached_load(wqk[:, wi], W[wname][l].rearrange("h (dc p) e -> p (h dc) e", p=128), 8, 256)
        for s in range(NCH):
            sl = slice(s * 128, (s + 1) * 128)
            B.cp(stb, mlC[l], eng="act")
            B.cp(nb16, mln[l])
            for dstT, wi, scl in ((mqT, 0, 1.0), (mkT, 1, 1.0 / 16.0)):
                ps = B.pb(2)
                psv = ps.re("p (a t) -> p a t", a=8)
                for h in range(4):
                    for ec in range(2):
                        for dc in range(2):
                            B.mm(psv[:, h * 2 + ec, :], wqk[:, wi, h * 2 + dc, ec * 128:(ec + 1) * 128],
                                 ucT_b(h * 2 + dc)[:, sl], start=(dc == 0), stop=(dc == 1))
                B.act(dstT, psv, AF.Copy, scale=scl)
                B.pf(ps)
            z = sm(8)
            B.tt(z, igf[:, s, 0:8], p["gateb"], ALU.add)
            e = sm(4)
            B.act(e, z[:, 4:8], AF.Exp, scale=-1.0)
            sp = sm(4)
            B.act(sp, e, AF.Ln, bias=1.0)
            cps = B.pb()
            B.mm(cps[:, 0:4], tri, sp)
            B.mm(cps[:, 4:8], ones, sp)
            bsp = sm(4)
            B.cp(bsp, cps[:, 0:4])
            btot = sm(4)
            B.cp(btot, cps[:, 4:8])
            B.pf(cps)
            a = sm(4)
            B.tt(a, z[:, 0:4], bsp, ALU.add)
            B.tt(dg, identc4, bcast_mid(a, 4, 128), ALU.mult)
            aps = B.pb()
            B.mm(aps, ones, dg.re("p h s -> p (h s)"))
            B.tt(Dm, aps.re("p (h s) -> p h s", h=4), maskTM4, ALU.add)
            B.pf(aps)
            cm = sm(4)
            B.red(cm, Dm, ALU.max)
            Mx = sm(4)
            B.tt(Mx, cm, mlm[l], ALU.max)
            B.tt(dg, identc4, bcast_mid(Mx, 4, 128), ALU.mult)
            mps = B.pb()
            B.mm(mps, ones, dg.re("p h s -> p (h s)"))
            mpv = mps.re("p (h t) -> p h t", h=4)
            mlast = sm(4)
            B.cp(mlast, mpv[:, :, 127])
            B.stt(Dm, mpv, -1.0, maskST4, ALU.mult, ALU.add)
            B.pf(mps)
            B.tt(Dm, Dm, bcast_mid(a, 4, 128), ALU.add)
            B.act(Dm, Dm, AF.Exp)
            wint = sm(4)
            B.tt(wint, mlm[l], Mx, ALU.subtract)
            B.act(wint, wint, AF.Exp)
            flo = sm(4)
            B.tt(flo, bsp, Mx, ALU.subtract)
            B.act(flo, flo, AF.Exp)
            ws16 = sm(4)
            B.tt(ws16, a, mlast, ALU.subtract)
            B.act(ws16, ws16, AF.Exp)
            B.ts(ws16, ws16, 1.0 / 16.0, ALU.mult)
            wprev = sm(4)
            B.tt(wprev, mlm[l], mlast, ALU.subtract)
            B.act(wprev, wprev, AF.Exp)
            sps = B.pb()
            spv = sps.re("p (h t) -> p h t", h=4)
            for h in range(4):
                for ec in range(2):
                    B.mm(spv[:, h, :], mkT[:, h * 2 + ec, :], mqT[:, h * 2 + ec, :], start=(ec == 0), stop=(ec == 1))
            B.tt(scm, spv, Dm, ALU.mult)
            B.pf(sps)
            nps = B.pb(2)
            for h in range(4):
                B.mm(nps[:, h * 256:(h + 1) * 256], scm[:, h, :], v_tm[:, s, h * 256:(h + 1) * 256])
            dps = B.pb()
            for h in range(4):
                B.mm(dps[:, h:h + 1], scm[:, h, :], onesb[:, 0:1])
            for h in range(4):
                for dc in range(2):
                    B.mm(dps[:, 4 + h:5 + h], mqT[:, h * 2 + dc, :], nb16[:, h * 2 + dc:h * 2 + dc + 1],
                         start=(dc == 0), stop=(dc == 1))
            ips = B.pb(2)
            for h in range(4):
                for dc in range(2):
                    B.mm(ips[:, h * 256:(h + 1) * 256], mqT[:, h * 2 + dc, :], stb[:, h * 2 + dc, :],
                         start=(dc == 0), stop=(dc == 1))
            B.cp(f32a, nps, eng="act")
            B.pf(nps)
            for h in range(4):
                q = fa(h * 256, (h + 1) * 256)
                B.stt(q, ips[:, h * 256:(h + 1) * 256], wint[:, h:h + 1], q, ALU.mult, ALU.add)
            B.pf(ips)
            dd = sm(8)
            B.cp(dd, dps[:, 0:8])
            B.pf(dps)
            den = sm(4)
            B.tt(den, dd[:, 4:8], wint, ALU.mult)
            B.tt(den, den, dd[:, 0:4], ALU.add)
            nden = sm(4)
            B.ts(nden, den, -1.0, ALU.mult)
            B.tt(den, den, nden, ALU.max)
            B.tt(den, den, flo, ALU.max)
            r = sm(4)
            B.recip(r, den)
            kps = B.pb(2)
            for h in range(4):
                for dc in range(2):
                    B.mm(kps[:, h * 256:(h + 1) * 256], ucT_b(h * 2 + dc)[:, sl], wqk[:, 1, h * 2 + dc, :],
                         start=(dc == 0), stop=(dc == 1))
            for h in range(4):
                B.act(kw[:, h * 256:(h + 1) * 256], kps[:, h * 256:(h + 1) * 256], AF.Copy, scale=ws16[:, h:h + 1])
            B.pf(kps)
            cpsl = []
            for hp in range(2):
                cps = B.pb(2)
                cpsl.append(cps)
                for hh in range(2):
                    h = hp * 2 + hh
                    for dc in range(2):
                        B.mm(cps[:, (hh * 2 + dc) * 256:(hh * 2 + dc + 1) * 256],
                             kw[:, h * 256 + dc * 128:h * 256 + (dc + 1) * 128], v_tm[:, s, h * 256:(h + 1) * 256])
            n2 = B.pb()
            for h in range(4):
                for dc in range(2):
                    B.mm(n2[:, h * 2 + dc:h * 2 + dc + 1], kw[:, h * 256 + dc * 128:h * 256 + (dc + 1) * 128], onesb[:, 0:1])
            for h in range(4):
                q = fa(h * 256, (h + 1) * 256)
                mv = headnorm_center(f32a, h)
                t = sm(1)
                B.tt(t, r[:, h:h + 1], r[:, h:h + 1], ALU.mult)
                B.tt(t, t, mv[:, 1:2], ALU.mult)
                sq = sm(1)
                B.act(sq, t, AF.Ln, bias=EPS)
                rs = sm(1)
                B.act(rs, sq, AF.Exp, scale=-0.5)
                B.tt(rs, rs, r[:, h:h + 1], ALU.mult)
                B.ts(q, q, mv[:, 0:1], ALU.subtract, rs, ALU.mult)
                B.tt(yb[:, h * 256:(h + 1) * 256], q, gsil[:, s, h * 256:(h + 1) * 256], ALU.mult)
            transpose_to(yT[:, :, sl], yb, 8, gain=p["g_ml"])
            for hp in range(2):
                cps = cpsl[hp]
                for hh in range(2):
                    h = hp * 2 + hh
                    dst = mlC[l][:, h * 2:h * 2 + 2, :]
                    B.stt(dst, dst, wprev[:, h:h + 1], cps[:, hh * 512:(hh + 1) * 512].re("p (a e) -> p a e", a=2),
                          ALU.mult, ALU.add)
                B.pf(cps)
            mv4 = mln[l].re("p (h c) -> p h c", c=2)
            B.tt(mv4, mv4, bcast_mid(wprev, 4, 2), ALU.mult)
            B.tt(mln[l], mln[l], n2[:, 0:8], ALU.add)
            B.pf(n2)
            B.tt(mlm[l], mlast, btot, ALU.subtract)
        branch(l, C_GM, "w_br_ml")

    def ssd_phase(l, ti):
        p = P[l]
        last = (ti == n_tiles - 1)
        for half in range(2):
            slab = win_slab(l, C_Z + half * 512)
            for s in range(NCH):
                proj_tm(hT, s, slab, lambda ps, s=s, half=half: B.act(gsil[:, s, half * 512:(half + 1) * 512], ps, AF.Silu))
        for q4 in range(4):
            slab = win_slab(l, C_XBC + q4 * 512)
            conv_blocks(l, slab, q4 * 4, p["cw_ssm"], p["cb_ssm"], hist_ssm[l])
            if last:
                conv_state_out(l, slab, "ssmconv_p", q4 * 512)
        slab = win_slab(l, C_DT, 16)
        for s in range(NCH):
            proj_tm(hT, s, slab, lambda ps, s=s: B.cp(igf[:, s, :], ps))
        for s in range(NCH):
            sl = slice(s * 128, (s + 1) * 128)
            B.cp(stb[:, 0:4, :], ssmS[l], eng="act")
            z = sm(16)
            B.tt(z, igf[:, s, :], p["dtb"], ALU.add)
            B.act(z, z, AF.Exp)
            dt = sm(16)
            B.act(dt, z, AF.Ln, bias=1.0)
            dA = sm(16)
            B.tt(dA, dt, p["nega"], ALU.mult)
            c1 = B.pb()
            B.mm(c1[0:16, 0:128], dA, tri)
            B.cp(cumT, c1[0:16, 0:128])
            B.pf(c1)
            c2 = B.pb()
            B.mm(c2[:, 0:16], tri, dA)
            B.mm(c2[:, 16:32], ones, dA)
            ecum = sm(16)
            B.act(ecum, c2[:, 0:16], AF.Exp)
            ncum = sm(16)
            B.ts(ncum, c2[:, 0:16], -1.0, ALU.mult)
            edec = sm(16)
            B.act(edec, c2[:, 16:32], AF.Exp)
            B.pf(c2)
            ps = B.pb().bitcast(BF16)
            for j in range(8):
                B.tr(ps[:, j * 128:(j + 1) * 128], ucT_b(j)[:, sl], identb)
            B.cp(xtm, ps, eng="act")
            B.tt(xdt.re("p (h c) -> p h c", h=16), ps.re("p (h c) -> p h c", h=16), bcast_mid(dt, 16, 64), ALU.mult)
            B.pf(ps)
            ps = B.pb().bitcast(BF16)
            for g in range(4):
                B.tr(ps[:, g * 128:(g + 1) * 128], ucT_b(8 + g)[:, sl], identb)
            B.cp(Btm, ps[:, 0:512], eng="act")
            B.pf(ps)
            ps = B.pb()
            for g in range(4):
                B.mm(ps[:, g * 128:(g + 1) * 128], ucT_b(8 + g)[:, sl], ucT_b(12 + g)[:, sl])
            B.cp(MTs, ps.re("p (g t) -> p g t", g=4), eng="act")
            B.pf(ps)
            yps = B.pb(2)
            wsl = sm(16)
            for g in range(4):
                B.tt(rblk, V(cumT.ap.rearrange("k (o t) -> k o t", o=1).to_broadcast([16, 4, 128]), cumT.keys),
                     V(ident.ap[0:16, 4 * g:4 * g + 4].rearrange("k (i o) -> k i o", o=1).to_broadcast([16, 4, 128]), ident.keys),
                     ALU.mult)
                cr = B.pb()
                B.mm(cr, ones[0:16, :], rblk.re("k i t -> k (i t)"))
                B.tt(dg, cr.re("p (i t) -> p i t", i=4), maskST4, ALU.add)
                B.pf(cr)
                Dg = Dm if g % 2 == 0 else Dm2
                Mg = MTh if g % 2 == 0 else MTh2
                for i in range(4):
                    B.act(Dg[:, i, :], dg[:, i, :], AF.Exp, bias=ncum[:, 4 * g + i:4 * g + i + 1])
                B.tt(Mg, Dg, V(MTs.ap[:, g:g + 1, :].to_broadcast([128, 4, 128]), MTs.keys), ALU.mult)
                B.cp(wsl[:, 4 * g:4 * g + 4], Dg[:, :, 127])
                for i in range(4):
                    h = 4 * g + i
                    B.mm(yps[:, h * 64:(h + 1) * 64], Mg[:, i, :], xdt[:, h * 64:(h + 1) * 64])
            ips = B.pb(2)
            for g in range(4):
                B.mm(ips[:, g * 256:(g + 1) * 256], ucT_b(12 + g)[:, sl], stb[:, g, :])
            y3 = f32a.re("p (h c) -> p h c", h=16)
            B.tt(y3, ips.re("p (h c) -> p h c", h=16), bcast_mid(ecum, 16, 64), ALU.mult)
            B.pf(ips)
            B.tt(f32a, f32a, yps, ALU.add)
            B.pf(yps)
            B.tt(xw.re("p (h c) -> p h c", h=16), xdt.re("p (h c) -> p h c", h=16), bcast_mid(wsl, 16, 64), ALU.mult)
            sps = B.pb(2)
            for g in range(4):
                B.mm(sps[:, g * 256:(g + 1) * 256], Btm[:, g * 128:(g + 1) * 128], xw[:, g * 256:(g + 1) * 256])
            m3 = merged[:, s, :].re("p (h c) -> p h c", h=16) if False else None
            xd = V(gtmp.ap.bitcast(BF16)[:, 0:1024], gtmp.keys)
            B.tt(xd.re("p (h c) -> p h c", h=16), xtm.re("p (h c) -> p h c", h=16), bcast_mid(p["ssmD"], 16, 64), ALU.mult)
            B.tt(f32a, f32a, xd, ALU.add)
            B.tt(f32a, f32a, gsil[:, s, :], ALU.mult)
            ss = sm(4)
            for g in range(4):
                B.act(junk[:, 0:256], fa(g * 256, (g + 1) * 256), AF.Square, accum=ss[:, g:g + 1])
            sq = sm(4)
            B.act(sq, ss, AF.Ln, scale=1.0 / 256.0, bias=EPS)
            rs = sm(4)
            B.act(rs, sq, AF.Exp, scale=-0.5)
            for g in range(4):
                B.act(yb[:, g * 256:(g + 1) * 256], fa(g * 256, (g + 1) * 256), AF.Copy, scale=rs[:, g:g + 1])
            transpose_to(yT[:, :, sl], yb, 8, gain=p["g_ssm"])
            S3 = ssmS[l].re("p g (r c) -> p (g r) c", r=4)
            B.tt(S3, S3, bcast_mid(edec, 16, 64), ALU.mult)
            B.tt(ssmS[l].re("p g c -> p (g c)"), ssmS[l].re("p g c -> p (g c)"), sps, ALU.add)
            B.pf(sps)
        branch(l, C_GS, "w_br_ssm")

    def add_proj_to_x(src_T, w2d):
        for half in range(2):
            slab = sq_slab(w2d, half)
            for s in range(NCH):
                dst = xt[:, s, half * 512:(half + 1) * 512]
                proj_tm(src_T, s, slab, lambda ps, dst=dst: B.tt(dst, ps, dst, ALU.add))

    def outmix_phase(l):
        for s in range(NCH):
            B.cp(hb, merged[:, s, :], eng="act")
            transpose_to(yT[:, :, s * 128:(s + 1) * 128], hb, 8, evac="act")
        add_proj_to_x(yT, W["w_out_mix"][l])

    def memattn_phase(l):
        norm_to_hT(P[l]["g_mem"])
        for half in range(2):
            slab = sq_slab(W["mem_wq"][l], half)
            for cb in range(4):
                proj_fm(hT, slab, cb, lambda ps, j=half * 4 + cb: B.cp(qaT[:, j, :], ps, eng="act"))
        for s in range(NCH):
            sl = slice(s * 128, (s + 1) * 128)
            for h in range(4):
                ps = B.pb()
                for dc in range(2):
                    B.mm(ps[:, 0:256], qaT[:, h * 2 + dc, sl], KT[l][:, h * 2 + dc, :], start=(dc == 0), stop=(dc == 1))
                mx = sm(1)
                B.red(mx, ps[:, 0:256], ALU.max)
                B.ts(mx, mx, -1.0 / 16.0, ALU.mult)
                ssum = sm(1)
                pe32 = fa(0, 256)
                B.act(pe32, ps[:, 0:256], AF.Exp, bias=mx, scale=1.0 / 16.0, accum=ssum)
                B.pf(ps)
                rs = sm(1)
                B.recip(rs, ssum)
                B.ts(pn, pe32, rs, ALU.mult)
                ps = B.pb().bitcast(BF16)
                for mc in range(2):
                    B.tr(ps[:, mc * 128:(mc + 1) * 128], pn[:, mc * 128:(mc + 1) * 128], identb)
                B.cp(pT[:, h, :, sl], ps[:, 0:256].re("p (m t) -> p m t", m=2), eng="act")
                B.pf(ps)
        for h in range(4):
            for eb in range(2):
                ps = B.pb()
                for mc in range(2):
                    B.mm(ps[:, 0:TT], Vtm[l][:, mc, h * 256 + eb * 128:h * 256 + (eb + 1) * 128], pT[:, h, mc, :],
                         start=(mc == 0), stop=(mc == 1))
                B.cp(yT[:, h * 2 + eb, :], ps[:, 0:TT])
                B.pf(ps)
        add_proj_to_x(yT, W["mem_wo"][l])

    def mlp_phase(l):
        norm_to_hT(P[l]["g_mlp"])
        for j8 in range(8):
            slab = load_slab(W["mlp_w1"][l][:, j8 * 512:(j8 + 1) * 512].rearrange("(kc p) n -> p kc n", p=128), 8, 512)
            for cb in range(4):
                j = j8 * 4 + cb

                def ev(ps, j=j):
                    B.act(rl, ps, AF.Relu)
                    B.tt(h1T[:, j, :], rl, rl, ALU.mult)
                proj_fm(hT, slab, cb, ev)
        for half in range(2):
            pss = [B.pb() for s in range(NCH)]
            for kg in range(4):
                slab = load_slab(W["mlp_w2"][l][kg * 1024:(kg + 1) * 1024, half * 512:(half + 1) * 512]
                                 .rearrange("(kc p) n -> p kc n", p=128), 8, 512)
                for s in range(NCH):
                    for kc in range(8):
                        B.mm(pss[s], h1T[:, kg * 8 + kc, s * 128:(s + 1) * 128], slab[:, kc, :],
                             start=(kg == 0 and kc == 0), stop=(kg == 3 and kc == 7))
            for s in range(NCH):
                dst = xt[:, s, half * 512:(half + 1) * 512]
                B.tt(dst, pss[s], dst, ALU.add)
                B.pf(pss[s])

    def final_norm_out(ti):
        B.dma_in(f32a, W["norm_f_w"].partition_broadcast(128))
        for s in range(NCH):
            rs = rms_rstd(xt[:, s, :], D, "f")
            B.stt(merged[:, s, :], xt[:, s, :], rs, f32a, ALU.mult, ALU.mult)
        B.dma_out(O["y_p"][ti * TT:(ti + 1) * TT, :].rearrange("(s p) d -> p s d", p=128), merged)

    def prompt_state_out(l):
        B.dma_out(O["ret_p"][l].rearrange("h d e -> d h e"), retS[l])
        B.dma_out(O["mlC_p"][l].rearrange("h (dc p) e -> p (h dc) e", p=128), mlC[l])
        ps = B.pb()
        B.tr(ps[0:8, 0:128], mln[l], ident)
        B.cp(stT[0:8, :], ps[0:8, 0:128])
        B.pf(ps)
        B.dma_out(O["mln_p"][l].rearrange("h (dc p) -> (h dc) p", p=128), stT[0:8, :])
        B.dma_out(O["mlm_p"][l:l + 1, :], mlm[l][0:1, :])
        for g in range(4):
            for hf in range(2):
                ps = B.pb()
                B.tr(ps[:, 0:128], ssmS[l][:, g, hf * 128:(hf + 1) * 128], ident)
                B.cp(stT, ps[:, 0:128])
                B.pf(ps)
                h0 = 4 * g + 2 * hf
                B.dma_out(O["ssm_p"][l][h0:h0 + 2].rearrange("h p n -> (h p) n"), stT)

    if do_prompt:
        for ti in range(n_tiles):
            B.dma_in(xt, I["x_p"][ti * TT:(ti + 1) * TT, :].rearrange("(s p) d -> p s d", p=128))
            B.dma_in(ropet, I["rope"][ti * TT:(ti + 1) * TT, :].rearrange("(s p) d -> p s d", p=128))
            for l in range(int(os.environ.get('NLAYERS', DEPTH))):
                phases = [lambda: norm_to_hT(P[l]["g_mix"]), lambda: retention_phase(l, ti), lambda: mlstm_phase(l, ti),
                          lambda: ssd_phase(l, ti), lambda: outmix_phase(l), lambda: memattn_phase(l), lambda: mlp_phase(l)]
                for pi, ph in enumerate(phases):
                    if pi < stage:
                        ph()
            if not os.environ.get('SKIP_FINAL'):
                final_norm_out(ti)
        if stage >= 99:
            for l in range(DEPTH):
                prompt_state_out(l)

    if do_sample:
        build_sample(B, nc, I, W, O, P, locals())

    cnt = B.T.emit_all(B.st)
    return nc, B, cnt


def build_sample(B, nc, I, W, O, P, g):
    ident, ones, identb, i16bc = g["ident"], g["ones"], g["identb"], g["i16bc"]
    load_slab, win_slab, sq_slab, sm = g["load_slab"], g["win_slab"], g["sq_slab"], g["sm"]
    f32a, hb, junk, gtmp = g["f32a"], g["hb"], g["junk"], g["gtmp"]
    big, big_keys, merged, xt = g["big"], g["big_keys"], g["merged"], g["xt"]
    R = NSMP

    def f32v(v, pat=None, **kw):
        ap = v.ap
        if pat is not None:
            ap = ap.rearrange(pat, **kw)
        if ap.dtype != F32:
            ap = ap.bitcast(F32)
        return V(ap, v.keys)

    def sm16(n):
        return sm(n)[0:R, :]

    pS = V(big.ap.bitcast(F32), big_keys)
    mergedS = merged[:, 0, :]
    cw = [merged[:, 1, :], xt[:, 0, :], xt[:, 1, :], f32v(g["v_tm"], "p a b -> p (a b)")]
    cbias = f32v(g["gsil"], "p a b -> p (a b)")
    xp = [f32v(g["yT"], "p a b -> p (a b)"), f32v(g["qkr"], "p a b -> p (a b)"), f32v(g["hT"], "p a b -> p (a b)")]
    yacc = xp[0]
    qmS = xp[1]
    kmS = xp[2]
    nS = cw[1]
    kwS = cw[2]
    tmp2 = f32v(g["ssmS"][0], "p a b -> p (a b)")
    xc0 = f32v(g["ssmS"][1], "p a b -> p (a b)")
    xs = f32v(g["stb"], "p a b -> p (a b)")
    sel16 = V(g["mlC"][0].ap.rearrange("p a b -> p (a b)")[0:R, :].rearrange("p (a b) -> p a b", a=16), g["mlC"][0].keys)
    SA = f32v(g["mlC"][1], "p a b -> p (a b)")
    SB = f32v(g["wqk"], "p a b c -> p (a b c)")
    QP = [f32v(g["retS"][0], "p a b -> p (a b)"), f32v(g["retS"][1], "p a b -> p (a b)")]
    hTs = B.sb("hTs", [128, 8, R], BF16)
    yTs = B.sb("yTs", [128, 8, R], BF16)
    ucTs = B.sb("ucTs", [128, 8, R], BF16)
    h1Ts = B.sb("h1Ts", [128, 32, R], BF16)
    qT8 = B.sb("qT8", [128, 8, R], F32)
    dAT = B.sb("dAT", [128, 8, R], F32)
    xdtT = B.sb("xdtT", [128, 8, R], F32)
    yTall = B.sb("yTall", [128, 8, R], F32)
    sAll = B.sb("sAll", [128, 2, R * 4], F32)
    Pall = B.sb("Pall", [128, 2, R * 4], F32)
    ropeS = B.sb("ropeS", [R, 256], F32)
    wpbc = B.sb("wpbc", [128, 64], F32)
    id16 = ident[0:R, 0:R]
    idb16 = identb[0:R, 0:R]

    B.cp(sel16, V(ident.ap[0:R, 0:R].rearrange("p (a o) -> p a o", o=1).to_broadcast([R, 16, 128]), ident.keys))
    B.dma_in(xs[0:R, :], I["x_s"])
    B.dma_in(ropeS, I["rope"][SEQ:SEQ + R, :])

    def tm_proj(srcT, slab, evac, n):
        ps = B.pb()
        kcn = slab.ap.shape[1]
        for kc in range(kcn):
            B.mm(ps[0:R, 0:n], srcT[:, kc, :], slab[:, kc, :], start=(kc == 0), stop=(kc == kcn - 1))
        evac(ps[0:R, 0:n])
        B.pf(ps)

    def toT(dstT, src16, nblk, gain=None):
        ps = B.pb().bitcast(BF16)
        psv = ps[:, 0:nblk * R].re("p (a b) -> p a b", a=nblk)
        for j in range(nblk):
            B.tr(psv[:, j, :], src16[:, j * 128:(j + 1) * 128], idb16)
        if gain is not None:
            B.tt(dstT, psv, gain.re("p (a o) -> p a o", o=1).bc([128, nblk, R]), ALU.mult)
        else:
            B.cp(dstT, psv)
        B.pf(ps)

    def toT32(dstT, src16, nblk):
        ps = B.pb()
        psv = ps[:, 0:nblk * R].re("p (a b) -> p a b", a=nblk)
        for j in range(nblk):
            B.tr(psv[:, j, :], src16[:, j * 128:(j + 1) * 128], id16)
        B.cp(dstT, psv)
        B.pf(ps)

    def rstd16(src, width):
        ss = sm16(1)
        B.act(junk[0:R, 0:width], src, AF.Square, accum=ss)
        sq = sm16(1)
        B.act(sq, ss, AF.Ln, scale=1.0 / width, bias=EPS)
        rs = sm16(1)
        B.act(rs, sq, AF.Exp, scale=-0.5)
        return rs

    def norm_hTs(gain):
        rs = rstd16(xs[0:R, :], D)
        B.ts(hb[0:R, :], xs[0:R, :], rs, ALU.mult)
        toT(hTs, hb[0:R, :], 8, gain=gain)

    def bc3(v, n_mid, n_in):
        return v.re("p (a o) -> p a o", o=1).bc([R, n_mid, n_in])

    def branch_s(l, c_gate, wname):
        first = (wname == "w_br_ret")
        for half in range(2):
            gs = win_slab(l, c_gate + half * 512)
            ws = sq_slab(W[wname][l], half)
            tm_proj(hTs, gs, lambda ps: B.act(gtmp[0:R, :], ps, AF.Sigmoid), 512)
            dst = mergedS[0:R, half * 512:(half + 1) * 512]

            def ev(ps, dst=dst):
                if first:
                    B.tt(dst, ps, gtmp[0:R, :], ALU.mult)
                else:
                    B.tt(gtmp[0:R, :], ps, gtmp[0:R, :], ALU.mult)
                    B.tt(dst, dst, gtmp[0:R, :], ALU.add)
            tm_proj(yTs, ws, ev, 512)

    def add_proj_s(srcT, w2d):
        for half in range(2):
            slab = sq_slab(w2d, half)
            dst = xs[0:R, half * 512:(half + 1) * 512]
            tm_proj(srcT, slab, lambda ps, dst=dst: B.tt(dst, ps, dst, ALU.add), 512)

    def load_cols(l, c0, ncols, dst0, func=None):
        for o in range(0, ncols, 512):
            n = min(512, ncols - o)
            slab = win_slab(l, c0 + o, n)
            d = pS[0:R, dst0 + o:dst0 + o + n]
            if func is None:
                tm_proj(hTs, slab, lambda ps, d=d: B.cp(d, ps, eng="act"), n)
            else:
                tm_proj(hTs, slab, lambda ps, d=d: B.act(d, ps, func), n)

    def conv_s(l, wkey, bkey, skey, okey, c0, xr, dst, dst_is_bf16):
        for j in range(3):
            B.dma_in(xp[j][0:R, :], I[skey][l][:, j, c0:c0 + 1024])
        for j in range(4):
            B.dma_in(cw[j][0:R, :], W[wkey][l][j][c0:c0 + 1024].partition_broadcast(R))
        B.dma_in(cbias[0:R, :], W[bkey][l][c0:c0 + 1024].partition_broadcast(R))
        acc = f32a[0:R, :]
        t2 = tmp2[0:R, :]
        B.tt(acc, xp[0][0:R, :], cw[0][0:R, :], ALU.mult)
        B.tt(acc, acc, cbias[0:R, :], ALU.add)
        for j in range(1, 4):
            src = xp[j][0:R, :] if j < 3 else xr
            B.tt(t2, src, cw[j][0:R, :], ALU.mult)
            B.tt(acc, acc, t2, ALU.add)
        B.act(dst, acc, AF.Silu)
        B.dma_out(O[okey][l][:, 0, c0:c0 + 1024], xp[1][0:R, :])
        B.dma_out(O[okey][l][:, 1, c0:c0 + 1024], xp[2][0:R, :])
        B.dma_out(O[okey][l][:, 2, c0:c0 + 1024], xr)

    def headnorm_gate(src, gate, center, ngrp, scale=None):
        wd = 1024 // ngrp
        for h in range(ngrp):
            sl = slice(h * wd, (h + 1) * wd)
            st6 = sm16(6)
            B.bnstats(st6, src[:, sl])
            mv = sm16(2)
            B.bnaggr(mv, st6)
            t = sm16(1)
            if center:
                if scale is not None:
                    B.tt(t, scale[:, h:h + 1], scale[:, h:h + 1], ALU.mult)
                    B.tt(t, t, mv[:, 1:2], ALU.mult)
                else:
                    B.cp(t, mv[:, 1:2])
            else:
                B.tt(t, mv[:, 0:1], mv[:, 0:1], ALU.mult)
                B.tt(t, t, mv[:, 1:2], ALU.add)
            sq = sm16(1)
            B.act(sq, t, AF.Ln, bias=EPS)
            rs = sm16(1)
            B.act(rs, sq, AF.Exp, scale=-0.5)
            if scale is not None:
                B.tt(rs, rs, scale[:, h:h + 1], ALU.mult)
            q = f32a[0:R, sl]
            if center:
                B.ts(q, src[:, sl], mv[:, 0:1], ALU.subtract, rs, ALU.mult)
            else:
                B.ts(q, src[:, sl], rs, ALU.mult)
            if gate is None:
                B.cp(hb[0:R, sl], q)
            else:
                B.tt(hb[0:R, sl], q, gate[:, sl], ALU.mult)

    def acc_rows(dst, ps, b):
        if b == 0:
            B.cp(dst, ps)
        else:
            B.tt(dst, dst, ps, ALU.add)

    def ret_dec(l):
        for which in range(2):
            slab = win_slab(l, C_Q + which * 512)

            def ev(ps, which=which):
                cs = ropeS[:, which * 128: which * 128 + 64].re("p (o d) -> p o d", o=1).bc([R, 4, 64])
                sn = ropeS[:, which * 128 + 64: which * 128 + 128].re("p (o d) -> p o d", o=1).bc([R, 4, 64])
                pv = ps.re("p (h d) -> p h d", h=4)
                t1 = f32a[0:R, 0:256].re("p (h d) -> p h d", h=4)
                t2 = f32a[0:R, 256:512].re("p (h d) -> p h d", h=4)
                ov = pS[0:R, which * 512:(which + 1) * 512].re("p (h d) -> p h d", h=4)
                B.tt(t1, pv[:, :, 0:64], cs, ALU.mult)
                B.tt(t2, pv[:, :, 64:128], sn, ALU.mult)
                B.tt(ov[:, :, 0:64], t1, t2, ALU.subtract)
                B.tt(t1, pv[:, :, 0:64], sn, ALU.mult)
                B.tt(t2, pv[:, :, 64:128], cs, ALU.mult)
                B.tt(ov[:, :, 64:128], t1, t2, ALU.add)
            tm_proj(hTs, slab, ev, 512)
        load_cols(l, C_V, 1024, 1024)
        load_cols(l, C_G, 1024, 2048, AF.Silu)
        qr = pS[0:R, 0:512]
        kr = pS[0:R, 512:1024]
        v = pS[0:R, 1024:2048]
        gs = pS[0:R, 2048:3072]
        toT32(qT8[:, 0:4, :], qr, 4)
        QT4 = QP[0].re("p (h b c) -> p h b c", h=4, b=R)
        B.tt(QT4, V(qT8.ap[:, 0:4, :].rearrange("p h (b o) -> p h b o", o=1).to_broadcast([128, 4, R, R]), qT8.keys),
             V(i16bc.ap.rearrange("p (o b) c -> p o b c", o=1).to_broadcast([128, 4, R, R]), i16bc.keys), ALU.mult)
        for b in range(R):
            S = SA[:, (b % 2) * 1024:(b % 2 + 1) * 1024].re("p (h e) -> p h e", h=4)
            B.dma_in(S, I["st_ret"][l, b].rearrange("h d e -> d h e"))
            kmb = gtmp[0:R, :]
            B.ts(kmb, kr, ident[0:R, b:b + 1], ALU.mult)
            kv = B.pb(2)
            for h in range(4):
                B.mm(kv[:, h * 256:(h + 1) * 256], kmb[:, h * 128:(h + 1) * 128], v[:, h * 256:(h + 1) * 256])
            for h in range(4):
                B.stt(S[:, h, :], S[:, h, :], RET_G[h], kv[:, h * 256:(h + 1) * 256], ALU.mult, ALU.add)
            B.pf(kv)
            B.dma_out(O["ret_s"][l, b].rearrange("h d e -> d h e"), S)
            ops = B.pb(2)
            for h in range(4):
                B.mm(ops[0:R, h * 256:(h + 1) * 256], QT4[:, h, b, :], S[:, h, :])
            acc_rows(yacc[0:R, :], ops[0:R, :], b)
            B.pf(ops)
        headnorm_gate(yacc[0:R, :], gs, True, 4)
        toT(yTs, hb[0:R, :], 8, gain=P[l]["g_ret"])
        branch_s(l, C_GR, "w_br_ret")

    def ml_dec(l):
        p = P[l]
        load_cols(l, C_U, 1024, 0)
        load_cols(l, C_VM, 1024, 1024)
        load_cols(l, C_OM, 1024, 2048, AF.Sigmoid)
        load_cols(l, C_IG, 8, 3072)
        u = pS[0:R, 0:1024]
        vm = pS[0:R, 1024:2048]
        osig = pS[0:R, 2048:3072]
        conv_s(l, "ml_conv_w", "ml_conv_b", "st_mlconv", "mlconv_s", 0, u, hb[0:R, :], True)
        toT(ucTs, hb[0:R, :], 8)
        for wi, wname, dst, scl in ((0, "ml_wq", qmS, 1.0), (1, "ml_wk", kmS, 1.0 / 16.0)):
            slab = load_slab(W[wname][l].rearrange("h (dc p) e -> p (h dc) e", p=128), 8, 256)
            ps = B.pb(2)
            for h in range(4):
                for dc in range(2):
                    B.mm(ps[0:R, h * 256:(h + 1) * 256], ucTs[:, h * 2 + dc, :], slab[:, h * 2 + dc, :],
                         start=(dc == 0), stop=(dc == 1))
            B.act(dst[0:R, :], ps[0:R, :], AF.Copy, scale=scl)
            B.pf(ps)
        z = sm16(8)
        B.tt(z, pS[0:R, 3072:3080], p["gateb"][0:R, :], ALU.add)
        e = sm16(4)
        B.act(e, z[:, 4:8], AF.Exp, scale=-1.0)
        sp = sm16(4)
        B.act(sp, e, AF.Ln, bias=1.0)
        mold = sm16(4)
        B.dma_in(mold, I["st_mlm"][l])
        inter = sm16(4)
        B.tt(inter, mold, sp, ALU.subtract)
        mnew = sm16(4)
        B.tt(mnew, inter, z[:, 0:4], ALU.max)
        B.dma_out(O["mlm_s"][l], mnew)
        ws = sm16(4)
        B.tt(ws, z[:, 0:4], mnew, ALU.subtract)
        B.act(ws, ws, AF.Exp)
        wp = sm16(4)
        B.tt(wp, inter, mnew, ALU.subtract)
        B.act(wp, wp, AF.Exp)
        flo = sm16(4)
        B.act(flo, mnew, AF.Exp, scale=-1.0)
        B.dma_in(nS[0:R, :], I["st_mln"][l].rearrange("b h d -> b (h d)"))
        k3 = kwS[0:R, :].re("p (h d) -> p h d", h=4)
        B.tt(k3, kmS[0:R, :].re("p (h d) -> p h d", h=4), bc3(ws, 4, 256), ALU.mult)
        n3 = nS[0:R, :].re("p (h d) -> p h d", h=4)
        B.tt(n3, n3, bc3(wp, 4, 256), ALU.mult)
        B.tt(nS[0:R, :], nS[0:R, :], kwS[0:R, :], ALU.add)
        B.dma_out(O["mln_s"][l].rearrange("b h d -> b (h d)"), nS[0:R, :])
        wd = sm16(64)
        B.tt(wd.re("p (b h) -> p b h", b=R), V(wp.ap.rearrange("p (o h) -> p o h", o=1).to_broadcast([R, R, 4]), wp.keys),
             V(ident.ap[0:R, 0:R].rearrange("p (b o) -> p b o", o=1).to_broadcast([R, R, 4]), ident.keys), ALU.mult)
        ps = B.pb()
        B.mm(ps[:, 0:64], ones[0:R, :], wd)
        B.cp(wpbc, ps[:, 0:64])
        B.pf(ps)
        toT32(qT8, qmS[0:R, :], 8)
        for hf in range(2):
            Q4 = QP[hf].re("p (h b c) -> p h b c", h=4, b=R)
            B.tt(Q4, V(qT8.ap[:, hf * 4:(hf + 1) * 4, :].rearrange("p h (b o) -> p h b o", o=1).to_broadcast([128, 4, R, R]), qT8.keys),
                 V(i16bc.ap.rearrange("p (o b) c -> p o b c", o=1).to_broadcast([128, 4, R, R]), i16bc.keys), ALU.mult)

        def qpad(idx, b):
            return QP[idx // 4].re("p (h b c) -> p h b c", h=4, b=R)[:, idx % 4, b, :]
        for b in range(R):
            S = (SA if b % 2 == 0 else SB).re("p (a e) -> p a e", a=8)
            B.dma_in(S, I["st_mlC"][l, b].rearrange("h (dc p) e -> p (h dc) e", p=128))
            kmb = f32a[0:R, :]
            B.ts(kmb, kwS[0:R, :], ident[0:R, b:b + 1], ALU.mult)
            for hp in range(2):
                cps = B.pb(2)
                for hh in range(2):
                    h = hp * 2 + hh
                    for dc in range(2):
                        B.mm(cps[:, (hh * 2 + dc) * 256:(hh * 2 + dc + 1) * 256],
                             kmb[:, h * 256 + dc * 128:h * 256 + (dc + 1) * 128], vm[:, h * 256:(h + 1) * 256])
                for hh in range(2):
                    h = hp * 2 + hh
                    dst = S[:, h * 2:h * 2 + 2, :]
                    B.stt(dst, dst, wpbc[:, b * 4 + h:b * 4 + h + 1],
                          cps[:, hh * 512:(hh + 1) * 512].re("p (a e) -> p a e", a=2), ALU.mult, ALU.add)
                B.pf(cps)
            B.dma_out(O["mlC_s"][l, b].rearrange("h (dc p) e -> p (h dc) e", p=128), S)
            ops = B.pb(2)
            for h in range(4):
                for dc in range(2):
                    B.mm(ops[0:R, h * 256:(h + 1) * 256], qpad(h * 2 + dc, b), S[:, h * 2 + dc, :],
                         start=(dc == 0), stop=(dc == 1))
            acc_rows(yacc[0:R, :], ops[0:R, :], b)
            B.pf(ops)
        B.tt(tmp2[0:R, :], qmS[0:R, :], nS[0:R, :], ALU.mult)
        qn = sm16(4)
        B.red(qn, tmp2[0:R, :].re("p (h d) -> p h d", h=4), ALU.add)
        nq = sm16(4)
        B.ts(nq, qn, -1.0, ALU.mult)
        B.tt(qn, qn, nq, ALU.max)
        B.tt(qn, qn, flo, ALU.max)
        r = sm16(4)
        B.recip(r, qn)
        headnorm_gate(yacc[0:R, :], osig, True, 4, scale=r)
        toT(yTs, hb[0:R, :], 8, gain=p["g_ml"])
        branch_s(l, C_GM, "w_br_ml")

    def ssd_dec(l):
        p = P[l]
        load_cols(l, C_Z, 1024, 0, AF.Silu)
        load_cols(l, C_XBC, 2048, 1024)
        load_cols(l, C_DT, 16, 3072)
        zs = pS[0:R, 0:1024]
        xc1 = pS[0:R, 3104:4128]
        conv_s(l, "ssm_conv_w", "ssm_conv_b", "st_ssmconv", "ssmconv_s", 0, pS[0:R, 1024:2048], xc0[0:R, :], False)
        conv_s(l, "ssm_conv_w", "ssm_conv_b", "st_ssmconv", "ssmconv_s", 1024, pS[0:R, 2048:3072], xc1, False)
        x = xc0[0:R, :]
        Bm = xc1[:, 0:512]
        Cm = xc1[:, 512:1024]
        zt = sm16(16)
        B.tt(zt, pS[0:R, 3072:3088], p["dtb"][0:R, :], ALU.add)
        B.act(zt, zt, AF.Exp)
        dt = sm16(16)
        B.act(dt, zt, AF.Ln, bias=1.0)
        dA = sm16(16)
        B.tt(dA, dt, p["nega"][0:R, :], ALU.mult)
        B.act(dA, dA, AF.Exp)
        B.cp(f32a[0:R, :].re("p (h c) -> p h c", h=16), bc3(dA, 16, 64))
        toT32(dAT, f32a[0:R, :], 8)
        B.tt(f32a[0:R, :].re("p (h c) -> p h c", h=16), x.re("p (h c) -> p h c", h=16), bc3(dt, 16, 64), ALU.mult)
        toT32(xdtT, f32a[0:R, :], 8)
        t2 = SB[:, 0:1024]
        for b in range(R):
            S = SA[:, (b % 2) * 1024:(b % 2 + 1) * 1024]
            S3 = S.re("p (a n) -> p a n", a=8)
            B.dma_in(S3, I["st_ssm"][l, b].rearrange("(hp h2) p n -> (h2 p) hp n", h2=2))
            bb = B.pb()
            B.mm(bb, sel16[:, b, :], Bm)
            cc = B.pb()
            B.mm(cc, sel16[:, b, :], Cm)
            B.tt(S3, S3, V(dAT.ap[:, :, b:b + 1].to_broadcast([128, 8, 128]), dAT.keys), ALU.mult)
            t4 = t2.re("p (g r n) -> p g r n", g=4, r=2)
            B.tt(t4, V(bb.ap.rearrange("p (g o n) -> p g o n", g=4, o=1).to_broadcast([128, 4, 2, 128]), bb.keys),
                 V(xdtT.ap[:, :, b:b + 1].rearrange("p (g r) o -> p g r o", g=4).to_broadcast([128, 4, 2, 128]), xdtT.keys),
                 ALU.mult)
            B.pf(bb)
            B.tt(S, S, t2, ALU.add)
            B.dma_out(O["ssm_s"][l, b].rearrange("(hp h2) p n -> (h2 p) hp n", h2=2), S3)
            B.tt(t4, S.re("p (g r n) -> p g r n", g=4, r=2),
                 V(cc.ap.rearrange("p (g o n) -> p g o n", g=4, o=1).to_broadcast([128, 4, 2, 128]), cc.keys), ALU.mult)
            B.pf(cc)
            B.red(yTall[:, :, b], t2.re("p (a n) -> p a n", a=8), ALU.add)
        ps = B.pb(2)
        for j in range(8):
            B.tr(ps[0:R, j * 128:(j + 1) * 128], yTall[:, j, :], ident)
        ysd = tmp2[0:R, :]
        B.tt(ysd.re("p (h c) -> p h c", h=16), x.re("p (h c) -> p h c", h=16), bc3(p["ssmD"][0:R, :], 16, 64), ALU.mult)
        B.tt(ysd, ysd, ps[0:R, :], ALU.add)
        B.pf(ps)
        B.tt(ysd, ysd, zs, ALU.mult)
        headnorm_gate(ysd, None, False, 4)
        toT(yTs, hb[0:R, :], 8, gain=p["g_ssm"])
        branch_s(l, C_GS, "w_br_ssm")

    def mem_dec(l):
        norm_hTs(P[l]["g_mem"])
        for half in range(2):
            slab = sq_slab(W["mem_wq"][l], half)
            d = pS[0:R, half * 512:(half + 1) * 512]
            tm_proj(hTs, slab, lambda ps, d=d: B.cp(d, ps, eng="act"), 512)
        qa = pS[0:R, 0:1024]
        slots = [SA[:, 0:1024], SA[:, 1024:2048], SB[:, 0:1024], SB[:, 1024:2048]]
        sctr = 0
        for b in range(R):
            q0 = B.pb()
            B.mm(q0, sel16[:, b, :], qa[:, 0:512])
            q1 = B.pb()
            B.mm(q1, sel16[:, b, :], qa[:, 512:1024])
            for mc in range(2):
                kt = slots[sctr % 4]
                sctr += 1
                B.dma_in(kt, I["c_memk"][l, b, mc * 128:(mc + 1) * 128, :])
                B.tt(f32a[:, 0:512], kt[:, 0:512], q0, ALU.mult)
                B.tt(f32a[:, 512:1024], kt[:, 512:1024], q1, ALU.mult)
                B.red(sAll[:, mc, b * 4:(b + 1) * 4], f32a.re("p (h d) -> p h d", h=4), ALU.add)
            B.pf(q0)
            B.pf(q1)
        ps = B.pb()
        for mc in range(2):
            B.tr(ps[0:64, mc * 128:(mc + 1) * 128], sAll[:, mc, :], ident)
        mx = sm(1)[0:64, :]
        B.red(mx, ps[0:64, 0:256], ALU.max)
        B.ts(mx, mx, -1.0 / 16.0, ALU.mult)
        ssum = sm(1)[0:64, :]
        pe = gtmp[0:64, 0:256]
        B.act(pe, ps[0:64, 0:256], AF.Exp, bias=mx, scale=1.0 / 16.0, accum=ssum)
        B.pf(ps)
        rs = sm(1)[0:64, :]
        B.recip(rs, ssum)
        B.ts(pe, pe, rs, ALU.mult)
        ps = B.pb()
        for mc in range(2):
            B.tr(ps[:, mc * 64:(mc + 1) * 64], pe[:, mc * 128:(mc + 1) * 128], ident[0:64, 0:64])
        B.cp(Pall, ps[:, 0:128].re("p (m c) -> p m c", m=2))
        B.pf(ps)
        for mc in range(2):
            Pp = QP[mc].re("p (b h c) -> p b h c", b=R, h=4)
            B.tt(Pp, V(Pall.ap[:, mc, :].rearrange("p (b h o) -> p b h o", b=R, o=1).to_broadcast([128, R, 4, R]), Pall.keys),
                 V(i16bc.ap.rearrange("p b (o c) -> p b o c", o=1).to_broadcast([128, R, 4, R]), i16bc.keys), ALU.mult)
        oacc = yacc
        for b in range(R):
            vts = []
            for mc in range(2):
                vt = slots[sctr % 4]
                sctr += 1
                B.dma_in(vt, I["c_memv"][l, b, mc * 128:(mc + 1) * 128, :])
                vts.append(vt)
            ops = B.pb(2)
            for h in range(4):
                for mc in range(2):
                    B.mm(ops[0:R, h * 256:(h + 1) * 256], QP[mc].re("p (b h c) -> p b h c", b=R, h=4)[:, b, h, :],
                         vts[mc][:, h * 256:(h + 1) * 256], start=(mc == 0), stop=(mc == 1))
            acc_rows(oacc[0:R, :], ops[0:R, :], b)
            B.pf(ops)
        B.cp(hb[0:R, :], oacc[0:R, :], eng="act")
        toT(yTs, hb[0:R, :], 8)
        add_proj_s(yTs, W["mem_wo"][l])

    def mlp_dec(l):
        norm_hTs(P[l]["g_mlp"])
        h1S = V(big.ap[0:R, 0:4096], big_keys)
        for j8 in range(8):
            slab = load_slab(W["mlp_w1"][l][:, j8 * 512:(j8 + 1) * 512].rearrange("(kc p) n -> p kc n", p=128), 8, 512)

            def ev(ps, j8=j8):
                B.act(gtmp[0:R, :], ps, AF.Relu)
                B.tt(h1S[:, j8 * 512:(j8 + 1) * 512], gtmp[0:R, :], gtmp[0:R, :], ALU.mult)
            tm_proj(hTs, slab, ev, 512)
        for q4 in range(4):
            toT(h1Ts[:, q4 * 8:(q4 + 1) * 8, :], h1S[:, q4 * 1024:(q4 + 1) * 1024], 8)
        for half in range(2):
            ps = B.pb()
            for kg in range(4):
                slab = load_slab(W["mlp_w2"][l][kg * 1024:(kg + 1) * 1024, half * 512:(half + 1) * 512]
                                 .rearrange("(kc p) n -> p kc n", p=128), 8, 512)
                for kc in range(8):
                    B.mm(ps[0:R, :], h1Ts[:, kg * 8 + kc, :], slab[:, kc, :],
                         start=(kg == 0 and kc == 0), stop=(kg == 3 and kc == 7))
            dst = xs[0:R, half * 512:(half + 1) * 512]
            B.tt(dst, ps[0:R, :], dst, ALU.add)
            B.pf(ps)

    for l in range(int(os.environ.get('NLAYERS', DEPTH))):
        phases = [lambda: norm_hTs(P[l]["g_mix"]), lambda: ret_dec(l), lambda: ml_dec(l), lambda: ssd_dec(l)]

        def outmix():
            B.cp(hb[0:R, :], mergedS[0:R, :], eng="act")
            toT(yTs, hb[0:R, :], 8)
            add_proj_s(yTs, W["w_out_mix"][l])
        phases += [outmix, lambda: mem_dec(l), lambda: mlp_dec(l)]
        sst = int(os.environ.get("SSTAGE", "99"))
        for pi, ph in enumerate(phases):
            if pi < sst:
                ph()
    rs = rstd16(xs[0:R, :], D)
    B.dma_in(f32a[0:R, :], W["norm_f_w"].partition_broadcast(R))
    B.stt(mergedS[0:R, :], xs[0:R, :], rs, f32a[0:R, :], ALU.mult, ALU.mult)
    B.dma_out(O["y_s"], mergedS[0:R, :])


_WNAMES = ["norm_mix_w", "w_in", "ret_norm_w", "ml_conv_w", "ml_conv_b", "ml_wq", "ml_wk", "ml_gate_b", "ml_norm_w",
           "ssm_conv_w", "ssm_conv_b", "ssm_dt_bias", "ssm_A_log", "ssm_D", "ssm_norm_w", "w_br_ret", "w_br_ml",
           "w_br_ssm", "w_out_mix", "norm_mem_w", "mem_wq", "mem_wk", "mem_wv", "mem_wo", "norm_mlp_w", "mlp_w1",
           "mlp_w2", "norm_f_w"]

_PROG = {}


def _get_prog(**kw):
    key = tuple(sorted(kw.items()))
    if key not in _PROG:
        _PROG[key] = build_program(**kw)
    return _PROG[key]


def make_in_maps(inputs):
    cst_np, rope_np, _, _ = make_consts()
    f = lambda a: np.ascontiguousarray(a, dtype=np.float32)
    wts = {k: f(inputs[k]) for k in _WNAMES}
    maps = []
    for c in range(NCORES):
        r = slice(NSMP * c, NSMP * (c + 1))
        m = dict(wts)
        m["x_p"] = f(inputs["x_prompt"][c])
        m["x_s"] = f(inputs["x_sample"][r, 0])
        m["mem"] = f(inputs["mem_prompt"][c])
        m["cst"] = cst_np
        m["rope"] = rope_np
        m["st_ret"] = f(inputs["state_ret"][:, r])
        m["st_mlC"] = f(inputs["state_mlstm_C"][:, r])
        m["st_mln"] = f(inputs["state_mlstm_n"][:, r])
        m["st_mlm"] = f(inputs["state_mlstm_m"][:, r])
        m["st_mlconv"] = f(inputs["state_mlstm_conv"][:, r])
        m["st_ssm"] = f(inputs["state_ssm"][:, r])
        m["st_ssmconv"] = f(inputs["state_ssm_conv"][:, r])
        m["c_memk"] = f(inputs["cache_mem_k"][:, r]).reshape(DEPTH, NSMP, MEMLEN, 1024)
        m["c_memv"] = f(inputs["cache_mem_v"][:, r]).reshape(DEPTH, NSMP, MEMLEN, 1024)
        maps.append(m)
    return maps


def gather_outputs(res):
    R = res.results
    cat = lambda name, axis: np.concatenate([np.asarray(R[c][name], dtype=np.float32) for c in range(NCORES)], axis=axis)
    stack1 = lambda name: np.stack([np.asarray(R[c][name], dtype=np.float32) for c in range(NCORES)], axis=1)
    y_p = np.stack([np.asarray(R[c]["y_p"], dtype=np.float32) for c in range(NCORES)], axis=0)
    y_s = cat("y_s", 0).reshape(NCORES * NSMP, 1, D)
    outs = [y_p, y_s,
            stack1("ret_p"), stack1("mlC_p"), stack1("mln_p"), stack1("mlm_p"), stack1("mlconv_p"),
            stack1("ssm_p"), stack1("ssmconv_p"),
            stack1("memk_p").reshape(DEPTH, NCORES, MEMLEN, 4, 256), stack1("memv_p").reshape(DEPTH, NCORES, MEMLEN, 4, 256),
            cat("ret_s", 1), cat("mlC_s", 1), cat("mln_s", 1), cat("mlm_s", 1), cat("mlconv_s", 1),
            cat("ssm_s", 1), cat("ssmconv_s", 1)]
    return tuple(outs)


def kernel(**inputs):
    nc, B, cnt = _get_prog()
    maps = make_in_maps(inputs)
    res = run_bass_kernel_spmd(nc, maps, core_ids=list(range(NCORES)))
    return gather_outputs(res)
```

```python
import contextlib
import os
import numpy as np
import concourse.bass as bass
import concourse.mybir as mybir
from concourse.bass_utils import run_bass_kernel_spmd

F32 = mybir.dt.float32
BF16 = mybir.dt.bfloat16
AF = mybir.ActivationFunctionType
ALU = mybir.AluOpType
AX = mybir.AxisListType

NCORES = 8
D = 1024
SEQ = 2048
DEPTH = 2
NSMP = 16
MEMLEN = 256
EPS = 1e-6
NCH = 2
TT = 128 * NCH
NTILES = SEQ // TT
NIN = 12312
C_Q, C_K, C_V, C_G, C_U, C_VM, C_OM, C_IG, C_Z, C_XBC, C_DT, C_GR, C_GM, C_GS = (
    0, 512, 1024, 2048, 3072, 4096, 5120, 6144, 6152, 7176, 9224, 9240, 10264, 11288)
PAST = 16384
NEG = -1.0e30


class Op:
    __slots__ = ("eng", "emit", "deps", "pos", "need_inc", "is_dma", "sem", "val", "waits", "swkey")


class Tracker:
    ENGS = ("pe", "act", "dve", "pool", "sp")

    def __init__(self, nc, n_dma_sems=48):
        self.nc = nc
        self.ops = {e: [] for e in self.ENGS}
        self.bufs = {}
        self.n_dma_sems = n_dma_sems
        self.dma_count = 0
        self.dma_hist = []
        self.out_dmas = []
        self.sw_count = 0
        self.sw_hist = []
        self.n_sw_sems = 8

    def add(self, eng, emit, reads=(), writes=(), dma=False, is_out=False, swkey=None):
        op = Op()
        op.eng = eng
        op.emit = emit
        op.is_dma = dma
        op.need_inc = dma
        op.sem = None
        op.val = None
        deps = []
        bufs = self.bufs
        for k in reads:
            st = bufs.get(k)
            if st is not None and st[0] is not None:
                deps.append(st[0])
            if st is not None and k[0] == "ps":
                for r in st[1]:
                    if r.eng != eng:
                        deps.append(r)
        for k in writes:
            st = bufs.get(k)
            if st is not None:
                if st[0] is not None:
                    deps.append(st[0])
                deps.extend(st[1])
        for k in reads:
            st = bufs.get(k)
            if st is None:
                st = bufs[k] = [None, []]
            st[1].append(op)
        for k in writes:
            bufs[k] = [op, []]
        op.swkey = None
        if dma and eng == "pool":
            j = self.sw_count
            self.sw_count += 1
            op.sem = ("sw", j % self.n_sw_sems)
            op.val = 16 * (j // self.n_sw_sems + 1)
            if j >= self.n_sw_sems:
                deps.append(self.sw_hist[j - self.n_sw_sems])
            self.sw_hist.append(op)
            if is_out:
                self.out_dmas.append(op)
        elif dma:
            j = self.dma_count
            self.dma_count += 1
            op.sem = ("dma", j % self.n_dma_sems)
            op.val = 16 * (j // self.n_dma_sems + 1)
            if j >= self.n_dma_sems:
                deps.append(self.dma_hist[j - self.n_dma_sems])
            self.dma_hist.append(op)
            if is_out:
                self.out_dmas.append(op)
        fd = []
        seen = set()
        for d in deps:
            if id(d) in seen or d is op:
                continue
            seen.add(id(d))
            if eng == "pe" and d.eng == "pe" and not d.is_dma:
                continue
            d.need_inc = True
            fd.append(d)
        op.deps = fd
        op.pos = len(self.ops[eng])
        self.ops[eng].append(op)
        return op

    def emit_all(self, stack):
        nc = self.nc
        esem = {}
        for e in ("pe", "act", "dve", "pool"):
            esem[e] = stack.enter_context(nc.semaphore("sem_" + e))
        nds = min(self.n_dma_sems, max(1, self.dma_count))
        dsem = [stack.enter_context(nc.semaphore("dsem%d" % i)) for i in range(nds)]
        for e in self.ENGS:
            c = 0
            for op in self.ops[e]:
                if op.is_dma:
                    continue
                if op.need_inc:
                    c += 1
                    op.sem = ("eng", e)
                    op.val = c

        swsem = [stack.enter_context(nc.semaphore("swsem%d" % i)) for i in range(self.n_sw_sems)]

        def semobj(s):
            if s[0] == "eng":
                return esem[s[1]]
            if s[0] == "sw":
                return swsem[s[1]]
            return dsem[s[1]]

        for e in self.ENGS:
            wm = {}
            for op in self.ops[e]:
                need = {}
                for d in op.deps:
                    if need.get(d.sem, 0) < d.val:
                        need[d.sem] = d.val
                ws = []
                for s, v in need.items():
                    if wm.get(s, 0) >= v:
                        continue
                    wm[s] = v
                    ws.append((s, v))
                op.waits = ws
        block = stack.enter_context(nc.Block())
        cnt = {e: 0 for e in self.ENGS}

        def run(engname, eng):
            for op in self.ops[engname]:
                for s, v in op.waits:
                    eng.wait_ge(semobj(s), v)
                    cnt[engname] += 1
                ins = op.emit(eng)
                cnt[engname] += 1
                if op.need_inc:
                    ins.then_inc(semobj(op.sem), 16 if op.is_dma else 1)
            if engname == "sp":
                fin = {}
                for op in self.out_dmas:
                    if fin.get(op.sem, 0) < op.val:
                        fin[op.sem] = op.val
                for s, v in fin.items():
                    eng.wait_ge(semobj(s), v)

        @block.tensor
        def _(eng):
            run("pe", eng)

        @block.scalar
        def _(eng):
            run("act", eng)

        @block.vector
        def _(eng):
            run("dve", eng)

        @block.gpsimd
        def _(eng):
            run("pool", eng)

        @block.sync
        def _(eng):
            run("sp", eng)
        return cnt


class V:
    __slots__ = ("ap", "keys")

    def __init__(self, ap, keys):
        self.ap = ap
        self.keys = tuple(keys)

    def __getitem__(self, idx):
        return V(self.ap[idx], self.keys)

    def k(self, *keys):
        return V(self.ap, keys)

    def bc(self, shape):
        return V(self.ap.to_broadcast(list(shape)), self.keys)

    def re(self, s, **kw):
        return V(self.ap.rearrange(s, **kw), self.keys)

    def bitcast(self, dt):
        return V(self.ap.bitcast(dt), self.keys)


def _keys(vs):
    out = []
    for v in vs:
        if v is None:
            continue
        out.extend(v.keys)
    return out


class Bld:
    def __init__(self, nc):
        self.nc = nc
        self.T = Tracker(nc)
        self.st = contextlib.ExitStack()
        self.sb_bytes = 0
        ps = self.st.enter_context(nc.psum_tensor("psum", [128, 4096], F32))
        self.psum = ps
        self.ps_ctr = 0
        self.uid = 0
        self.ps_open = {}

    def sb(self, name, shape, dt, keys=None):
        t = self.st.enter_context(self.nc.sbuf_tensor("s_" + name, list(shape), dt))
        n = 1
        for s in shape[1:]:
            n *= s
        self.sb_bytes += n * (2 if dt == BF16 else 4)
        return V(t[:], keys if keys is not None else [name])

    def pb(self, n=1):
        if n == 2 and self.ps_ctr % 2 == 1:
            self.ps_ctr += 1
        i = self.ps_ctr % 8
        self.ps_ctr += n
        for j in range(n):
            assert not self.ps_open.get(i + j, False), "PSUM bank %d still open" % (i + j)
            self.ps_open[i + j] = True
        return V(self.psum[:, i * 512:(i + n) * 512], [("ps", i + j) for j in range(n)])

    def pf(self, v):
        for k in v.keys:
            self.ps_open[k[1]] = False

    def mm(self, out, lhsT, rhs, start=True, stop=True):
        o, l, r = out.ap, lhsT.ap, rhs.ap
        self.T.add("pe", lambda e: e.matmul(o, lhsT=l, rhs=r, start=start, stop=stop),
                   reads=_keys([lhsT, rhs]), writes=out.keys)

    def tr(self, out, in_, ident):
        o, i, d = out.ap, in_.ap, ident.ap
        self.T.add("pe", lambda e: e.transpose(o, i, d), reads=_keys([in_, ident]), writes=out.keys)

    def act(self, out, in_, func, bias=None, scale=None, accum=None, extra_reads=()):
        kw = {}
        rd = [in_]
        if bias is not None:
            if isinstance(bias, V):
                kw["bias"] = bias.ap
                rd.append(bias)
            else:
                kw["bias"] = float(bias)
        if scale is not None:
            if isinstance(scale, V):
                kw["scale"] = scale.ap
                rd.append(scale)
            else:
                kw["scale"] = float(scale)
        wr = list(out.keys)
        if accum is not None:
            kw["accum_out"] = accum.ap
            wr += list(accum.keys)
        o, i = out.ap, in_.ap
        self.T.add("act", lambda e: e.activation(out=o, in_=i, func=func, **kw),
                   reads=_keys(rd) + list(extra_reads), writes=wr)

    def tt(self, out, a, b, op, eng="dve"):
        o, x, y = out.ap, a.ap, b.ap
        self.T.add(eng, lambda e: e.tensor_tensor(out=o, in0=x, in1=y, op=op),
                   reads=_keys([a, b]), writes=out.keys)

    def ts(self, out, a, s1, op0, s2=None, op1=None, eng="dve", accum=None):
        rd = [a]
        if isinstance(s1, V):
            rd.append(s1)
            s1 = s1.ap
        else:
            s1 = float(s1)
        if isinstance(s2, V):
            rd.append(s2)
            s2 = s2.ap
        elif s2 is not None:
            s2 = float(s2)
        o, x = out.ap, a.ap
        kw = {}
        wr = list(out.keys)
        if op1 is not None:
            kw["op1"] = op1
        if accum is not None:
            kw["accum_out"] = accum.ap
            wr += list(accum.keys)
        self.T.add(eng, lambda e: e.tensor_scalar(out=o, in0=x, scalar1=s1, scalar2=s2, op0=op0, **kw),
                   reads=_keys(rd), writes=wr)

    def stt(self, out, a, s, b, op0, op1):
        rd = [a, b]
        if isinstance(s, V):
            rd.append(s)
            s = s.ap
        else:
            s = float(s)
        o, x, y = out.ap, a.ap, b.ap
        self.T.add("dve", lambda e: e.scalar_tensor_tensor(out=o, in0=x, scalar=s, in1=y, op0=op0, op1=op1),
                   reads=_keys(rd), writes=out.keys)

    def cp(self, out, in_, eng="dve"):
        o, i = out.ap, in_.ap
        if eng == "act":
            self.T.add("act", lambda e: e.copy(out=o, in_=i), reads=in_.keys, writes=out.keys)
        else:
            self.T.add(eng, lambda e: e.tensor_copy(out=o, in_=i), reads=in_.keys, writes=out.keys)

    def red(self, out, in_, op, axis=AX.X):
        o, i = out.ap, in_.ap
        self.T.add("dve", lambda e: e.tensor_reduce(out=o, in_=i, axis=axis, op=op),
                   reads=in_.keys, writes=out.keys)

    def recip(self, out, in_):
        o, i = out.ap, in_.ap
        self.T.add("dve", lambda e: e.reciprocal(out=o, in_=i), reads=in_.keys, writes=out.keys)

    def memset(self, out, val, eng="dve"):
        o = out.ap
        self.T.add(eng, lambda e: e.memset(o, val), writes=out.keys)

    def bnstats(self, out, in_):
        o, i = out.ap, in_.ap
        self.T.add("dve", lambda e: e.bn_stats(out=o, in_=i), reads=in_.keys, writes=out.keys)

    def bnaggr(self, out, in_):
        o, i = out.ap, in_.ap
        self.T.add("dve", lambda e: e.bn_aggr(out=o, in_=i), reads=in_.keys, writes=out.keys)

    def dma_in(self, out, src_ap, q="sp", small=False):
        o = out.ap
        swk = out.keys[0] if q == "pool" else None
        if small:
            self.T.add(q, lambda e: e.dma_start(out=o, in_=src_ap, allow_slow_non_contiguous=True),
                       writes=out.keys, dma=True, swkey=swk)
        else:
            self.T.add(q, lambda e: e.dma_start(out=o, in_=src_ap), writes=out.keys, dma=True, swkey=swk)

    def dma_out(self, dst_ap, src, q="sp", small=False):
        i = src.ap
        if small:
            self.T.add(q, lambda e: e.dma_start(out=dst_ap, in_=i, allow_slow_non_contiguous=True),
                       reads=src.keys, dma=True, is_out=True)
        else:
            self.T.add(q, lambda e: e.dma_start(out=dst_ap, in_=i), reads=src.keys, dma=True, is_out=True)


CST_W = 1928


def make_consts():
    c = np.zeros((128, CST_W), np.float32)
    idx = np.arange(128)
    c[:, 0:128] = np.eye(128, dtype=np.float32)
    le = (idx[:, None] <= idx[None, :])
    c[:, 128:256] = le.astype(np.float32)
    c[:, 256:384] = 1.0
    c[:, 384:512] = np.where(le, 0.0, NEG)
    c[:, 512:640] = np.where(idx[None, :] <= idx[:, None], 0.0, NEG)
    g = 1.0 - np.exp2(-5.0 - np.arange(4, dtype=np.float64))
    lg = np.log(g)
    diff = (idx[None, :] - idx[:, None]).astype(np.float64)
    for h in range(4):
        dec = np.where(diff >= 0, np.exp(lg[h] * np.where(diff >= 0, diff, 0.0)), 0.0)
        c[:, 640 + h * 128: 640 + (h + 1) * 128] = dec
        c[:, 1152 + h * 128: 1152 + (h + 1) * 128] = np.exp(lg[h] * (idx + 1.0))[None, :]
        c[:, 1664 + h] = np.exp(lg[h] * (127.0 - idx))
    c[:, 1672:1928] = np.eye(16, dtype=np.float32).reshape(1, 256)
    cdec = [float(np.exp(lg[h] * 128.0)) for h in range(4)]
    g1 = [float(g[h]) for h in range(4)]
    half = 64
    freqs = (np.float32(10000.0) ** (-(np.arange(half, dtype=np.float32) / np.float32(half)))).astype(np.float32)
    pos = np.concatenate([np.arange(SEQ), np.full(16, PAST)]).astype(np.float32)
    ang = (pos[:, None] * freqs[None, :]).astype(np.float32).astype(np.float64)
    sc = 128.0 ** -0.5
    rope = np.concatenate([np.cos(ang), np.sin(ang), np.cos(ang) * sc, np.sin(ang) * sc], axis=1).astype(np.float32)
    return c, rope, cdec, g1


RET_CDEC = None
RET_G = None


def build_program(do_prompt=True, do_sample=True, n_tiles=NTILES, dbg=False, stage=99, memkv=True):
    global RET_CDEC, RET_G
    cst_np, rope_np, RET_CDEC, RET_G = make_consts()
    nc = bass.Bass("TRN2", target_bir_lowering=False)
    B = Bld(nc)

    def din(name, shape):
        return nc.dram_tensor(name, list(shape), F32, kind="ExternalInput").ap()

    def dout(name, shape):
        return nc.dram_tensor(name, list(shape), F32, kind="ExternalOutput").ap()

    I = {}
    I["x_p"] = din("x_p", [SEQ, D])
    I["x_s"] = din("x_s", [NSMP, D])
    I["mem"] = din("mem", [MEMLEN, D])
    I["cst"] = din("cst", [128, CST_W])
    I["rope"] = din("rope", [SEQ + 16, 256])
    I["st_ret"] = din("st_ret", [DEPTH, NSMP, 4, 128, 256])
    I["st_mlC"] = din("st_mlC", [DEPTH, NSMP, 4, 256, 256])
    I["st_mln"] = din("st_mln", [DEPTH, NSMP, 4, 256])
    I["st_mlm"] = din("st_mlm", [DEPTH, NSMP, 4])
    I["st_mlconv"] = din("st_mlconv", [DEPTH, NSMP, 3, 1024])
    I["st_ssm"] = din("st_ssm", [DEPTH, NSMP, 16, 64, 128])
    I["st_ssmconv"] = din("st_ssmconv", [DEPTH, NSMP, 3, 2048])
    I["c_memk"] = din("c_memk", [DEPTH, NSMP, MEMLEN, 1024])
    I["c_memv"] = din("c_memv", [DEPTH, NSMP, MEMLEN, 1024])
    wshapes = dict(
        norm_mix_w=[DEPTH, D], w_in=[DEPTH, D, NIN], ret_norm_w=[DEPTH, 1024], ml_conv_w=[DEPTH, 4, 1024],
        ml_conv_b=[DEPTH, 1024], ml_wq=[DEPTH, 4, 256, 256], ml_wk=[DEPTH, 4, 256, 256], ml_gate_b=[DEPTH, 8],
        ml_norm_w=[DEPTH, 1024], ssm_conv_w=[DEPTH, 4, 2048], ssm_conv_b=[DEPTH, 2048], ssm_dt_bias=[DEPTH, 16],
        ssm_A_log=[DEPTH, 16], ssm_D=[DEPTH, 16], ssm_norm_w=[DEPTH, 1024], w_br_ret=[DEPTH, 1024, 1024],
        w_br_ml=[DEPTH, 1024, 1024], w_br_ssm=[DEPTH, 1024, 1024], w_out_mix=[DEPTH, 1024, 1024],
        norm_mem_w=[DEPTH, D], mem_wq=[DEPTH, D, 1024], mem_wk=[DEPTH, D, 1024], mem_wv=[DEPTH, D, 1024],
        mem_wo=[DEPTH, 1024, D], norm_mlp_w=[DEPTH, D], mlp_w1=[DEPTH, D, 4096], mlp_w2=[DEPTH, 4096, D],
        norm_f_w=[D])
    W = {k: din(k, s) for k, s in wshapes.items()}
    O = {}
    O["y_p"] = dout("y_p", [SEQ, D])
    O["y_s"] = dout("y_s", [NSMP, D])
    O["ret_p"] = dout("ret_p", [DEPTH, 4, 128, 256])
    O["mlC_p"] = dout("mlC_p", [DEPTH, 4, 256, 256])
    O["mln_p"] = dout("mln_p", [DEPTH, 4, 256])
    O["mlm_p"] = dout("mlm_p", [DEPTH, 4])
    O["mlconv_p"] = dout("mlconv_p", [DEPTH, 3, 1024])
    O["ssm_p"] = dout("ssm_p", [DEPTH, 16, 64, 128])
    O["ssmconv_p"] = dout("ssmconv_p", [DEPTH, 3, 2048])
    O["memk_p"] = dout("memk_p", [DEPTH, MEMLEN, 1024])
    O["memv_p"] = dout("memv_p", [DEPTH, MEMLEN, 1024])
    O["ret_s"] = dout("ret_s", [DEPTH, NSMP, 4, 128, 256])
    O["mlC_s"] = dout("mlC_s", [DEPTH, NSMP, 4, 256, 256])
    O["mln_s"] = dout("mln_s", [DEPTH, NSMP, 4, 256])
    O["mlm_s"] = dout("mlm_s", [DEPTH, NSMP, 4])
    O["mlconv_s"] = dout("mlconv_s", [DEPTH, NSMP, 3, 1024])
    O["ssm_s"] = dout("ssm_s", [DEPTH, NSMP, 16, 64, 128])
    O["ssmconv_s"] = dout("ssmconv_s", [DEPTH, NSMP, 3, 2048])
    if dbg:
        O["dbg"] = dout("dbg", [SEQ, D])

    cst = B.sb("cst", [128, CST_W], F32)
    B.dma_in(cst, I["cst"])
    ident = cst[:, 0:128]
    tri = cst[:, 128:256]
    ones = cst[:, 256:384]
    maskST = cst[:, 384:512]
    maskTM = cst[:, 512:640]
    decayT = cst[:, 640:1152].re("p (h t) -> p h t", h=4)
    qdec = cst[:, 1152:1664].re("p (h t) -> p h t", h=4)
    kdec = cst[:, 1664:1668]
    i16bc = cst[:, 1672:1928].re("p (a b) -> p a b", a=16)
    identb = B.sb("identb", [128, 128], BF16)
    B.cp(identb, ident)
    onesb = B.sb("onesb", [128, 128], BF16)
    B.cp(onesb, ones)

    NSLAB = 4
    slabs = [B.sb("slab%d" % i, [128, 8, 512], BF16, keys=[("slab", i)]) for i in range(NSLAB)]
    slab_ctr = [0]

    WC_MAX = 136
    wcache = nc.dram_tensor("wcache", [WC_MAX, 128, 4096], BF16, kind="Internal").ap()
    wc_ids = {}
    use_cache = not os.environ.get("NO_WCACHE")

    def cached_load(v, src, kc, n):
        key = repr(src)
        if not use_cache:
            B.dma_in(v, src, q="pool")
            return
        if key not in wc_ids:
            sid = len(wc_ids)
            assert sid < WC_MAX
            wc_ids[key] = sid
            B.dma_in(v, src, q="pool")
            dst = wcache[sid][:, 0:kc * n].rearrange("p (k n) -> p k n", k=kc)
            vap = v.ap
            B.T.add("sp", lambda e: e.dma_start(out=dst, in_=vap), reads=v.keys, writes=[("wc", sid)], dma=True)
        else:
            sid = wc_ids[key]
            srcc = wcache[sid][:, 0:kc * n].rearrange("p (k n) -> p k n", k=kc)
            vap = v.ap
            B.T.add("sp", lambda e: e.dma_start(out=vap, in_=srcc), reads=[("wc", sid)], writes=v.keys, dma=True)

    def load_slab(src, kc, n):
        i = slab_ctr[0] % NSLAB
        slab_ctr[0] += 1
        v = slabs[i][:, 0:kc, 0:n]
        cached_load(v, src, kc, n)
        return v

    def win_slab(l, c0, n=512):
        return load_slab(W["w_in"][l][:, c0:c0 + n].rearrange("(kc p) n -> p kc n", p=128), 8, n)

    def sq_slab(w2d, half):
        return load_slab(w2d[:, half * 512:(half + 1) * 512].rearrange("(kc p) n -> p kc n", p=128), 8, 512)

    pstage = B.sb("pstage", [128, 128], F32)
    P = []
    for l in range(DEPTH if not os.environ.get('SKIP_PARAMS') else 0):
        p = {}
        pa = B.sb("parA%d" % l, [128, 72], F32)
        pbt = B.sb("parB%d" % l, [128, 96], F32)
        r = 0
        offs = {}
        for nm, key, nb in (("g_mix", "norm_mix_w", 8), ("g_mem", "norm_mem_w", 8), ("g_mlp", "norm_mlp_w", 8),
                            ("g_ret", "ret_norm_w", 8), ("g_ml", "ml_norm_w", 8), ("g_ssm", "ssm_norm_w", 8),
                            ("cb_ml", "ml_conv_b", 8), ("cb_ssm", "ssm_conv_b", 16)):
            B.dma_in(pstage[r:r + nb, :], W[key][l].rearrange("(b p) -> b p", p=128))
            offs[nm] = (r, nb)
            r += nb
        ps = B.pb()
        B.tr(ps[:, 0:72], pstage[0:72, :], ident[0:72, 0:72])
        B.cp(pa, ps[:, 0:72])
        B.pf(ps)
        for nm, (r0, nb) in offs.items():
            p[nm] = pa[:, r0:r0 + nb]
        B.dma_in(pstage[0:32, :], W["ml_conv_w"][l].rearrange("j (b p) -> (j b) p", p=128))
        B.dma_in(pstage[32:96, :], W["ssm_conv_w"][l].rearrange("j (b p) -> (j b) p", p=128))
        ps = B.pb()
        B.tr(ps[:, 0:96], pstage[0:96, :], ident[0:96, 0:96])
        B.cp(pbt, ps[:, 0:96])
        B.pf(ps)
        p["cw_ml"] = pbt[:, 0:32].re("p (j b) -> p b j", j=4)
        p["cw_ssm"] = pbt[:, 32:96].re("p (j b) -> p b j", j=4)
        t = B.sb("gateb%d" % l, [128, 8], F32)
        B.dma_in(t, W["ml_gate_b"][l].partition_broadcast(128))
        p["gateb"] = t
        t = B.sb("dtb%d" % l, [128, 16], F32)
        B.dma_in(t, W["ssm_dt_bias"][l].partition_broadcast(128))
        p["dtb"] = t
        t = B.sb("alog%d" % l, [128, 16], F32)
        B.dma_in(t, W["ssm_A_log"][l].partition_broadcast(128))
        nega = B.sb("nega%d" % l, [128, 16], F32)
        B.act(nega, t, AF.Exp)
        B.ts(nega, nega, -1.0, ALU.mult)
        p["nega"] = nega
        t = B.sb("ssmD%d" % l, [128, 16], F32)
        B.dma_in(t, W["ssm_D"][l].partition_broadcast(128))
        p["ssmD"] = t
        P.append(p)

    xt = B.sb("xt", [128, NCH, D], F32)
    hT = B.sb("hT", [128, 8, TT], BF16)
    hb = B.sb("hb", [128, D], BF16)
    junk = B.sb("junk", [128, D], BF16)
    f32a = B.sb("f32a", [128, D], F32, keys=[("f32a", q) for q in range(4)])
    f32a_ap = f32a.ap
    small = B.sb("small", [128, 256], F32)
    sm_ctr = [0]

    def sm(n, key=None):
        if sm_ctr[0] + n > 256:
            sm_ctr[0] = 0
        a = sm_ctr[0]
        sm_ctr[0] += n
        return V(small.ap[:, a:a + n], [("small", c) for c in range(a, a + n)])

    def fa(c0, c1):
        return V(f32a_ap[:, c0:c1], [("f32a", q) for q in range(c0 // 256, (c1 + 255) // 256)])

    def rms_rstd(src, width, key):
        ss = sm(1)
        B.act(junk[:, 0:width], src, AF.Square, accum=ss)
        sq = sm(1, key + "sq")
        B.act(sq, ss, AF.Ln, scale=1.0 / width, bias=EPS)
        rs = sm(1)
        B.act(rs, sq, AF.Exp, scale=-0.5)
        return rs

    def transpose_to(dst, src_tm, nblk, gain=None, evac="dve"):
        for b0 in range(0, nblk, 8):
            nb = min(8, nblk - b0)
            ps = B.pb().bitcast(BF16)
            psv = ps[:, 0:nb * 128].re("p (a b) -> p a b", a=nb)
            for j in range(nb):
                B.tr(psv[:, j, :], src_tm[:, (b0 + j) * 128:(b0 + j + 1) * 128], identb)
            d = dst[:, b0:b0 + nb, :]
            if gain is not None:
                B.tt(d, psv, gain[:, b0:b0 + nb].re("p (a o) -> p a o", o=1).bc([128, nb, 128]), ALU.mult)
            elif evac == "act":
                B.cp(d, psv, eng="act")
            else:
                B.cp(d, psv)
            B.pf(ps)

    def norm_to_hT(gain):
        for s in range(NCH):
            rs = rms_rstd(xt[:, s, :], D, "n")
            B.ts(hb, xt[:, s, :], rs, ALU.mult)
            transpose_to(hT[:, :, s * 128:(s + 1) * 128], hb, 8, gain=gain)

    def proj_tm(src_T, s, slab, evac):
        n = slab.ap.shape[2]
        ps = B.pb()
        for kc in range(8):
            B.mm(ps[:, 0:n], src_T[:, kc, s * 128:(s + 1) * 128], slab[:, kc, :], start=(kc == 0), stop=(kc == 7))
        evac(ps[:, 0:n])
        B.pf(ps)

    def proj_fm(src_T, slab, cb, evac, ntok=TT):
        ps = B.pb()
        for kc in range(8):
            B.mm(ps[:, 0:ntok], slab[:, kc, cb * 128:(cb + 1) * 128], src_T[:, kc, 0:ntok],
                 start=(kc == 0), stop=(kc == 7))
        evac(ps[:, 0:ntok])
        B.pf(ps)

    memT = B.sb("memT", [128, 8, MEMLEN], BF16)
    KT = [B.sb("KT%d" % l, [128, 8, MEMLEN], BF16) for l in range(DEPTH)]
    Vtm = [B.sb("Vtm%d" % l, [128, 2, 1024], BF16) for l in range(DEPTH)]
    MKL = int(os.environ.get("MEMKV_LEVEL", "9"))
    mk_ng = [0]
    if do_prompt and memkv:
        for mc in range(2):
            B.dma_in(hb, I["mem"][mc * 128:(mc + 1) * 128, :], q="pool")
            transpose_to(memT[:, :, mc * 128:(mc + 1) * 128], hb, 8)
        for l in range(DEPTH if MKL >= 2 else 0):
            for which, wname, oname in ((0, "mem_wk", "memk_p"), (1, "mem_wv", "memv_p")):
                for half in range(2):
                    slab = sq_slab(W[wname][l], half)
                    for mc in range(2 if MKL >= 3 else 0):
                        mk_ng[0] += 1
                        if mk_ng[0] > int(os.environ.get("MK_NG", "999")):
                            continue
                        ps = B.pb()
                        for kc in range(8):
                            B.mm(ps, memT[:, kc, mc * 128:(mc + 1) * 128], slab[:, kc, :], start=(kc == 0), stop=(kc == 7))
                        stg = fa(0, 512) if (mc % 2 == 0) else fa(512, 1024)
                        B.cp(stg, ps, eng="act")
                        if MKL >= 4:
                            B.dma_out(O[oname][l][mc * 128:(mc + 1) * 128, half * 512:(half + 1) * 512], stg)
                        if which == 1 and not os.environ.get("MK_NOV"):
                            B.cp(Vtm[l][:, mc, half * 512:(half + 1) * 512], ps)
                        B.pf(ps)
                    if which == 0 and MKL >= 5:
                        for cb in range(4):
                            ps = B.pb()
                            for kc in range(8):
                                B.mm(ps[:, 0:MEMLEN], slab[:, kc, cb * 128:(cb + 1) * 128], memT[:, kc, :],
                                     start=(kc == 0), stop=(kc == 7))
                            B.cp(KT[l][:, half * 4 + cb, :], ps[:, 0:MEMLEN])
                            B.pf(ps)

    retS = [B.sb("retS%d" % l, [128, 4, 256], F32) for l in range(DEPTH)]
    mlC = [B.sb("mlC%d" % l, [128, 8, 256], F32) for l in range(DEPTH)]
    mln = [B.sb("mln%d" % l, [128, 8], F32) for l in range(DEPTH)]
    mlm = [B.sb("mlm%d" % l, [128, 4], F32) for l in range(DEPTH)]
    ssmS = [B.sb("ssmS%d" % l, [128, 4, 256], F32) for l in range(DEPTH)]
    hist_ml = [B.sb("hist_ml%d" % l, [128, 8, 3], BF16) for l in range(DEPTH)]
    hist_ssm = [B.sb("hist_ssm%d" % l, [128, 16, 3], BF16) for l in range(DEPTH)]
    stb = B.sb("stb", [128, 8, 256], BF16)
    nb16 = B.sb("nb16", [128, 8], BF16)
    if do_prompt and not os.environ.get('SKIP_MEMSET'):
        for l in range(DEPTH):
            for t in (retS[l], mlC[l], mln[l], mlm[l], ssmS[l], hist_ml[l], hist_ssm[l]):
                B.memset(t, 0.0)

    ropet = B.sb("ropet", [128, NCH, 256], F32)
    qkr = B.sb("qkr", [128, NCH, 1024], BF16)
    v_tm = B.sb("v_tm", [128, NCH, 1024], BF16)
    gsil = B.sb("gsil", [128, NCH, 1024], BF16)
    yT = B.sb("yT", [128, 8, TT], BF16)
    merged = B.sb("merged", [128, NCH, D], F32)
    gtmp = B.sb("gtmp", [128, 512], F32)
    yb = B.sb("yb", [128, D], BF16)
    qkT = B.sb("qkT", [128, 8, 128], BF16)
    qTd = B.sb("qTd", [128, 4, 128], BF16)
    kd = B.sb("kd", [128, 512], BF16)
    scm = B.sb("scm", [128, 4, 128], BF16)
    UW = TT + 8
    big = B.sb("big", [128, 16 * UW + 16 * TT], BF16)
    upre_ap = big.ap[:, 0:16 * UW].rearrange("p (b w) -> p b w", b=16)
    ucT_ap = big.ap[:, 16 * UW:16 * UW + 16 * TT].rearrange("p (b w) -> p b w", b=16)
    big_keys = [("upre", b) for b in range(16)] + [("ucT", b) for b in range(16)]
    upre = V(upre_ap, [("upre", b) for b in range(16)])
    ucT = V(ucT_ap, [("ucT", b) for b in range(16)])

    def upre_b(b):
        return V(upre_ap[:, b, :], [("upre", b)])

    def ucT_b(b):
        return V(ucT_ap[:, b, :], [("ucT", b)])
    h1T = V(big.ap[:, 0:32 * TT].rearrange("p (j t) -> p j t", j=32), big_keys)
    diag = B.sb("diag", [128, 4, 128], BF16)
    igf = B.sb("igf", [128, NCH, 16], F32)
    wqk = B.sb("wqk", [128, 2, 8, 256], BF16)
    mqT = B.sb("mqT", [128, 8, 128], BF16)
    mkT = B.sb("mkT", [128, 8, 128], BF16)
    kw = B.sb("kw", [128, 1024], BF16)
    dg = B.sb("dg", [128, 4, 128], F32)
    Dm = B.sb("Dm", [128, 4, 128], F32)
    xtm = V(kw.ap, kw.keys)
    xdt = V(mqT.ap.rearrange("p a b -> p (a b)"), mqT.keys)
    xw = V(mkT.ap.rearrange("p a b -> p (a b)"), mkT.keys)
    Btm = B.sb("Btm", [128, 512], BF16)
    MTs = B.sb("MTs", [128, 4, 128], F32)
    MTh = B.sb("MTh", [128, 4, 128], BF16)
    cumT = B.sb("cumT", [16, 128], F32)
    rblk = B.sb("rblk", [16, 4, 128], F32)
    qaT = V(qkr.ap.rearrange("p a (k t) -> p (a k) t", t=TT)[:, 0:8, :], qkr.keys)
    pn = B.sb("pn", [128, 256], BF16)
    pT = V(v_tm.ap.rearrange("p a (h m t) -> p (a h) m t", m=2, t=TT)[:, 0:4, :, :], v_tm.keys)
    rl = B.sb("rl", [128, TT], F32)
    rl2 = B.sb("rl2", [128, TT], F32)
    c3 = gtmp[0:3, :]
    stT = B.sb("stT", [128, 128], F32)

    def bcast_mid(v, n_mid, n_in):
        return v.re("p (a o) -> p a o", o=1).bc([128, n_mid, n_in])

    def branch(l, c_gate, wname):
        first = (wname == "w_br_ret")
        for half in range(2):
            gs = win_slab(l, c_gate + half * 512)
            ws = sq_slab(W[wname][l], half)
            for s in range(NCH):
                proj_tm(hT, s, gs, lambda ps: B.act(gtmp, ps, AF.Sigmoid))
                dst = merged[:, s, half * 512:(half + 1) * 512]

                def ev(ps, dst=dst):
                    if first:
                        B.tt(dst, ps, gtmp, ALU.mult)
                    else:
                        B.tt(gtmp, ps, gtmp, ALU.mult)
                        B.tt(dst, dst, gtmp, ALU.add)
                proj_tm(yT, s, ws, ev)

    def headnorm_center(src, h, scale_extra=None):
        st6 = sm(6, "bn6")
        B.bnstats(st6, src[:, h * 256:(h + 1) * 256])
        mv = sm(2, "bnmv")
        B.bnaggr(mv, st6)
        return mv

    def retention_phase(l, ti):
        p = P[l]
        for which in range(2):
            slab = win_slab(l, C_Q + which * 512)
            for s in range(NCH):
                def ev(ps, s=s, which=which):
                    cs = ropet[:, s, which * 128: which * 128 + 64]
                    sn = ropet[:, s, which * 128 + 64: which * 128 + 128]
                    csb = cs.re("p (o d) -> p o d", o=1).bc([128, 4, 64])
                    snb = sn.re("p (o d) -> p o d", o=1).bc([128, 4, 64])
                    pv = ps.re("p (h d) -> p h d", h=4)
                    x1 = pv[:, :, 0:64]
                    x2 = pv[:, :, 64:128]
                    t1 = fa(0, 256).re("p (h d) -> p h d", h=4)
                    t2 = fa(256, 512).re("p (h d) -> p h d", h=4)
                    ov = qkr[:, s, which * 512:(which + 1) * 512].re("p (h d) -> p h d", h=4)
                    B.tt(t1, x1, csb, ALU.mult)
                    B.tt(t2, x2, snb, ALU.mult)
                    B.tt(ov[:, :, 0:64], t1, t2, ALU.subtract)
                    B.tt(t1, x1, snb, ALU.mult)
                    B.tt(t2, x2, csb, ALU.mult)
                    B.tt(ov[:, :, 64:128], t1, t2, ALU.add)
                proj_tm(hT, s, slab, ev)
        for half in range(2):
            slab = win_slab(l, C_V + half * 512)
            for s in range(NCH):
                proj_tm(hT, s, slab, lambda ps, s=s, half=half: B.cp(v_tm[:, s, half * 512:(half + 1) * 512], ps, eng="act"))
        for half in range(2):
            slab = win_slab(l, C_G + half * 512)
            for s in range(NCH):
                proj_tm(hT, s, slab, lambda ps, s=s, half=half: B.act(gsil[:, s, half * 512:(half + 1) * 512], ps, AF.Silu))
        for s in range(NCH):
            ps = B.pb().bitcast(BF16).re("p (a b) -> p a b", a=8)
            for j in range(8):
                B.tr(ps[:, j, :], qkr[:, s, j * 128:(j + 1) * 128], identb)
            B.cp(qkT, ps, eng="act")
            B.tt(qTd, ps[:, 0:4, :], qdec, ALU.mult)
            B.pf(ps)
            B.tt(kd.re("p (h d) -> p h d", h=4), qkr[:, s, 512:1024].re("p (h d) -> p h d", h=4),
                 bcast_mid(kdec, 4, 128), ALU.mult)
            ps = B.pb()
            psv = ps.re("p (h t) -> p h t", h=4)
            for h in range(4):
                B.mm(psv[:, h, :], qkT[:, 4 + h, :], qkT[:, h, :])
            B.tt(scm, psv, decayT, ALU.mult)
            B.pf(ps)
            B.cp(stb[:, 0:4, :], retS[l], eng="act")
            yps = B.pb(2)
            for h in range(4):
                o = yps[:, h * 256:(h + 1) * 256]
                B.mm(o, scm[:, h, :], v_tm[:, s, h * 256:(h + 1) * 256], start=True, stop=False)
                B.mm(o, qTd[:, h, :], stb[:, h, :], start=False, stop=True)
            sps = B.pb(2)
            for h in range(4):
                B.mm(sps[:, h * 256:(h + 1) * 256], kd[:, h * 128:(h + 1) * 128], v_tm[:, s, h * 256:(h + 1) * 256])
            for h in range(4):
                mv = headnorm_center(yps, h)
                sq = sm(1, "hsq")
                B.act(sq, mv[:, 1:2], AF.Ln, bias=EPS)
                rs = sm(1)
                B.act(rs, sq, AF.Exp, scale=-0.5)
                tmp = fa(h * 256, (h + 1) * 256)
                B.ts(tmp, yps[:, h * 256:(h + 1) * 256], mv[:, 0:1], ALU.subtract, rs, ALU.mult)
                B.tt(yb[:, h * 256:(h + 1) * 256], tmp, gsil[:, s, h * 256:(h + 1) * 256], ALU.mult)
            B.pf(yps)
            transpose_to(yT[:, :, s * 128:(s + 1) * 128], yb, 8, gain=p["g_ret"])
            for h in range(4):
                B.stt(retS[l][:, h, :], retS[l][:, h, :], RET_CDEC[h], sps[:, h * 256:(h + 1) * 256], ALU.mult, ALU.add)
            B.pf(sps)
        branch(l, C_GR, "w_br_ret")

    identbc4 = V(identb.ap.rearrange("p (o s) -> p o s", o=1).to_broadcast([128, 4, 128]), identb.keys)
    identc4 = ident.re("p (o s) -> p o s", o=1).bc([128, 4, 128])
    maskST4 = maskST.re("p (o s) -> p o s", o=1).bc([128, 4, 128])
    maskTM4 = maskTM.re("p (o s) -> p o s", o=1).bc([128, 4, 128])

    def conv_blocks(l, slab, blk0, cwv, cbv, hist):
        for cb in range(4):
            blk = blk0 + cb
            ub = upre_b(blk)
            B.cp(ub[:, 0:3], hist[:, blk, :])
            proj_fm(hT, slab, cb, lambda ps, ub=ub: B.cp(ub[:, 3:3 + TT], ps, eng="act"))
            B.cp(hist[:, blk, :], ub[:, TT:TT + 3])
            B.tt(diag, identbc4, bcast_mid(cwv[:, blk, :], 4, 128), ALU.mult)
            ps = B.pb()
            for j in range(4):
                B.mm(ps[:, 0:TT], diag[:, j, :], ub[:, j:j + TT], start=(j == 0), stop=(j == 3))
            B.act(ucT_b(blk), ps[:, 0:TT], AF.Silu, bias=cbv[:, blk:blk + 1])
            B.pf(ps)

    def conv_state_out(l, slab, oname, c0):
        ps = B.pb()
        for kc in range(8):
            B.mm(ps[0:3, :], hT[:, kc, TT - 3:TT], slab[:, kc, :], start=(kc == 0), stop=(kc == 7))
        B.cp(c3, ps[0:3, :])
        B.pf(ps)
        B.dma_out(O[oname][l][:, c0:c0 + 512], c3)

    def mlstm_phase(l, ti):
        p = P[l]
        last = (ti == n_tiles - 1)
        for half in range(2):
            slab = win_slab(l, C_U + half * 512)
            conv_blocks(l, slab, half * 4, p["cw_ml"], p["cb_ml"], hist_ml[l])
            if last:
                conv_state_out(l, slab, "mlconv_p", half * 512)
        for half in range(2):
            slab = win_slab(l, C_VM + half * 512)
            for s in range(NCH):
                proj_tm(hT, s, slab, lambda ps, s=s, half=half: B.cp(v_tm[:, s, half * 512:(half + 1) * 512], ps, eng="act"))
        for half in range(2):
            slab = win_slab(l, C_OM + half * 512)
            for s in range(NCH):
                proj_tm(hT, s, slab, lambda ps, s=s, half=half: B.act(gsil[:, s, half * 512:(half + 1) * 512], ps, AF.Sigmoid))
        slab = win_slab(l, C_IG, 8)
        for s in range(NCH):
            proj_tm(hT, s, slab, lambda ps, s=s: B.cp(igf[:, s, 0:8], ps))
        for wi, wname in ((0, "ml_wq"), (1, "ml_wk")):
            cached_load(wqk[:, wi], W[wname][l].rearrange("h (dc p) e -> p (h dc) e", p=128), 8, 256)
        for s in range(NCH):
            sl = slice(s * 128, (s + 1) * 128)
            B.cp(stb, mlC[l], eng="act")
            B.cp(nb16, mln[l])
            for dstT, wi, scl in ((mqT, 0, 1.0), (mkT, 1, 1.0 / 16.0)):
                ps = B.pb(2)
                psv = ps.re("p (a t) -> p a t", a=8)
                for h in range(4):
                    for ec in range(2):
                        for dc in range(2):
                            B.mm(psv[:, h * 2 + ec, :], wqk[:, wi, h * 2 + dc, ec * 128:(ec + 1) * 128],
                                 ucT_b(h * 2 + dc)[:, sl], start=(dc == 0), stop=(dc == 1))
                B.act(dstT, psv, AF.Copy, scale=scl)
                B.pf(ps)
            z = sm(8)
            B.tt(z, igf[:, s, 0:8], p["gateb"], ALU.add)
            e = sm(4)
            B.act(e, z[:, 4:8], AF.Exp, scale=-1.0)
            sp = sm(4)
            B.act(sp, e, AF.Ln, bias=1.0)
            cps = B.pb()
            B.mm(cps[:, 0:4], tri, sp)
            B.mm(cps[:, 4:8], ones, sp)
            bsp = sm(4)
            B.cp(bsp, cps[:, 0:4])
            btot = sm(4)
            B.cp(btot, cps[:, 4:8])
            B.pf(cps)
            a = sm(4)
            B.tt(a, z[:, 0:4], bsp, ALU.add)
            B.tt(dg, identc4, bcast_mid(a, 4, 128), ALU.mult)
            aps = B.pb()
            B.mm(aps, ones, dg.re("p h s -> p (h s)"))
            B.tt(Dm, aps.re("p (h s) -> p h s", h=4), maskTM4, ALU.add)
            B.pf(aps)
            cm = sm(4)
            B.red(cm, Dm, ALU.max)
            Mx = sm(4)
            B.tt(Mx, cm, mlm[l], ALU.max)
            B.tt(dg, identc4, bcast_mid(Mx, 4, 128), ALU.mult)
            mps = B.pb()
            B.mm(mps, ones, dg.re("p h s -> p (h s)"))
            mpv = mps.re("p (h t) -> p h t", h=4)
            mlast = sm(4)
            B.cp(mlast, mpv[:, :, 127])
            B.stt(Dm, mpv, -1.0, maskST4, ALU.mult, ALU.add)
            B.pf(mps)
            B.tt(Dm, Dm, bcast_mid(a, 4, 128), ALU.add)
            B.act(Dm, Dm, AF.Exp)
            wint = sm(4)
            B.tt(wint, mlm[l], Mx, ALU.subtract)
            B.act(wint, wint, AF.Exp)
            flo = sm(4)
            B.tt(flo, bsp, Mx, ALU.subtract)
            B.act(flo, flo, AF.Exp)
            ws16 = sm(4)
            B.tt(ws16, a, mlast, ALU.subtract)
            B.act(ws16, ws16, AF.Exp)
            B.ts(ws16, ws16, 1.0 / 16.0, ALU.mult)
            wprev = sm(4)
            B.tt(wprev, mlm[l], mlast, ALU.subtract)
            B.act(wprev, wprev, AF.Exp)
            sps = B.pb()
            spv = sps.re("p (h t) -> p h t", h=4)
            for h in range(4):
                for ec in range(2):
                    B.mm(spv[:, h, :], mkT[:, h * 2 + ec, :], mqT[:, h * 2 + ec, :], start=(ec == 0), stop=(ec == 1))
            B.tt(scm, spv, Dm, ALU.mult)
            B.pf(sps)
            nps = B.pb(2)
            for h in range(4):
                B.mm(nps[:, h * 256:(h + 1) * 256], scm[:, h, :], v_tm[:, s, h * 256:(h + 1) * 256])
            dps = B.pb()
            for h in range(4):
                B.mm(dps[:, h:h + 1], scm[:, h, :], onesb[:, 0:1])
            for h in range(4):
                for dc in range(2):
                    B.mm(dps[:, 4 + h:5 + h], mqT[:, h * 2 + dc, :], nb16[:, h * 2 + dc:h * 2 + dc + 1],
                         start=(dc == 0), stop=(dc == 1))
            ips = B.pb(2)
            for h in range(4):
                for dc in range(2):
                    B.mm(ips[:, h * 256:(h + 1) * 256], mqT[:, h * 2 + dc, :], stb[:, h * 2 + dc, :],
                         start=(dc == 0), stop=(dc == 1))
            B.cp(f32a, nps, eng="act")
            B.pf(nps)
            for h in range(4):
                q = fa(h * 256, (h + 1) * 256)
                B.stt(q, ips[:, h * 256:(h + 1) * 256], wint[:, h:h + 1], q, ALU.mult, ALU.add)
            B.pf(ips)
            dd = sm(8)
            B.cp(dd, dps[:, 0:8])
            B.pf(dps)
            den = sm(4)
            B.tt(den, dd[:, 4:8], wint, ALU.mult)
            B.tt(den, den, dd[:, 0:4], ALU.add)
            nden = sm(4)
            B.ts(nden, den, -1.0, ALU.mult)
            B.tt(den, den, nden, ALU.max)
            B.tt(den, den, flo, ALU.max)
            r = sm(4)
            B.recip(r, den)
            kps = B.pb(2)
            for h in range(4):
                for dc in range(2):
                    B.mm(kps[:, h * 256:(h + 1) * 256], ucT_b(h * 2 + dc)[:, sl], wqk[:, 1, h * 2 + dc, :],
                         start=(dc == 0), stop=(dc == 1))
            for h in range(4):
                B.act(kw[:, h * 256:(h + 1) * 256], kps[:, h * 256:(h + 1) * 256], AF.Copy, scale=ws16[:, h:h + 1])
            B.pf(kps)
            cpsl = []
            for hp in range(2):
                cps = B.pb(2)
                cpsl.append(cps)
                for hh in range(2):
                    h = hp * 2 + hh
                    for dc in range(2):
                        B.mm(cps[:, (hh * 2 + dc) * 256:(hh * 2 + dc + 1) * 256],
                             kw[:, h * 256 + dc * 128:h * 256 + (dc + 1) * 128], v_tm[:, s, h * 256:(h + 1) * 256])
            n2 = B.pb()
            for h in range(4):
                for dc in range(2):
                    B.mm(n2[:, h * 2 + dc:h * 2 + dc + 1], kw[:, h * 256 + dc * 128:h * 256 + (dc + 1) * 128], onesb[:, 0:1])
            for h in range(4):
                q = fa(h * 256, (h + 1) * 256)
                mv = headnorm_center(f32a, h)
                t = sm(1)
                B.tt(t, r[:, h:h + 1], r[:, h:h + 1], ALU.mult)
                B.tt(t, t, mv[:, 1:2], ALU.mult)
                sq = sm(1)
                B.act(sq, t, AF.Ln, bias=EPS)
                rs = sm(1)
                B.act(rs, sq, AF.Exp, scale=-0.5)
                B.tt(rs, rs, r[:, h:h + 1], ALU.mult)
                B.ts(q, q, mv[:, 0:1], ALU.subtract, rs, ALU.mult)
                B.tt(yb[:, h * 256:(h + 1) * 256], q, gsil[:, s, h * 256:(h + 1) * 256], ALU.mult)
            transpose_to(yT[:, :, sl], yb, 8, gain=p["g_ml"])
            for hp in range(2):
                cps = cpsl[hp]
                for hh in range(2):
                    h = hp * 2 + hh
                    dst = mlC[l][:, h * 2:h * 2 + 2, :]
                    B.stt(dst, dst, wprev[:, h:h + 1], cps[:, hh * 512:(hh + 1) * 512].re("p (a e) -> p a e", a=2),
                          ALU.mult, ALU.add)
                B.pf(cps)
            mv4 = mln[l].re("p (h c) -> p h c", c=2)
            B.tt(mv4, mv4, bcast_mid(wprev, 4, 2), ALU.mult)
            B.tt(mln[l], mln[l], n2[:, 0:8], ALU.add)
            B.pf(n2)
            B.tt(mlm[l], mlast, btot, ALU.subtract)
        branch(l, C_GM, "w_br_ml")

    def ssd_phase(l, ti):
        p = P[l]
        last = (ti == n_tiles - 1)
        for half in range(2):
            slab = win_slab(l, C_Z + half * 512)
            for s in range(NCH):
                proj_tm(hT, s, slab, lambda ps, s=s, half=half: B.act(gsil[:, s, half * 512:(half + 1) * 512], ps, AF.Silu))
        for q4 in range(4):
            slab = win_slab(l, C_XBC + q4 * 512)
            conv_blocks(l, slab, q4 * 4, p["cw_ssm"], p["cb_ssm"], hist_ssm[l])
            if last:
                conv_state_out(l, slab, "ssmconv_p", q4 * 512)
        slab = win_slab(l, C_DT, 16)
        for s in range(NCH):
            proj_tm(hT, s, slab, lambda ps, s=s: B.cp(igf[:, s, :], ps))
        for s in range(NCH):
            sl = slice(s * 128, (s + 1) * 128)
            B.cp(stb[:, 0:4, :], ssmS[l], eng="act")
            z = sm(16)
            B.tt(z, igf[:, s, :], p["dtb"], ALU.add)
            B.act(z, z, AF.Exp)
            dt = sm(16)
            B.act(dt, z, AF.Ln, bias=1.0)
            dA = sm(16)
            B.tt(dA, dt, p["nega"], ALU.mult)
            c1 = B.pb()
            B.mm(c1[0:16, 0:128], dA, tri)
            B.cp(cumT, c1[0:16, 0:128])
            B.pf(c1)
            c2 = B.pb()
            B.mm(c2[:, 0:16], tri, dA)
            B.mm(c2[:, 16:32], ones, dA)
            ecum = sm(16)
            B.act(ecum, c2[:, 0:16], AF.Exp)
            ncum = sm(16)
            B.ts(ncum, c2[:, 0:16], -1.0, ALU.mult)
            edec = sm(16)
            B.act(edec, c2[:, 16:32], AF.Exp)
            B.pf(c2)
            ps = B.pb().bitcast(BF16)
            for j in range(8):
                B.tr(ps[:, j * 128:(j + 1) * 128], ucT_b(j)[:, sl], identb)
            B.cp(xtm, ps, eng="act")
            B.tt(xdt.re("p (h c) -> p h c", h=16), ps.re("p (h c) -> p h c", h=16), bcast_mid(dt, 16, 64), ALU.mult)
            B.pf(ps)
            ps = B.pb().bitcast(BF16)
            for g in range(4):
                B.tr(ps[:, g * 128:(g + 1) * 128], ucT_b(8 + g)[:, sl], identb)
            B.cp(Btm, ps[:, 0:512], eng="act")
            B.pf(ps)
            ps = B.pb()
            for g in range(4):
                B.mm(ps[:, g * 128:(g + 1) * 128], ucT_b(8 + g)[:, sl], ucT_b(12 + g)[:, sl])
            B.cp(MTs, ps.re("p (g t) -> p g t", g=4), eng="act")
            B.pf(ps)
            yps = B.pb(2)
            wsl = sm(16)
            for g in range(4):
                B.tt(rblk, V(cumT.ap.rearrange("k (o t) -> k o t", o=1).to_broadcast([16, 4, 128]), cumT.keys),
                     V(ident.ap[0:16, 4 * g:4 * g + 4].rearrange("k (i o) -> k i o", o=1).to_broadcast([16, 4, 128]), ident.keys),
                     ALU.mult)
                cr = B.pb()
                B.mm(cr, ones[0:16, :], rblk.re("k i t -> k (i t)"))
                B.tt(dg, cr.re("p (i t) -> p i t", i=4), maskST4, ALU.add)
                B.pf(cr)
                for i in range(4):
                    B.act(Dm[:, i, :], dg[:, i, :], AF.Exp, bias=ncum[:, 4 * g + i:4 * g + i + 1])
                B.tt(MTh, Dm, V(MTs.ap[:, g:g + 1, :].to_broadcast([128, 4, 128]), MTs.keys), ALU.mult)
                B.cp(wsl[:, 4 * g:4 * g + 4], Dm[:, :, 127])
                for i in range(4):
                    h = 4 * g + i
                    B.mm(yps[:, h * 64:(h + 1) * 64], MTh[:, i, :], xdt[:, h * 64:(h + 1) * 64])
            ips = B.pb(2)
            for g in range(4):
                B.mm(ips[:, g * 256:(g + 1) * 256], ucT_b(12 + g)[:, sl], stb[:, g, :])
            y3 = f32a.re("p (h c) -> p h c", h=16)
            B.tt(y3, ips.re("p (h c) -> p h c", h=16), bcast_mid(ecum, 16, 64), ALU.mult)
            B.pf(ips)
            B.tt(f32a, f32a, yps, ALU.add)
            B.pf(yps)
            B.tt(xw.re("p (h c) -> p h c", h=16), xdt.re("p (h c) -> p h c", h=16), bcast_mid(wsl, 16, 64), ALU.mult)
            sps = B.pb(2)
            for g in range(4):
                B.mm(sps[:, g * 256:(g + 1) * 256], Btm[:, g * 128:(g + 1) * 128], xw[:, g * 256:(g + 1) * 256])
            m3 = merged[:, s, :].re("p (h c) -> p h c", h=16) if False else None
            xd = V(gtmp.ap.bitcast(BF16)[:, 0:1024], gtmp.keys)
            B.tt(xd.re("p (h c) -> p h c", h=16), xtm.re("p (h c) -> p h c", h=16), bcast_mid(p["ssmD"], 16, 64), ALU.mult)
            B.tt(f32a, f32a, xd, ALU.add)
            B.tt(f32a, f32a, gsil[:, s, :], ALU.mult)
            ss = sm(4)
            for g in range(4):
                B.act(junk[:, 0:256], fa(g * 256, (g + 1) * 256), AF.Square, accum=ss[:, g:g + 1])
            sq = sm(4)
            B.act(sq, ss, AF.Ln, scale=1.0 / 256.0, bias=EPS)
            rs = sm(4)
            B.act(rs, sq, AF.Exp, scale=-0.5)
            for g in range(4):
                B.act(yb[:, g * 256:(g + 1) * 256], fa(g * 256, (g + 1) * 256), AF.Copy, scale=rs[:, g:g + 1])
            transpose_to(yT[:, :, sl], yb, 8, gain=p["g_ssm"])
            S3 = ssmS[l].re("p g (r c) -> p (g r) c", r=4)
            B.tt(S3, S3, bcast_mid(edec, 16, 64), ALU.mult)
            B.tt(ssmS[l].re("p g c -> p (g c)"), ssmS[l].re("p g c -> p (g c)"), sps, ALU.add)
            B.pf(sps)
        branch(l, C_GS, "w_br_ssm")

    def add_proj_to_x(src_T, w2d):
        for half in range(2):
            slab = sq_slab(w2d, half)
            for s in range(NCH):
                dst = xt[:, s, half * 512:(half + 1) * 512]
                proj_tm(src_T, s, slab, lambda ps, dst=dst: B.tt(dst, ps, dst, ALU.add))

    def outmix_phase(l):
        for s in range(NCH):
            B.cp(hb, merged[:, s, :], eng="act")
            transpose_to(yT[:, :, s * 128:(s + 1) * 128], hb, 8, evac="act")
        add_proj_to_x(yT, W["w_out_mix"][l])

    def memattn_phase(l):
        norm_to_hT(P[l]["g_mem"])
        for half in range(2):
            slab = sq_slab(W["mem_wq"][l], half)
            for cb in range(4):
                proj_fm(hT, slab, cb, lambda ps, j=half * 4 + cb: B.cp(qaT[:, j, :], ps, eng="act"))
        for s in range(NCH):
            sl = slice(s * 128, (s + 1) * 128)
            for h in range(4):
                ps = B.pb()
                for dc in range(2):
                    B.mm(ps[:, 0:256], qaT[:, h * 2 + dc, sl], KT[l][:, h * 2 + dc, :], start=(dc == 0), stop=(dc == 1))
                mx = sm(1)
                B.red(mx, ps[:, 0:256], ALU.max)
                B.ts(mx, mx, -1.0 / 16.0, ALU.mult)
                ssum = sm(1)
                pe32 = fa(0, 256)
                B.act(pe32, ps[:, 0:256], AF.Exp, bias=mx, scale=1.0 / 16.0, accum=ssum)
                B.pf(ps)
                rs = sm(1)
                B.recip(rs, ssum)
                B.ts(pn, pe32, rs, ALU.mult)
                ps = B.pb().bitcast(BF16)
                for mc in range(2):
                    B.tr(ps[:, mc * 128:(mc + 1) * 128], pn[:, mc * 128:(mc + 1) * 128], identb)
                B.cp(pT[:, h, :, sl], ps[:, 0:256].re("p (m t) -> p m t", m=2), eng="act")
                B.pf(ps)
        for h in range(4):
            for eb in range(2):
                ps = B.pb()
                for mc in range(2):
                    B.mm(ps[:, 0:TT], Vtm[l][:, mc, h * 256 + eb * 128:h * 256 + (eb + 1) * 128], pT[:, h, mc, :],
                         start=(mc == 0), stop=(mc == 1))
                B.cp(yT[:, h * 2 + eb, :], ps[:, 0:TT])
                B.pf(ps)
        add_proj_to_x(yT, W["mem_wo"][l])

    def mlp_phase(l):
        norm_to_hT(P[l]["g_mlp"])
        for j8 in range(8):
            slab = load_slab(W["mlp_w1"][l][:, j8 * 512:(j8 + 1) * 512].rearrange("(kc p) n -> p kc n", p=128), 8, 512)
            for cb in range(4):
                j = j8 * 4 + cb

                def ev(ps, j=j):
                    r_ = rl if j % 2 == 0 else rl2
                    B.act(r_, ps, AF.Relu)
                    B.act(h1T[:, j, :], r_, AF.Square)
                proj_fm(hT, slab, cb, ev)
        for half in range(2):
            pss = [B.pb() for s in range(NCH)]
            for kg in range(4):
                slab = load_slab(W["mlp_w2"][l][kg * 1024:(kg + 1) * 1024, half * 512:(half + 1) * 512]
                                 .rearrange("(kc p) n -> p kc n", p=128), 8, 512)
                for s in range(NCH):
                    for kc in range(8):
                        B.mm(pss[s], h1T[:, kg * 8 + kc, s * 128:(s + 1) * 128], slab[:, kc, :],
                             start=(kg == 0 and kc == 0), stop=(kg == 3 and kc == 7))
            for s in range(NCH):
                dst = xt[:, s, half * 512:(half + 1) * 512]
                B.tt(dst, pss[s], dst, ALU.add)
                B.pf(pss[s])

    def final_norm_out(ti):
        B.dma_in(f32a, W["norm_f_w"].partition_broadcast(128))
        for s in range(NCH):
            rs = rms_rstd(xt[:, s, :], D, "f")
            B.stt(merged[:, s, :], xt[:, s, :], rs, f32a, ALU.mult, ALU.mult)
        B.dma_out(O["y_p"][ti * TT:(ti + 1) * TT, :].rearrange("(s p) d -> p s d", p=128), merged)

    def prompt_state_out(l):
        B.dma_out(O["ret_p"][l].rearrange("h d e -> d h e"), retS[l])
        B.dma_out(O["mlC_p"][l].rearrange("h (dc p) e -> p (h dc) e", p=128), mlC[l])
        ps = B.pb()
        B.tr(ps[0:8, 0:128], mln[l], ident)
        B.cp(stT[0:8, :], ps[0:8, 0:128])
        B.pf(ps)
        B.dma_out(O["mln_p"][l].rearrange("h (dc p) -> (h dc) p", p=128), stT[0:8, :])
        B.dma_out(O["mlm_p"][l:l + 1, :], mlm[l][0:1, :])
        for g in range(4):
            for hf in range(2):
                ps = B.pb()
                B.tr(ps[:, 0:128], ssmS[l][:, g, hf * 128:(hf + 1) * 128], ident)
                B.cp(stT, ps[:, 0:128])
                B.pf(ps)
                h0 = 4 * g + 2 * hf
                B.dma_out(O["ssm_p"][l][h0:h0 + 2].rearrange("h p n -> (h p) n"), stT)

    if do_prompt:
        for ti in range(n_tiles):
            B.dma_in(xt, I["x_p"][ti * TT:(ti + 1) * TT, :].rearrange("(s p) d -> p s d", p=128))
            B.dma_in(ropet, I["rope"][ti * TT:(ti + 1) * TT, :].rearrange("(s p) d -> p s d", p=128))
            for l in range(int(os.environ.get('NLAYERS', DEPTH))):
                phases = [lambda: norm_to_hT(P[l]["g_mix"]), lambda: retention_phase(l, ti), lambda: mlstm_phase(l, ti),
                          lambda: ssd_phase(l, ti), lambda: outmix_phase(l), lambda: memattn_phase(l), lambda: mlp_phase(l)]
                for pi, ph in enumerate(phases):
                    if pi < stage:
                        ph()
            if not os.environ.get('SKIP_FINAL'):
                final_norm_out(ti)
        if stage >= 99:
            for l in range(DEPTH):
                prompt_state_out(l)

    if do_sample:
        build_sample(B, nc, I, W, O, P, locals())

    cnt = B.T.emit_all(B.st)
    return nc, B, cnt


def build_sample(B, nc, I, W, O, P, g):
    ident, ones, identb, i16bc = g["ident"], g["ones"], g["identb"], g["i16bc"]
    load_slab, win_slab, sq_slab, sm = g["load_slab"], g["win_slab"], g["sq_slab"], g["sm"]
    f32a, hb, junk, gtmp = g["f32a"], g["hb"], g["junk"], g["gtmp"]
    big, big_keys, merged, xt = g["big"], g["big_keys"], g["merged"], g["xt"]
    R = NSMP

    def f32v(v, pat=None, **kw):
        ap = v.ap
        if pat is not None:
            ap = ap.rearrange(pat, **kw)
        if ap.dtype != F32:
            ap = ap.bitcast(F32)
        return V(ap, v.keys)

    def sm16(n):
        return sm(n)[0:R, :]

    pS = V(big.ap.bitcast(F32), big_keys)
    mergedS = merged[:, 0, :]
    cw = [merged[:, 1, :], xt[:, 0, :], xt[:, 1, :], f32v(g["v_tm"], "p a b -> p (a b)")]
    cbias = f32v(g["gsil"], "p a b -> p (a b)")
    xp = [f32v(g["yT"], "p a b -> p (a b)"), f32v(g["qkr"], "p a b -> p (a b)"), f32v(g["hT"], "p a b -> p (a b)")]
    yacc = xp[0]
    qmS = xp[1]
    kmS = xp[2]
    nS = cw[1]
    kwS = cw[2]
    tmp2 = f32v(g["ssmS"][0], "p a b -> p (a b)")
    xc0 = f32v(g["ssmS"][1], "p a b -> p (a b)")
    xs = f32v(g["stb"], "p a b -> p (a b)")
    sel16 = V(g["mlC"][0].ap.rearrange("p a b -> p (a b)")[0:R, :].rearrange("p (a b) -> p a b", a=16), g["mlC"][0].keys)
    SA = f32v(g["mlC"][1], "p a b -> p (a b)")
    SB = f32v(g["wqk"], "p a b c -> p (a b c)")
    QP = [f32v(g["retS"][0], "p a b -> p (a b)"), f32v(g["retS"][1], "p a b -> p (a b)")]
    hTs = B.sb("hTs", [128, 8, R], BF16)
    yTs = B.sb("yTs", [128, 8, R], BF16)
    ucTs = B.sb("ucTs", [128, 8, R], BF16)
    h1Ts = B.sb("h1Ts", [128, 32, R], BF16)
    qT8 = B.sb("qT8", [128, 8, R], F32)
    dAT = B.sb("dAT", [128, 8, R], F32)
    xdtT = B.sb("xdtT", [128, 8, R], F32)
    yTall = B.sb("yTall", [128, 8, R], F32)
    sAll = B.sb("sAll", [128, 2, R * 4], F32)
    Pall = B.sb("Pall", [128, 2, R * 4], F32)
    ropeS = B.sb("ropeS", [R, 256], F32)
    wpbc = B.sb("wpbc", [128, 64], F32)
    id16 = ident[0:R, 0:R]
    idb16 = identb[0:R, 0:R]

    B.cp(sel16, V(ident.ap[0:R, 0:R].rearrange("p (a o) -> p a o", o=1).to_broadcast([R, 16, 128]), ident.keys))
    B.dma_in(xs[0:R, :], I["x_s"])
    B.dma_in(ropeS, I["rope"][SEQ:SEQ + R, :])

    def tm_proj(srcT, slab, evac, n):
        ps = B.pb()
        kcn = slab.ap.shape[1]
        for kc in range(kcn):
            B.mm(ps[0:R, 0:n], srcT[:, kc, :], slab[:, kc, :], start=(kc == 0), stop=(kc == kcn - 1))
        evac(ps[0:R, 0:n])
        B.pf(ps)

    def toT(dstT, src16, nblk, gain=None):
        ps = B.pb().bitcast(BF16)
        psv = ps[:, 0:nblk * R].re("p (a b) -> p a b", a=nblk)
        for j in range(nblk):
            B.tr(psv[:, j, :], src16[:, j * 128:(j + 1) * 128], idb16)
        if gain is not None:
            B.tt(dstT, psv, gain.re("p (a o) -> p a o", o=1).bc([128, nblk, R]), ALU.mult)
        else:
            B.cp(dstT, psv)
        B.pf(ps)

    def toT32(dstT, src16, nblk):
        ps = B.pb()
        psv = ps[:, 0:nblk * R].re("p (a b) -> p a b", a=nblk)
        for j in range(nblk):
            B.tr(psv[:, j, :], src16[:, j * 128:(j + 1) * 128], id16)
        B.cp(dstT, psv)
        B.pf(ps)

    def rstd16(src, width):
        ss = sm16(1)
        B.act(junk[0:R, 0:width], src, AF.Square, accum=ss)
        sq = sm16(1)
        B.act(sq, ss, AF.Ln, scale=1.0 / width, bias=EPS)
        rs = sm16(1)
        B.act(rs, sq, AF.Exp, scale=-0.5)
        return rs

    def norm_hTs(gain):
        rs = rstd16(xs[0:R, :], D)
        B.ts(hb[0:R, :], xs[0:R, :], rs, ALU.mult)
        toT(hTs, hb[0:R, :], 8, gain=gain)

    def bc3(v, n_mid, n_in):
        return v.re("p (a o) -> p a o", o=1).bc([R, n_mid, n_in])

    def branch_s(l, c_gate, wname):
        first = (wname == "w_br_ret")
        for half in range(2):
            gs = win_slab(l, c_gate + half * 512)
            ws = sq_slab(W[wname][l], half)
            tm_proj(hTs, gs, lambda ps: B.act(gtmp[0:R, :], ps, AF.Sigmoid), 512)
            dst = mergedS[0:R, half * 512:(half + 1) * 512]

            def ev(ps, dst=dst):
                if first:
                    B.tt(dst, ps, gtmp[0:R, :], ALU.mult)
                else:
                    B.tt(gtmp[0:R, :], ps, gtmp[0:R, :], ALU.mult)
                    B.tt(dst, dst, gtmp[0:R, :], ALU.add)
            tm_proj(yTs, ws, ev, 512)

    def add_proj_s(srcT, w2d):
        for half in range(2):
            slab = sq_slab(w2d, half)
            dst = xs[0:R, half * 512:(half + 1) * 512]
            tm_proj(srcT, slab, lambda ps, dst=dst: B.tt(dst, ps, dst, ALU.add), 512)

    def load_cols(l, c0, ncols, dst0, func=None):
        for o in range(0, ncols, 512):
            n = min(512, ncols - o)
            slab = win_slab(l, c0 + o, n)
            d = pS[0:R, dst0 + o:dst0 + o + n]
            if func is None:
                tm_proj(hTs, slab, lambda ps, d=d: B.cp(d, ps, eng="act"), n)
            else:
                tm_proj(hTs, slab, lambda ps, d=d: B.act(d, ps, func), n)

    def conv_s(l, wkey, bkey, skey, okey, c0, xr, dst, dst_is_bf16):
        for j in range(3):
            B.dma_in(xp[j][0:R, :], I[skey][l][:, j, c0:c0 + 1024])
        for j in range(4):
            B.dma_in(cw[j][0:R, :], W[wkey][l][j][c0:c0 + 1024].partition_broadcast(R))
        B.dma_in(cbias[0:R, :], W[bkey][l][c0:c0 + 1024].partition_broadcast(R))
        acc = f32a[0:R, :]
        t2 = tmp2[0:R, :]
        B.tt(acc, xp[0][0:R, :], cw[0][0:R, :], ALU.mult)
        B.tt(acc, acc, cbias[0:R, :], ALU.add)
        for j in range(1, 4):
            src = xp[j][0:R, :] if j < 3 else xr
            B.tt(t2, src, cw[j][0:R, :], ALU.mult)
            B.tt(acc, acc, t2, ALU.add)
        B.act(dst, acc, AF.Silu)
        B.dma_out(O[okey][l][:, 0, c0:c0 + 1024], xp[1][0:R, :])
        B.dma_out(O[okey][l][:, 1, c0:c0 + 1024], xp[2][0:R, :])
        B.dma_out(O[okey][l][:, 2, c0:c0 + 1024], xr)

    def headnorm_gate(src, gate, center, ngrp, scale=None):
        wd = 1024 // ngrp
        for h in range(ngrp):
            sl = slice(h * wd, (h + 1) * wd)
            st6 = sm16(6)
            B.bnstats(st6, src[:, sl])
            mv = sm16(2)
            B.bnaggr(mv, st6)
            t = sm16(1)
            if center:
                if scale is not None:
                    B.tt(t, scale[:, h:h + 1], scale[:, h:h + 1], ALU.mult)
                    B.tt(t, t, mv[:, 1:2], ALU.mult)
                else:
                    B.cp(t, mv[:, 1:2])
            else:
                B.tt(t, mv[:, 0:1], mv[:, 0:1], ALU.mult)
                B.tt(t, t, mv[:, 1:2], ALU.add)
            sq = sm16(1)
            B.act(sq, t, AF.Ln, bias=EPS)
            rs = sm16(1)
            B.act(rs, sq, AF.Exp, scale=-0.5)
            if scale is not None:
                B.tt(rs, rs, scale[:, h:h + 1], ALU.mult)
            q = f32a[0:R, sl]
            if center:
                B.ts(q, src[:, sl], mv[:, 0:1], ALU.subtract, rs, ALU.mult)
            else:
                B.ts(q, src[:, sl], rs, ALU.mult)
            if gate is None:
                B.cp(hb[0:R, sl], q)
            else:
                B.tt(hb[0:R, sl], q, gate[:, sl], ALU.mult)

    def acc_rows(dst, ps, b):
        if b == 0:
            B.cp(dst, ps)
        else:
            B.tt(dst, dst, ps, ALU.add)

    def ret_dec(l):
        for which in range(2):
            slab = win_slab(l, C_Q + which * 512)

            def ev(ps, which=which):
                cs = ropeS[:, which * 128: which * 128 + 64].re("p (o d) -> p o d", o=1).bc([R, 4, 64])
                sn = ropeS[:, which * 128 + 64: which * 128 + 128].re("p (o d) -> p o d", o=1).bc([R, 4, 64])
                pv = ps.re("p (h d) -> p h d", h=4)
                t1 = f32a[0:R, 0:256].re("p (h d) -> p h d", h=4)
                t2 = f32a[0:R, 256:512].re("p (h d) -> p h d", h=4)
                ov = pS[0:R, which * 512:(which + 1) * 512].re("p (h d) -> p h d", h=4)
                B.tt(t1, pv[:, :, 0:64], cs, ALU.mult)
                B.tt(t2, pv[:, :, 64:128], sn, ALU.mult)
                B.tt(ov[:, :, 0:64], t1, t2, ALU.subtract)
                B.tt(t1, pv[:, :, 0:64], sn, ALU.mult)
                B.tt(t2, pv[:, :, 64:128], cs, ALU.mult)
                B.tt(ov[:, :, 64:128], t1, t2, ALU.add)
            tm_proj(hTs, slab, ev, 512)
        load_cols(l, C_V, 1024, 1024)
        load_cols(l, C_G, 1024, 2048, AF.Silu)
        qr = pS[0:R, 0:512]
        kr = pS[0:R, 512:1024]
        v = pS[0:R, 1024:2048]
        gs = pS[0:R, 2048:3072]
        toT32(qT8[:, 0:4, :], qr, 4)
        QT4 = QP[0].re("p (h b c) -> p h b c", h=4, b=R)
        B.tt(QT4, V(qT8.ap[:, 0:4, :].rearrange("p h (b o) -> p h b o", o=1).to_broadcast([128, 4, R, R]), qT8.keys),
             V(i16bc.ap.rearrange("p (o b) c -> p o b c", o=1).to_broadcast([128, 4, R, R]), i16bc.keys), ALU.mult)
        for b in range(R):
            S = SA[:, (b % 2) * 1024:(b % 2 + 1) * 1024].re("p (h e) -> p h e", h=4)
            B.dma_in(S, I["st_ret"][l, b].rearrange("h d e -> d h e"))
            kmb = gtmp[0:R, :]
            B.ts(kmb, kr, ident[0:R, b:b + 1], ALU.mult)
            kv = B.pb(2)
            for h in range(4):
                B.mm(kv[:, h * 256:(h + 1) * 256], kmb[:, h * 128:(h + 1) * 128], v[:, h * 256:(h + 1) * 256])
            for h in range(4):
                B.stt(S[:, h, :], S[:, h, :], RET_G[h], kv[:, h * 256:(h + 1) * 256], ALU.mult, ALU.add)
            B.pf(kv)
            B.dma_out(O["ret_s"][l, b].rearrange("h d e -> d h e"), S)
            ops = B.pb(2)
            for h in range(4):
                B.mm(ops[0:R, h * 256:(h + 1) * 256], QT4[:, h, b, :], S[:, h, :])
            acc_rows(yacc[0:R, :], ops[0:R, :], b)
            B.pf(ops)
        headnorm_gate(yacc[0:R, :], gs, True, 4)
        toT(yTs, hb[0:R, :], 8, gain=P[l]["g_ret"])
        branch_s(l, C_GR, "w_br_ret")

    def ml_dec(l):
        p = P[l]
        load_cols(l, C_U, 1024, 0)
        load_cols(l, C_VM, 1024, 1024)
        load_cols(l, C_OM, 1024, 2048, AF.Sigmoid)
        load_cols(l, C_IG, 8, 3072)
        u = pS[0:R, 0:1024]
        vm = pS[0:R, 1024:2048]
        osig = pS[0:R, 2048:3072]
        conv_s(l, "ml_conv_w", "ml_conv_b", "st_mlconv", "mlconv_s", 0, u, hb[0:R, :], True)
        toT(ucTs, hb[0:R, :], 8)
        for wi, wname, dst, scl in ((0, "ml_wq", qmS, 1.0), (1, "ml_wk", kmS, 1.0 / 16.0)):
            slab = load_slab(W[wname][l].rearrange("h (dc p) e -> p (h dc) e", p=128), 8, 256)
            ps = B.pb(2)
            for h in range(4):
                for dc in range(2):
                    B.mm(ps[0:R, h * 256:(h + 1) * 256], ucTs[:, h * 2 + dc, :], slab[:, h * 2 + dc, :],
                         start=(dc == 0), stop=(dc == 1))
            B.act(dst[0:R, :], ps[0:R, :], AF.Copy, scale=scl)
            B.pf(ps)
        z = sm16(8)
        B.tt(z, pS[0:R, 3072:3080], p["gateb"][0:R, :], ALU.add)
        e = sm16(4)
        B.act(e, z[:, 4:8], AF.Exp, scale=-1.0)
        sp = sm16(4)
        B.act(sp, e, AF.Ln, bias=1.0)
        mold = sm16(4)
        B.dma_in(mold, I["st_mlm"][l])
        inter = sm16(4)
        B.tt(inter, mold, sp, ALU.subtract)
        mnew = sm16(4)
        B.tt(mnew, inter, z[:, 0:4], ALU.max)
        B.dma_out(O["mlm_s"][l], mnew)
        ws = sm16(4)
        B.tt(ws, z[:, 0:4], mnew, ALU.subtract)
        B.act(ws, ws, AF.Exp)
        wp = sm16(4)
        B.tt(wp, inter, mnew, ALU.subtract)
        B.act(wp, wp, AF.Exp)
        flo = sm16(4)
        B.act(flo, mnew, AF.Exp, scale=-1.0)
        B.dma_in(nS[0:R, :], I["st_mln"][l].rearrange("b h d -> b (h d)"))
        k3 = kwS[0:R, :].re("p (h d) -> p h d", h=4)
        B.tt(k3, kmS[0:R, :].re("p (h d) -> p h d", h=4), bc3(ws, 4, 256), ALU.mult)
        n3 = nS[0:R, :].re("p (h d) -> p h d", h=4)
        B.tt(n3, n3, bc3(wp, 4, 256), ALU.mult)
        B.tt(nS[0:R, :], nS[0:R, :], kwS[0:R, :], ALU.add)
        B.dma_out(O["mln_s"][l].rearrange("b h d -> b (h d)"), nS[0:R, :])
        wd = sm16(64)
        B.tt(wd.re("p (b h) -> p b h", b=R), V(wp.ap.rearrange("p (o h) -> p o h", o=1).to_broadcast([R, R, 4]), wp.keys),
             V(ident.ap[0:R, 0:R].rearrange("p (b o) -> p b o", o=1).to_broadcast([R, R, 4]), ident.keys), ALU.mult)
        ps = B.pb()
        B.mm(ps[:, 0:64], ones[0:R, :], wd)
        B.cp(wpbc, ps[:, 0:64])
        B.pf(ps)
        toT32(qT8, qmS[0:R, :], 8)
        for hf in range(2):
            Q4 = QP[hf].re("p (h b c) -> p h b c", h=4, b=R)
            B.tt(Q4, V(qT8.ap[:, hf * 4:(hf + 1) * 4, :].rearrange("p h (b o) -> p h b o", o=1).to_broadcast([128, 4, R, R]), qT8.keys),
                 V(i16bc.ap.rearrange("p (o b) c -> p o b c", o=1).to_broadcast([128, 4, R, R]), i16bc.keys), ALU.mult)

        def qpad(idx, b):
            return QP[idx // 4].re("p (h b c) -> p h b c", h=4, b=R)[:, idx % 4, b, :]
        for b in range(R):
            S = (SA if b % 2 == 0 else SB).re("p (a e) -> p a e", a=8)
            B.dma_in(S, I["st_mlC"][l, b].rearrange("h (dc p) e -> p (h dc) e", p=128))
            kmb = f32a[0:R, :]
            B.ts(kmb, kwS[0:R, :], ident[0:R, b:b + 1], ALU.mult)
            for hp in range(2):
                cps = B.pb(2)
                for hh in range(2):
                    h = hp * 2 + hh
                    for dc in range(2):
                        B.mm(cps[:, (hh * 2 + dc) * 256:(hh * 2 + dc + 1) * 256],
                             kmb[:, h * 256 + dc * 128:h * 256 + (dc + 1) * 128], vm[:, h * 256:(h + 1) * 256])
                for hh in range(2):
                    h = hp * 2 + hh
                    dst = S[:, h * 2:h * 2 + 2, :]
                    B.stt(dst, dst, wpbc[:, b * 4 + h:b * 4 + h + 1],
                          cps[:, hh * 512:(hh + 1) * 512].re("p (a e) -> p a e", a=2), ALU.mult, ALU.add)
                B.pf(cps)
            B.dma_out(O["mlC_s"][l, b].rearrange("h (dc p) e -> p (h dc) e", p=128), S)
            ops = B.pb(2)
            for h in range(4):
                for dc in range(2):
                    B.mm(ops[0:R, h * 256:(h + 1) * 256], qpad(h * 2 + dc, b), S[:, h * 2 + dc, :],
                         start=(dc == 0), stop=(dc == 1))
            acc_rows(yacc[0:R, :], ops[0:R, :], b)
            B.pf(ops)
        B.tt(tmp2[0:R, :], qmS[0:R, :], nS[0:R, :], ALU.mult)
        qn = sm16(4)
        B.red(qn, tmp2[0:R, :].re("p (h d) -> p h d", h=4), ALU.add)
        nq = sm16(4)
        B.ts(nq, qn, -1.0, ALU.mult)
        B.tt(qn, qn, nq, ALU.max)
        B.tt(qn, qn, flo, ALU.max)
        r = sm16(4)
        B.recip(r, qn)
        headnorm_gate(yacc[0:R, :], osig, True, 4, scale=r)
        toT(yTs, hb[0:R, :], 8, gain=p["g_ml"])
        branch_s(l, C_GM, "w_br_ml")

    def ssd_dec(l):
        p = P[l]
        load_cols(l, C_Z, 1024, 0, AF.Silu)
        load_cols(l, C_XBC, 2048, 1024)
        load_cols(l, C_DT, 16, 3072)
        zs = pS[0:R, 0:1024]
        xc1 = pS[0:R, 3104:4128]
        conv_s(l, "ssm_conv_w", "ssm_conv_b", "st_ssmconv", "ssmconv_s", 0, pS[0:R, 1024:2048], xc0[0:R, :], False)
        conv_s(l, "ssm_conv_w", "ssm_conv_b", "st_ssmconv", "ssmconv_s", 1024, pS[0:R, 2048:3072], xc1, False)
        x = xc0[0:R, :]
        Bm = xc1[:, 0:512]
        Cm = xc1[:, 512:1024]
        zt = sm16(16)
        B.tt(zt, pS[0:R, 3072:3088], p["dtb"][0:R, :], ALU.add)
        B.act(zt, zt, AF.Exp)
        dt = sm16(16)
        B.act(dt, zt, AF.Ln, bias=1.0)
        dA = sm16(16)
        B.tt(dA, dt, p["nega"][0:R, :], ALU.mult)
        B.act(dA, dA, AF.Exp)
        B.cp(f32a[0:R, :].re("p (h c) -> p h c", h=16), bc3(dA, 16, 64))
        toT32(dAT, f32a[0:R, :], 8)
        B.tt(f32a[0:R, :].re("p (h c) -> p h c", h=16), x.re("p (h c) -> p h c", h=16), bc3(dt, 16, 64), ALU.mult)
        toT32(xdtT, f32a[0:R, :], 8)
        t2 = SB[:, 0:1024]
        for b in range(R):
            S = SA[:, (b % 2) * 1024:(b % 2 + 1) * 1024]
            S3 = S.re("p (a n) -> p a n", a=8)
            B.dma_in(S3, I["st_ssm"][l, b].rearrange("(hp h2) p n -> (h2 p) hp n", h2=2))
            bb = B.pb()
            B.mm(bb, sel16[:, b, :], Bm)
            cc = B.pb()
            B.mm(cc, sel16[:, b, :], Cm)
            B.tt(S3, S3, V(dAT.ap[:, :, b:b + 1].to_broadcast([128, 8, 128]), dAT.keys), ALU.mult)
            t4 = t2.re("p (g r n) -> p g r n", g=4, r=2)
            B.tt(t4, V(bb.ap.rearrange("p (g o n) -> p g o n", g=4, o=1).to_broadcast([128, 4, 2, 128]), bb.keys),
                 V(xdtT.ap[:, :, b:b + 1].rearrange("p (g r) o -> p g r o", g=4).to_broadcast([128, 4, 2, 128]), xdtT.keys),
                 ALU.mult)
            B.pf(bb)
            B.tt(S, S, t2, ALU.add)
            B.dma_out(O["ssm_s"][l, b].rearrange("(hp h2) p n -> (h2 p) hp n", h2=2), S3)
            B.tt(t4, S.re("p (g r n) -> p g r n", g=4, r=2),
                 V(cc.ap.rearrange("p (g o n) -> p g o n", g=4, o=1).to_broadcast([128, 4, 2, 128]), cc.keys), ALU.mult)
            B.pf(cc)
            B.red(yTall[:, :, b], t2.re("p (a n) -> p a n", a=8), ALU.add)
        ps = B.pb(2)
        for j in range(8):
            B.tr(ps[0:R, j * 128:(j + 1) * 128], yTall[:, j, :], ident)
        ysd = tmp2[0:R, :]
        B.tt(ysd.re("p (h c) -> p h c", h=16), x.re("p (h c) -> p h c", h=16), bc3(p["ssmD"][0:R, :], 16, 64), ALU.mult)
        B.tt(ysd, ysd, ps[0:R, :], ALU.add)
        B.pf(ps)
        B.tt(ysd, ysd, zs, ALU.mult)
        headnorm_gate(ysd, None, False, 4)
        toT(yTs, hb[0:R, :], 8, gain=p["g_ssm"])
        branch_s(l, C_GS, "w_br_ssm")

    def mem_dec(l):
        norm_hTs(P[l]["g_mem"])
        for half in range(2):
            slab = sq_slab(W["mem_wq"][l], half)
            d = pS[0:R, half * 512:(half + 1) * 512]
            tm_proj(hTs, slab, lambda ps, d=d: B.cp(d, ps, eng="act"), 512)
        qa = pS[0:R, 0:1024]
        slots = [SA[:, 0:1024], SA[:, 1024:2048], SB[:, 0:1024], SB[:, 1024:2048]]
        sctr = 0
        for b in range(R):
            q0 = B.pb()
            B.mm(q0, sel16[:, b, :], qa[:, 0:512])
            q1 = B.pb()
            B.mm(q1, sel16[:, b, :], qa[:, 512:1024])
            for mc in range(2):
                kt = slots[sctr % 4]
                sctr += 1
                B.dma_in(kt, I["c_memk"][l, b, mc * 128:(mc + 1) * 128, :])
                B.tt(f32a[:, 0:512], kt[:, 0:512], q0, ALU.mult)
                B.tt(f32a[:, 512:1024], kt[:, 512:1024], q1, ALU.mult)
                B.red(sAll[:, mc, b * 4:(b + 1) * 4], f32a.re("p (h d) -> p h d", h=4), ALU.add)
            B.pf(q0)
            B.pf(q1)
        ps = B.pb()
        for mc in range(2):
            B.tr(ps[0:64, mc * 128:(mc + 1) * 128], sAll[:, mc, :], ident)
        mx = sm(1)[0:64, :]
        B.red(mx, ps[0:64, 0:256], ALU.max)
        B.ts(mx, mx, -1.0 / 16.0, ALU.mult)
        ssum = sm(1)[0:64, :]
        pe = gtmp[0:64, 0:256]
        B.act(pe, ps[0:64, 0:256], AF.Exp, bias=mx, scale=1.0 / 16.0, accum=ssum)
        B.pf(ps)
        rs = sm(1)[0:64, :]
        B.recip(rs, ssum)
        B.ts(pe, pe, rs, ALU.mult)
        ps = B.pb()
        for mc in range(2):
            B.tr(ps[:, mc * 64:(mc + 1) * 64], pe[:, mc * 128:(mc + 1) * 128], ident[0:64, 0:64])
        B.cp(Pall, ps[:, 0:128].re("p (m c) -> p m c", m=2))
        B.pf(ps)
        for mc in range(2):
            Pp = QP[mc].re("p (b h c) -> p b h c", b=R, h=4)
            B.tt(Pp, V(Pall.ap[:, mc, :].rearrange("p (b h o) -> p b h o", b=R, o=1).to_broadcast([128, R, 4, R]), Pall.keys),
                 V(i16bc.ap.rearrange("p b (o c) -> p b o c", o=1).to_broadcast([128, R, 4, R]), i16bc.keys), ALU.mult)
        oacc = yacc
        for b in range(R):
            vts = []
            for mc in range(2):
                vt = slots[sctr % 4]
                sctr += 1
                B.dma_in(vt, I["c_memv"][l, b, mc * 128:(mc + 1) * 128, :])
                vts.append(vt)
            ops = B.pb(2)
            for h in range(4):
                for mc in range(2):
                    B.mm(ops[0:R, h * 256:(h + 1) * 256], QP[mc].re("p (b h c) -> p b h c", b=R, h=4)[:, b, h, :],
                         vts[mc][:, h * 256:(h + 1) * 256], start=(mc == 0), stop=(mc == 1))
            acc_rows(oacc[0:R, :], ops[0:R, :], b)
            B.pf(ops)
        B.cp(hb[0:R, :], oacc[0:R, :], eng="act")
        toT(yTs, hb[0:R, :], 8)
        add_proj_s(yTs, W["mem_wo"][l])

    def mlp_dec(l):
        norm_hTs(P[l]["g_mlp"])
        h1S = V(big.ap[0:R, 0:4096], big_keys)
        for j8 in range(8):
            slab = load_slab(W["mlp_w1"][l][:, j8 * 512:(j8 + 1) * 512].rearrange("(kc p) n -> p kc n", p=128), 8, 512)

            def ev(ps, j8=j8):
                B.act(gtmp[0:R, :], ps, AF.Relu)
                B.tt(h1S[:, j8 * 512:(j8 + 1) * 512], gtmp[0:R, :], gtmp[0:R, :], ALU.mult)
            tm_proj(hTs, slab, ev, 512)
        for q4 in range(4):
            toT(h1Ts[:, q4 * 8:(q4 + 1) * 8, :], h1S[:, q4 * 1024:(q4 + 1) * 1024], 8)
        for half in range(2):
            ps = B.pb()
            for kg in range(4):
                slab = load_slab(W["mlp_w2"][l][kg * 1024:(kg + 1) * 1024, half * 512:(half + 1) * 512]
                                 .rearrange("(kc p) n -> p kc n", p=128), 8, 512)
                for kc in range(8):
                    B.mm(ps[0:R, :], h1Ts[:, kg * 8 + kc, :], slab[:, kc, :],
                         start=(kg == 0 and kc == 0), stop=(kg == 3 and kc == 7))
            dst = xs[0:R, half * 512:(half + 1) * 512]
            B.tt(dst, ps[0:R, :], dst, ALU.add)
            B.pf(ps)

    for l in range(int(os.environ.get('NLAYERS', DEPTH))):
        phases = [lambda: norm_hTs(P[l]["g_mix"]), lambda: ret_dec(l), lambda: ml_dec(l), lambda: ssd_dec(l)]

        def outmix():
            B.cp(hb[0:R, :], mergedS[0:R, :], eng="act")
            toT(yTs, hb[0:R, :], 8)
            add_proj_s(yTs, W["w_out_mix"][l])
        phases += [outmix, lambda: mem_dec(l), lambda: mlp_dec(l)]
        sst = int(os.environ.get("SSTAGE", "99"))
        for pi, ph in enumerate(phases):
            if pi < sst:
                ph()
    rs = rstd16(xs[0:R, :], D)
    B.dma_in(f32a[0:R, :], W["norm_f_w"].partition_broadcast(R))
    B.stt(mergedS[0:R, :], xs[0:R, :], rs, f32a[0:R, :], ALU.mult, ALU.mult)
    B.dma_out(O["y_s"], mergedS[0:R, :])


_WNAMES = ["norm_mix_w", "w_in", "ret_norm_w", "ml_conv_w", "ml_conv_b", "ml_wq", "ml_wk", "ml_gate_b", "ml_norm_w",
           "ssm_conv_w", "ssm_conv_b", "ssm_dt_bias", "ssm_A_log", "ssm_D", "ssm_norm_w", "w_br_ret", "w_br_ml",
           "w_br_ssm", "w_out_mix", "norm_mem_w", "mem_wq", "mem_wk", "mem_wv", "mem_wo", "norm_mlp_w", "mlp_w1",
           "mlp_w2", "norm_f_w"]

_PROG = {}


def _get_prog(**kw):
    key = tuple(sorted(kw.items()))
    if key not in _PROG:
        _PROG[key] = build_program(**kw)
    return _PROG[key]


def make_in_maps(inputs):
    cst_np, rope_np, _, _ = make_consts()
    f = lambda a: np.ascontiguousarray(a, dtype=np.float32)
    wts = {k: f(inputs[k]) for k in _WNAMES}
    maps = []
    for c in range(NCORES):
        r = slice(NSMP * c, NSMP * (c + 1))
        m = dict(wts)
        m["x_p"] = f(inputs["x_prompt"][c])
        m["x_s"] = f(inputs["x_sample"][r, 0])
        m["mem"] = f(inputs["mem_prompt"][c])
        m["cst"] = cst_np
        m["rope"] = rope_np
        m["st_ret"] = f(inputs["state_ret"][:, r])
        m["st_mlC"] = f(inputs["state_mlstm_C"][:, r])
        m["st_mln"] = f(inputs["state_mlstm_n"][:, r])
        m["st_mlm"] = f(inputs["state_mlstm_m"][:, r])
        m["st_mlconv"] = f(inputs["state_mlstm_conv"][:, r])
        m["st_ssm"] = f(inputs["state_ssm"][:, r])
        m["st_ssmconv"] = f(inputs["state_ssm_conv"][:, r])
        m["c_memk"] = f(inputs["cache_mem_k"][:, r]).reshape(DEPTH, NSMP, MEMLEN, 1024)
        m["c_memv"] = f(inputs["cache_mem_v"][:, r]).reshape(DEPTH, NSMP, MEMLEN, 1024)
        maps.append(m)
    return maps


def gather_outputs(res):
    R = res.results
    cat = lambda name, axis: np.concatenate([np.asarray(R[c][name], dtype=np.float32) for c in range(NCORES)], axis=axis)
    stack1 = lambda name: np.stack([np.asarray(R[c][name], dtype=np.float32) for c in range(NCORES)], axis=1)
    y_p = np.stack([np.asarray(R[c]["y_p"], dtype=np.float32) for c in range(NCORES)], axis=0)
    y_s = cat("y_s", 0).reshape(NCORES * NSMP, 1, D)
    outs = [y_p, y_s,
            stack1("ret_p"), stack1("mlC_p"), stack1("mln_p"), stack1("mlm_p"), stack1("mlconv_p"),
            stack1("ssm_p"), stack1("ssmconv_p"),
            stack1("memk_p").reshape(DEPTH, NCORES, MEMLEN, 4, 256), stack1("memv_p").reshape(DEPTH, NCORES, MEMLEN, 4, 256),
            cat("ret_s", 1), cat("mlC_s", 1), cat("mln_s", 1), cat("mlm_s", 1), cat("mlconv_s", 1),
            cat("ssm_s", 1), cat("ssmconv_s", 1)]
    return tuple(outs)


def kernel(**inputs):
    nc, B, cnt = _get_prog()
    maps = make_in_maps(inputs)
    res = run_bass_kernel_spmd(nc, maps, core_ids=list(range(NCORES)))
    return gather_outputs(res)
```
